# Optimizing a Trainium2 kernel written in Bass

```python
import math
import jax
import jax.numpy as jnp
from jax import lax
import numpy as np

D_MODEL = 1024
BATCH = 2
SEQ = 8192
DEPTH = 4

GRID_W = 64
CTX_LEN = 256
N_MOD = 9
N_NORMS = 6
D_FF = 2816
MACARON_W = 0.5
NORM_EPS = 1e-6
NEG_INF = -1e30

HY_CH = 256
HY_ORDER = 2
HY_EMB = 33
HY_FILT = 64
HY_SHORT = 3
HY_TARGET = 1e-2
HY_FAST = 0.3
HY_SLOW = 1.5

RW_HEADS = 6
RW_HD = 64
RW_W = RW_HEADS * RW_HD
RW_DECAY_LORA = 64
RW_AAA_LORA = 64
RW_GATE_LORA = 128
RW_GN_EPS = 64e-5
ROPE_BASE = 10000.0

NA_HEADS = 6
NA_HD = 64
NA_W = NA_HEADS * NA_HD
WIN_ROWS = 8
WIN_COLS = 16
COL_BLOCK = 16
COL_BAND = 32

MIX_W = HY_CH + RW_W + NA_W
HY_IN = (HY_ORDER + 1) * HY_CH
RW_IN = 3 * RW_W + 2 * RW_DECAY_LORA + 2 * RW_AAA_LORA + RW_GATE_LORA
NA_IN = 3 * NA_W
IN_W = HY_IN + RW_IN + NA_IN

kernel_name = 'hybrid_hyena_rwkv7_natten_dit_trunk'


def rmsnorm(x, g):
    xf = x.astype(jnp.float32)
    y = xf * lax.rsqrt(jnp.mean(xf * xf, axis=-1, keepdims=True) + NORM_EPS)
    return (y * g.astype(jnp.float32)).astype(x.dtype)


def modulate(x, shift, scale):
    return x * (1 + scale) + shift


def swiglu(u, w_gu, w_dn):
    gate, up = jnp.split(u @ w_gu, 2, axis=-1)
    return (jax.nn.silu(gate) * up) @ w_dn


def ffn_sublayer(h, shift, scale, gate, g_pre, g_post, w_gu, w_dn):
    u = modulate(rmsnorm(h, g_pre), shift, scale)
    return h + MACARON_W * gate * rmsnorm(swiglu(u, w_gu, w_dn), g_post)


def centred_depthwise_conv(x, w, b):
    k = w.shape[0]
    half = k // 2
    L = x.shape[1]
    xp = jnp.pad(x, ((0, 0), (half, half), (0, 0)))
    return sum(xp[:, j:j + L] * w[j] for j in range(k)) + b


def neighbour_tokens(x):
    xp = jnp.pad(x, ((0, 0), (1, 1), (0, 0)))
    return xp[:, :-2], xp[:, 2:]


def axial_rope_tables(L, hd):
    nf = hd // 4
    t = jnp.arange(L)
    row = (t // GRID_W).astype(jnp.float32)
    col = (t % GRID_W).astype(jnp.float32)
    inv = ROPE_BASE ** (-jnp.arange(nf, dtype=jnp.float32) / nf)
    ang_r = row[:, None] * inv
    ang_c = col[:, None] * inv
    ang = jnp.concatenate([ang_r, ang_r, ang_c, ang_c], axis=-1)
    return jnp.cos(ang), jnp.sin(ang)


def apply_rope(x, cos, sin):
    sh = x.shape
    L, hd = cos.shape
    xa = x.reshape(sh[:-1] + (2, 2, hd // 4))
    rot = jnp.stack([-xa[..., 1, :], xa[..., 0, :]], axis=-2).reshape(sh)
    bshape = (1, L) + (1,) * (x.ndim - 3) + (hd,)
    return (x * cos.reshape(bshape) + rot * sin.reshape(bshape)).astype(x.dtype)


def hyena_filters(L, w1, b1, w2, b2, w3, freq):
    f32 = jnp.float32
    t = jnp.linspace(0.0, 1.0, L, dtype=f32)[:, None]
    bands = (HY_EMB - 1) // 2
    ang = 2.0 * math.pi * jnp.arange(L, dtype=f32)[:, None] / L
    fr = jnp.linspace(1e-4, bands - 1, bands, dtype=f32)[None, :]
    z = jnp.concatenate([t, jnp.cos(fr * ang), -jnp.sin(fr * ang)], axis=-1)
    fq = freq.astype(f32)
    h = jnp.sin(fq * (z @ w1.astype(f32) + b1.astype(f32)))
    h = jnp.sin(fq * (h @ w2.astype(f32) + b2.astype(f32)))
    h = (h @ w3.astype(f32)).reshape(L, HY_ORDER, 2, HY_CH)
    deltas = jnp.abs(jnp.linspace(math.log(HY_TARGET) / HY_SLOW, math.log(HY_TARGET) / HY_FAST, HY_CH, dtype=f32))
    return h * jnp.exp(-t * deltas)[:, None, None, :]


def bidir_long_conv(u, h_fwd, h_bwd, bias):
    L = u.shape[1]
    n = 2 * L
    k = jnp.concatenate([h_fwd, jnp.zeros_like(h_fwd[:1]), h_bwd[:0:-1]], axis=0)
    uf = u.astype(jnp.float32)
    y = jnp.fft.irfft(jnp.fft.rfft(uf, n=n, axis=1) * jnp.fft.rfft(k, n=n, axis=0)[None], n=n, axis=1)[:, :L]
    return (y + uf * bias.astype(jnp.float32)).astype(u.dtype)


def hyena_mixer(p, conv_w, conv_b, filters, bias):
    v, *gates = jnp.split(centred_depthwise_conv(p, conv_w, conv_b), HY_ORDER + 1, axis=-1)
    z = v
    for o, gate in enumerate(gates):
        z = gate * bidir_long_conv(z, filters[:, o, 0], filters[:, o, 1], bias[o])
    return z


def l2_normalize(x):
    xf = x.astype(jnp.float32)
    n = jnp.sqrt(jnp.sum(xf * xf, axis=-1, keepdims=True))
    return (xf / jnp.maximum(n, 1e-12)).astype(x.dtype)


def rwkv_inputs(p, mu, w0, w2, a0, a2, g2, k_k, k_a, rope_cs):
    B, L, _ = p.shape
    prev, nxt = neighbour_tokens(p)
    xs = p + mu[0] * (prev - p) + mu[1] * (nxt - p)
    i1 = 3 * RW_W + 2 * RW_DECAY_LORA
    r, k, v, wd, ad, gd = jnp.split(xs, [RW_W, 2 * RW_W, 3 * RW_W, i1, i1 + 2 * RW_AAA_LORA], axis=-1)
    heads = lambda t: t.reshape(t.shape[:-1] + (RW_HEADS, RW_HD))
    g = jax.nn.sigmoid(gd) @ g2
    wd = jnp.tanh(wd.reshape(B, L, 2, RW_DECAY_LORA))
    logw = -jax.nn.softplus(-(w0 + jnp.einsum('bldr,drc->bldc', wd, w2))) - 0.5
    decay = jnp.exp(-jnp.exp(logw.astype(jnp.float32)))
    a = jax.nn.sigmoid(a0 + jnp.einsum('bldr,drc->bldc', ad.reshape(B, L, 2, RW_AAA_LORA), a2))
    kk = l2_normalize(heads(k * k_k))
    k_dir = heads(k[:, :, None] * (1 + (a - 1) * k_a))
    b = kk[:, :, None] * heads(a)
    r, v = heads(r), heads(v)
    r_s, kk_s, k_s, b_s = r, kk, k_dir, b
    if rope_cs is not None:
        r_s, kk_s, k_s, b_s = (apply_rope(t, rope_cs[0], rope_cs[1]) for t in (r, kk, k_dir, b))
    return (r_s, heads(decay), k_s, v, -kk_s, b_s), (r, k_dir, v, g)


def wkv7_scan(state0, scan_in, d, reverse, emit):
    r, decay, k, v, a, b = scan_in
    seq = tuple(jnp.moveaxis(t.astype(jnp.float32), 1, 0) for t in (r, decay[:, :, d], k[:, :, d], v, a, b[:, :, d]))

    def step(S, inp):
        r_t, w_t, k_t, v_t, a_t, b_t = inp
        sa = jnp.einsum('bhij,bhj->bhi', S, a_t)
        S = S * w_t[:, :, None, :] + sa[..., None] * b_t[:, :, None, :] + v_t[..., None] * k_t[:, :, None, :]
        return S, (jnp.einsum('bhij,bhj->bhi', S, r_t) if emit else None)

    S, ys = lax.scan(step, state0, seq, reverse=reverse)
    return S, (jnp.moveaxis(ys, 0, 1) if emit else None)


def rwkv_output(y_f, y_b, out_in, r_k, ln_w, ln_b):
    r, k_dir, v, g = out_in
    B, L = y_f.shape[:2]
    y = y_f + y_b
    mean = jnp.mean(y, axis=-1, keepdims=True)
    var = jnp.mean(jnp.square(y - mean), axis=-1, keepdims=True)
    y = ((y - mean) * lax.rsqrt(var + RW_GN_EPS)).reshape(B, L, RW_W) * ln_w + ln_b
    bonus = jnp.sum(r[:, :, None] * k_dir * r_k, axis=(2, 4))[..., None] * v
    return (y.astype(g.dtype) + bonus.reshape(B, L, RW_W)) * g


def natten_column_tables():
    ncb = GRID_W // COL_BLOCK
    qcols = np.arange(GRID_W).reshape(ncb, COL_BLOCK)
    win0 = np.clip(qcols - WIN_COLS // 2, 0, GRID_W - WIN_COLS)
    band0 = np.clip(np.arange(ncb) * COL_BLOCK - WIN_COLS // 2, 0, GRID_W - COL_BAND)
    band_cols = band0[:, None] + np.arange(COL_BAND)
    kc = band_cols[:, None, :]
    valid = (kc >= win0[..., None]) & (kc < win0[..., None] + WIN_COLS)
    col_off = np.clip(kc - qcols[..., None] + WIN_COLS - 1, 0, 2 * WIN_COLS - 2)
    return band_cols, valid, col_off


def natten_latent(q, k, v, k_ctx, v_ctx, rpb):
    B, L = q.shape[:2]
    rows = L // GRID_W
    kr = min(WIN_ROWS, rows)
    ncb = GRID_W // COL_BLOCK
    band_cols, valid, col_off = natten_column_tables()
    valid = jnp.asarray(valid)
    grid = lambda t: t.reshape(B, rows, GRID_W, NA_HEADS, NA_HD)
    qg, kg, vg = grid(q), grid(k), grid(v)
    scale = NA_HD ** -0.5
    n_loc = kr * COL_BAND

    def row_block(i):
        start = jnp.clip(i - WIN_ROWS // 2, 0, rows - kr)
        q_i = lax.dynamic_index_in_dim(qg, i, axis=1, keepdims=False).reshape(B, ncb, COL_BLOCK, NA_HEADS, NA_HD)
        k_i = jnp.take(lax.dynamic_slice_in_dim(kg, start, kr, axis=1), band_cols, axis=2)
        v_i = jnp.take(lax.dynamic_slice_in_dim(vg, start, kr, axis=1), band_cols, axis=2)
        row_off = start + jnp.arange(kr) - i + WIN_ROWS - 1
        bias = jnp.transpose(rpb[:, row_off][:, :, col_off], (0, 2, 3, 1, 4))
        s_loc = jnp.einsum('bnqhd,brnkhd->bhnqrk', q_i, k_i).astype(jnp.float32) * scale + bias.astype(jnp.float32)
        s_loc = jnp.where(valid[:, :, None, :], s_loc, NEG_INF)
        s_ctx = jnp.einsum('bnqhd,bchd->bhnqc', q_i, k_ctx).astype(jnp.float32) * scale
        s = jnp.concatenate([s_loc.reshape(B, NA_HEADS, ncb, COL_BLOCK, n_loc), s_ctx], axis=-1)
        p = jax.nn.softmax(s, axis=-1).astype(v.dtype)
        p_loc = p[..., :n_loc].reshape(B, NA_HEADS, ncb, COL_BLOCK, kr, COL_BAND)
        o = jnp.einsum('bhnqrk,brnkhd->bnqhd', p_loc, v_i) + jnp.einsum('bhnqc,bchd->bnqhd', p[..., n_loc:], v_ctx)
        return o.reshape(B, GRID_W, NA_W)

    out = lax.map(row_block, jnp.arange(rows))
    return jnp.moveaxis(out, 0, 1).reshape(B, L, NA_W)


def context_attention(q, k, v):
    B, Lc = q.shape[:2]
    s = jnp.einsum('bqhd,bkhd->bhqk', q, k).astype(jnp.float32) * NA_HD ** -0.5
    p = jax.nn.softmax(s, axis=-1).astype(v.dtype)
    return jnp.einsum('bhqk,bkhd->bqhd', p, v).reshape(B, Lc, NA_W)


def token_mixer(u_ctx, u_lat, rope_cs, w_in, w_out, hy, rw, na_rpb, ctx_out):
    split = lambda p: jnp.split(p, [HY_IN, HY_IN + RW_IN], axis=-1)
    hy_c, rw_c, na_c = split(u_ctx @ w_in)
    hy_l, rw_l, na_l = split(u_lat @ w_in)
    hy_conv_w, hy_conv_b, f_w1, f_b1, f_w2, f_b2, f_w3, f_freq, hy_bias = hy
    o_hy_l = hyena_mixer(hy_l, hy_conv_w, hy_conv_b, hyena_filters(u_lat.shape[1], f_w1, f_b1, f_w2, f_b2, f_w3, f_freq), hy_bias)
    mu, w0, w2, a0, a2, g2, k_k, k_a, r_k, ln_w, ln_b = rw
    scan_c, out_c = rwkv_inputs(rw_c, mu, w0, w2, a0, a2, g2, k_k, k_a, None)
    scan_l, out_l = rwkv_inputs(rw_l, mu, w0, w2, a0, a2, g2, k_k, k_a, rope_cs)
    s0 = jnp.zeros((u_lat.shape[0], RW_HEADS, RW_HD, RW_HD), jnp.float32)
    s_f, y_fc = wkv7_scan(s0, scan_c, 0, False, ctx_out)
    s_b, y_bc = wkv7_scan(s0, scan_c, 1, True, ctx_out)
    _, y_fl = wkv7_scan(s_f, scan_l, 0, False, True)
    _, y_bl = wkv7_scan(s_b, scan_l, 1, True, True)
    o_rw_l = rwkv_output(y_fl, y_bl, out_l, r_k, ln_w, ln_b)
    heads = lambda t: t.reshape(t.shape[:-1] + (NA_HEADS, NA_HD))
    q_c, k_c, v_c = (heads(t) for t in jnp.split(na_c, 3, axis=-1))
    q_l, k_l, v_l = (heads(t) for t in jnp.split(na_l, 3, axis=-1))
    o_na_l = natten_latent(q_l, k_l, v_l, k_c, v_c, na_rpb)
    o_lat = jnp.concatenate([o_hy_l.astype(u_lat.dtype), o_rw_l.astype(u_lat.dtype), o_na_l], axis=-1) @ w_out
    if not ctx_out:
        return o_lat, None
    o_hy_c = hyena_mixer(hy_c, hy_conv_w, hy_conv_b, hyena_filters(u_ctx.shape[1], f_w1, f_b1, f_w2, f_b2, f_w3, f_freq), hy_bias)
    o_rw_c = rwkv_output(y_fc, y_bc, out_c, r_k, ln_w, ln_b)
    o_na_c = context_attention(q_c, k_c, v_c)
    o_ctx = jnp.concatenate([o_hy_c.astype(u_ctx.dtype), o_rw_c.astype(u_ctx.dtype), o_na_c], axis=-1) @ w_out
    return o_lat, o_ctx


def setup_inputs(seed: int = 0) -> dict:
    key = jax.random.key(seed)
    ks = iter(jax.random.split(key, 40))
    f32 = jnp.float32
    D = D_MODEL

    def nrm(shape, s=1.0):
        return s * jax.random.normal(next(ks), shape, f32)

    return {
        'x': nrm((BATCH, SEQ, D)),
        'c': nrm((BATCH, D)),
        'ctx': nrm((BATCH, CTX_LEN, D)),
        'c_ctx': nrm((D,)),
        'mod_w': nrm((DEPTH, D, N_MOD * D), 0.5 * D ** -0.5),
        'mod_b': nrm((DEPTH, N_MOD * D), 0.01),
        'norm_g': 1.0 + nrm((DEPTH, N_NORMS, D), 0.05),
        'ffn1_wgu': nrm((DEPTH, D, 2 * D_FF), D ** -0.5),
        'ffn1_wdn': nrm((DEPTH, D_FF, D), D_FF ** -0.5),
        'ffn2_wgu': nrm((DEPTH, D, 2 * D_FF), D ** -0.5),
        'ffn2_wdn': nrm((DEPTH, D_FF, D), D_FF ** -0.5),
        'w_in': nrm((DEPTH, D, IN_W), D ** -0.5),
        'w_out': nrm((DEPTH, MIX_W, D), MIX_W ** -0.5),
        'hy_conv_w': nrm((DEPTH, HY_SHORT, HY_IN), 0.5),
        'hy_conv_b': nrm((DEPTH, HY_IN), 0.01),
        'hy_f_w1': nrm((DEPTH, HY_EMB, HY_FILT), HY_EMB ** -0.5),
        'hy_f_b1': nrm((DEPTH, HY_FILT), 0.1),
        'hy_f_w2': nrm((DEPTH, HY_FILT, HY_FILT), HY_FILT ** -0.5),
        'hy_f_b2': nrm((DEPTH, HY_FILT), 0.1),
        'hy_f_w3': nrm((DEPTH, HY_FILT, HY_ORDER * 2 * HY_CH), 0.01),
        'hy_freq': 1.0 + nrm((DEPTH, HY_FILT), 0.05),
        'hy_bias': nrm((DEPTH, HY_ORDER, HY_CH), 0.5),
        'rw_mu': jax.random.uniform(next(ks), (DEPTH, 2, RW_IN), f32, 0.0, 0.5),
        'rw_w0': -1.0 + nrm((DEPTH, 2, RW_W), 0.5),
        'rw_w2': nrm((DEPTH, 2, RW_DECAY_LORA, RW_W), 0.5 * RW_DECAY_LORA ** -0.5),
        'rw_a0': nrm((DEPTH, 2, RW_W), 0.1),
        'rw_a2': nrm((DEPTH, 2, RW_AAA_LORA, RW_W), 0.5 * RW_AAA_LORA ** -0.5),
        'rw_g2': nrm((DEPTH, RW_GATE_LORA, RW_W), RW_GATE_LORA ** -0.5),
        'rw_k_k': 0.85 + nrm((DEPTH, RW_W), 0.05),
        'rw_k_a': 1.0 + nrm((DEPTH, RW_W), 0.05),
        'rw_r_k': nrm((DEPTH, RW_HEADS, RW_HD), 0.1),
        'rw_ln_w': 1.0 + nrm((DEPTH, RW_W), 0.05),
        'rw_ln_b': nrm((DEPTH, RW_W), 0.01),
        'na_rpb': nrm((DEPTH, NA_HEADS, 2 * WIN_ROWS - 1, 2 * WIN_COLS - 1), 0.1),
    }


def reference(x, c, ctx, c_ctx, mod_w, mod_b, norm_g, ffn1_wgu, ffn1_wdn, ffn2_wgu, ffn2_wdn, w_in, w_out,
              hy_conv_w, hy_conv_b, hy_f_w1, hy_f_b1, hy_f_w2, hy_f_b2, hy_f_w3, hy_freq, hy_bias,
              rw_mu, rw_w0, rw_w2, rw_a0, rw_a2, rw_g2, rw_k_k, rw_k_a, rw_r_k, rw_ln_w, rw_ln_b, na_rpb):
    rope_cs = axial_rope_tables(x.shape[1], RW_HD)
    silu_c = jax.nn.silu(c)
    silu_cc = jax.nn.silu(c_ctx)
    h_lat, h_ctx = x, ctx
    for l in range(DEPTH):
        ctx_out = l < DEPTH - 1
        m_l = jnp.split((silu_c @ mod_w[l] + mod_b[l])[:, None, :], N_MOD, axis=-1)
        m_c = jnp.split(silu_cc @ mod_w[l] + mod_b[l], N_MOD, axis=-1)
        g = norm_g[l]
        h_lat = ffn_sublayer(h_lat, m_l[0], m_l[1], m_l[2], g[0], g[1], ffn1_wgu[l], ffn1_wdn[l])
        h_ctx = ffn_sublayer(h_ctx, m_c[0], m_c[1], m_c[2], g[0], g[1], ffn1_wgu[l], ffn1_wdn[l])
        u_lat = modulate(rmsnorm(h_lat, g[2]), m_l[3], m_l[4])
        u_ctx = modulate(rmsnorm(h_ctx, g[2]), m_c[3], m_c[4])
        hy = (hy_conv_w[l], hy_conv_b[l], hy_f_w1[l], hy_f_b1[l], hy_f_w2[l], hy_f_b2[l], hy_f_w3[l], hy_freq[l], hy_bias[l])
        rw = (rw_mu[l], rw_w0[l], rw_w2[l], rw_a0[l], rw_a2[l], rw_g2[l], rw_k_k[l], rw_k_a[l], rw_r_k[l], rw_ln_w[l], rw_ln_b[l])
        o_lat, o_ctx = token_mixer(u_ctx, u_lat, rope_cs, w_in[l], w_out[l], hy, rw, na_rpb[l], ctx_out)
        h_lat = h_lat + m_l[5] * rmsnorm(o_lat, g[3])
        h_lat = ffn_sublayer(h_lat, m_l[6], m_l[7], m_l[8], g[4], g[5], ffn2_wgu[l], ffn2_wdn[l])
        if ctx_out:
            h_ctx = h_ctx + m_c[5] * rmsnorm(o_ctx, g[3])
            h_ctx = ffn_sublayer(h_ctx, m_c[6], m_c[7], m_c[8], g[4], g[5], ffn2_wgu[l], ffn2_wdn[l])
    return h_lat
```

```python
import numpy as np
import concourse.bass as bass
import concourse.mybir as mybir
from concourse.bass_utils import run_bass_kernel_spmd

F32 = mybir.dt.float32
BF16 = mybir.dt.bfloat16
AF = mybir.ActivationFunctionType
ALU = mybir.AluOpType

D = 1024
DFF = 2816
NF = DFF // 128
SEQ = 8192
CTX = 256
NCORE = 8
TL = 512
TC = 16
T = TL + TC
HALF = T // 2
NPASS = 4
TOK = NPASS * T
IN_W = 3456
NIN = IN_W // 128
EPS = 1e-6


class Res:
    __slots__ = ("w", "r")

    def __init__(self):
        self.w = None
        self.r = []


class Sched:
    def __init__(self, nc, n_dma_sems=6):
        self.nc = nc
        self.eng = {"pe": nc.tensor, "dve": nc.vector, "act": nc.scalar, "pool": nc.gpsimd, "sp": nc.sync}
        self.sems = {}
        self.cnt = {}
        for e in ["pe", "dve", "act", "pool"]:
            self.sems[e] = nc.alloc_semaphore(name="s_" + e)
            self.cnt[e] = 0
        self.dma_sems = {}
        for q in ["sp", "act", "pool"]:
            lst = []
            for i in range(n_dma_sems):
                k = "d_%s_%d" % (q, i)
                self.sems[k] = nc.alloc_semaphore(name=k)
                self.cnt[k] = 0
                lst.append(k)
            self.dma_sems[q] = [lst, 0]
        self.seen = {e: {} for e in self.eng}
        self.ninst = 0

    def _wait(self, e, deps):
        mx = {}
        for d in deps:
            if d is None:
                continue
            k, v = d
            if mx.get(k, 0) < v:
                mx[k] = v
        for k, v in mx.items():
            if k == e and e == "pe":
                continue
            if self.seen[e].get(k, 0) >= v:
                continue
            self.eng[e].wait_ge(self.sems[k], v)
            self.seen[e][k] = v

    def _deps(self, e, reads, writes):
        deps = []
        for r in reads:
            deps.append(r.w)
        for w in writes:
            deps.append(w.w)
            for rr in w.r:
                if rr[0] != e:
                    deps.append(rr)
        return deps

    def _mark(self, ev, reads, writes):
        for r in reads:
            r.r.append(ev)
            if len(r.r) > 64:
                mx = {}
                for k, v in r.r:
                    if mx.get(k, 0) < v:
                        mx[k] = v
                r.r = list(mx.items())
        for w in writes:
            w.w = ev
            w.r = []

    def op(self, e, fn, reads=(), writes=()):
        self._wait(e, self._deps(e, reads, writes))
        inst = fn()
        self.cnt[e] += 1
        inst.then_inc(self.sems[e], 1)
        ev = (e, self.cnt[e])
        self._mark(ev, reads, writes)
        self.ninst += 1
        return ev

    def dma(self, q, out, in_, reads=(), writes=(), **kw):
        lst, idx = self.dma_sems[q]
        k = lst[idx % len(lst)]
        self.dma_sems[q][1] = idx + 1
        deps = self._deps(q, reads, writes)
        if self.cnt[k] > 0:
            deps.append((k, self.cnt[k]))
        self._wait(q, deps)
        inst = self.eng[q].dma_start(out=out, in_=in_, **kw)
        self.cnt[k] += 16
        inst.then_inc(self.sems[k], 16)
        ev = (k, self.cnt[k])
        self._mark(ev, reads, writes)
        self.ninst += 1
        return ev

    def finish(self, e, resources):
        deps = []
        for r in resources:
            deps.append(r.w)
            deps.extend(r.r)
        self._wait(e, deps)


class TB:
    def __init__(self, t):
        self.t = t
        self.r = Res()


class Ctx:
    def __init__(self):
        self.nc = bass.Bass("TRN2", target_bir_lowering=False)
        self.S = Sched(self.nc)
        self.n = 0

    def sb(self, shape, dt=F32):
        self.n += 1
        return TB(self.nc.alloc_sbuf_tensor("sb%d" % self.n, list(shape), dt))

    def ps(self, shape, dt=F32):
        self.n += 1
        return TB(self.nc.alloc_psum_tensor("ps%d" % self.n, list(shape), dt))

    def din(self, name, shape, dt=F32):
        return self.nc.dram_tensor(name, list(shape), dt, kind="ExternalInput").ap()

    def dout(self, name, shape, dt=F32):
        return self.nc.dram_tensor(name, list(shape), dt, kind="ExternalOutput").ap()


def cols(lo, hi):
    return [(lo, hi)]


def emit_modulation(K, cc_d, modw_d, modb_d, j0, nj, pbank):
    nc, S = K.nc, K.S
    cc = K.sb([128, 8, 2])
    ccb = K.sb([128, 8, 2], BF16)
    S.dma("sp", cc.t[:], cc_d, writes=[cc.r])
    S.op("act", lambda: nc.scalar.activation(out=ccb.t[:], in_=cc.t[:], func=AF.Silu), reads=[cc.r], writes=[ccb.r])
    mb = K.sb([128, nj * 8])
    S.dma("sp", mb.t[:], modb_d[:, j0 * 8:(j0 + nj) * 8], writes=[mb.r])
    mps = TB(pbank.t[:, 0, 0:nj * 16].rearrange("p (a b) -> p a b", b=2))
    mps.r = pbank.r
    msb = K.sb([128, nj * 8, 2])
    wbuf = [K.sb([128, 8, 1024], BF16) for _ in range(2)]
    for j in range(nj):
        wb = wbuf[j % 2]
        S.dma("pool", wb.t[:], modw_d[:, (j0 + j) * 1024:(j0 + j + 1) * 1024].rearrange("(k p) n -> p k n", p=128),
              writes=[wb.r])
        for fc in range(8):
            for k in range(8):
                S.op("pe", lambda: nc.tensor.matmul(mps.t[:, j * 8 + fc, :], lhsT=wb.t[:, k, fc * 128:(fc + 1) * 128],
                                                    rhs=ccb.t[:, k, :], start=(k == 0), stop=(k == 7)),
                     reads=[wb.r, ccb.r], writes=[mps.r])
    for s in range(2):
        S.op("dve", lambda: nc.vector.tensor_tensor(out=msb.t[:, :, s], in0=mps.t[:, :, s], in1=mb.t[:], op=ALU.add),
             reads=[mps.r, mb.r], writes=[msb.r])
    return msb


def emit_rstd(K, src, sq, ssp, rstd, tmp, from_list=None):
    nc, S = K.nc, K.S
    for k in range(8):
        S.op("act", lambda: nc.scalar.activation(out=sq.t[:, k, :], in_=src.t[:, k, :], func=AF.Square),
             reads=[src.r], writes=[sq.r])
    for h in range(2):
        for k in range(8):
            S.op("pe", lambda: nc.tensor.matmul(ssp.t[:, h, 0:HALF], lhsT=K.ones.t[:], rhs=sq.t[:, k, h * HALF:(h + 1) * HALF],
                                                start=(k == 0), stop=(k == 7)),
                 reads=[K.ones.r, sq.r], writes=[ssp.r])
    for h in range(2):
        S.op("act", lambda: nc.scalar.activation(out=tmp.t[:, h * HALF:(h + 1) * HALF], in_=ssp.t[:, h, 0:HALF], func=AF.Sqrt,
                                                 scale=1.0 / D, bias=K.epsb.t[:, 0:1]),
             reads=[ssp.r, K.epsb.r], writes=[tmp.r])
    S.op("dve", lambda: nc.vector.reciprocal(out=rstd.t[:], in_=tmp.t[:]), reads=[tmp.r], writes=[rstd.r])


def emit_modnorm(K, src, rstd, tmp, s1, s2, uT):
    nc, S = K.nc, K.S
    for k in range(8):
        S.op("dve", lambda: nc.vector.tensor_tensor(out=tmp.t[:], in0=src.t[:, k, :], in1=rstd.t[:], op=ALU.mult),
             reads=[src.r, rstd.r], writes=[tmp.r])
        for (lo, hi, s) in ((0, TL, 0), (TL, T, 1)):
            S.op("act", lambda: nc.scalar.activation(out=uT.t[:, k, lo:hi], in_=tmp.t[:, lo:hi], func=AF.Identity,
                                                     scale=s1.t[:, k, s:s + 1], bias=s2.t[:, k, s:s + 1]),
                 reads=[tmp.r, s1.r, s2.r], writes=[uT.r])


def build_token_kernel(do_wout, do_win):
    K = Ctx()
    nc, S = K.nc, K.S
    h_d = K.din("h", [D, TOK])
    j0, nj = (0, 5) if do_win else (5, 4)
    m_d = K.din("m", [128, nj * 8, 2])
    ng_d = K.din("ng", [128, 6, 8])
    wgu_d = K.din("wgu", [D, 2 * DFF])
    wdn_d = K.din("wdn", [DFF, D])
    hout_d = K.dout("hout", [D, TOK])
    if do_wout:
        mixh_d = K.din("mixh", [256, TOK])
        mixn_d = K.din("mixn", [384, TOK])
        rw_d = K.din("rwin", [6, 384, TOK])
        lnp_d = K.din("lnp", [128, 3, 2])
        blk_d = K.din("blk", [128, 128])
        wout_d = K.din("wout", [D, D])
    if do_win:
        win_d = K.din("win", [D, IN_W])
        p_d = K.dout("p", [IN_W, TOK])

    K.ones = K.sb([128, 128], BF16)
    S.op("pool", lambda: nc.gpsimd.memset(K.ones.t[:], 1.0), writes=[K.ones.r])
    K.epsb = K.sb([128, 1])
    S.op("pool", lambda: nc.gpsimd.memset(K.epsb.t[:], EPS), writes=[K.epsb.r])

    P = [K.ps([128, 2, 512]) for _ in range(4)]
    m = K.sb([128, nj * 8, 2])
    S.dma("sp", m.t[:], m_d, writes=[m.r])
    ng = K.sb([128, 6, 8])
    S.dma("sp", ng.t[:], ng_d, writes=[ng.r])

    def mslice(j):
        return m.t[:, (j - j0) * 8:(j - j0 + 1) * 8, :]

    def mk_scale(gidx, jscale, mul=None):
        s = K.sb([128, 8, 2])
        for c in range(2):
            if mul is None:
                S.op("dve", lambda: nc.vector.scalar_tensor_tensor(out=s.t[:, :, c], in0=mslice(jscale)[:, :, c], scalar=1.0,
                                                                    in1=ng.t[:, gidx, :], op0=ALU.add, op1=ALU.mult),
                     reads=[m.r, ng.r], writes=[s.r])
            else:
                S.op("dve", lambda: nc.vector.scalar_tensor_tensor(out=s.t[:, :, c], in0=mslice(jscale)[:, :, c], scalar=float(mul),
                                                                    in1=ng.t[:, gidx, :], op0=ALU.mult, op1=ALU.mult),
                     reads=[m.r, ng.r], writes=[s.r])
        return s

    def mk_copy(j):
        s = K.sb([128, 8, 2])
        S.op("dve", lambda: nc.vector.tensor_copy(out=s.t[:], in_=mslice(j)), reads=[m.r], writes=[s.r])
        return s

    if do_win:
        f_s1 = mk_scale(0, 1)
        f_s2 = mk_copy(0)
        f_s3 = mk_scale(1, 2, mul=0.5)
        x_s1 = mk_scale(2, 4)
        x_s2 = mk_copy(3)
    else:
        o_s3 = mk_scale(3, 5, mul=1.0)
        f_s1 = mk_scale(4, 7)
        f_s2 = mk_copy(6)
        f_s3 = mk_scale(5, 8, mul=0.5)

    hT = K.sb([128, 8, T])
    sq = K.sb([128, 8, T], BF16)
    uT = K.sb([128, 8, T], BF16)
    yT = K.sb([128, 8, T])
    hid = K.sb([128, NF, T], BF16)
    tmpA = K.sb([128, T])
    tmpB = K.sb([128, T])
    rstd = K.sb([128, T])
    wg = [K.sb([128, 8, 512], BF16) for _ in range(2)]
    wu = [K.sb([128, 8, 512], BF16) for _ in range(2)]
    wdn = [K.sb([128, NF, 512], BF16) for _ in range(2)]
    if do_wout:
        mixT = K.sb([128, 8, T], BF16)
        lnp = K.sb([128, 3, 2]); S.dma("sp", lnp.t[:], lnp_d, writes=[lnp.r])
        blk = K.sb([128, 128]); S.dma("sp", blk.t[:], blk_d, writes=[blk.r])
        gnb = K.sb([128, 1]); S.op("pool", lambda: nc.gpsimd.memset(gnb.t[:], 64e-5), writes=[gnb.r])
        rwt = [K.sb([128, T]) for _ in range(6)]

    def residual_update(s3):
        for d in range(8):
            S.op("dve", lambda: nc.vector.tensor_tensor(out=tmpA.t[:], in0=yT.t[:, d, :], in1=rstd.t[:], op=ALU.mult),
                 reads=[yT.r, rstd.r], writes=[tmpA.r])
            for (lo, hi, s) in ((0, TL, 0), (TL, T, 1)):
                S.op("dve", lambda: nc.vector.scalar_tensor_tensor(out=hT.t[:, d, lo:hi], in0=tmpA.t[:, lo:hi],
                                                                    scalar=s3.t[:, d, s:s + 1], in1=hT.t[:, d, lo:hi],
                                                                    op0=ALU.mult, op1=ALU.add),
                     reads=[tmpA.r, s3.r, hT.r], writes=[hT.r])

    for ps_ in range(NPASS):
        c0 = ps_ * T
        S.dma("sp", hT.t[:], h_d[:, c0:c0 + T].rearrange("(k p) t -> p k t", p=128), writes=[hT.r])
        if do_wout:
            S.dma("pool", mixT.t[:, 0:2, :], mixh_d[:, c0:c0 + T].rearrange("(k p) t -> p k t", p=128), writes=[mixT.r])
            S.dma("pool", mixT.t[:, 5:8, :], mixn_d[:, c0:c0 + T].rearrange("(k p) t -> p k t", p=128), writes=[mixT.r])
            for ck in range(3):
                for i_ in range(6):
                    S.dma("sp" if i_ % 2 == 0 else "act", rwt[i_].t[:], rw_d[i_, ck * 128:(ck + 1) * 128, c0:c0 + T], writes=[rwt[i_].r])
                yf, yb, bf_, bb_, vv, gg = rwt

                def TTd(o, a, b, op):
                    S.op("dve", lambda: nc.vector.tensor_tensor(out=o.t[:], in0=a.t[:], in1=b.t[:], op=op), reads=[a.r, b.r], writes=[o.r])
                TTd(yf, yf, yb, ALU.add)
                pmean = P[1]
                for h in range(2):
                    S.op("pe", lambda: nc.tensor.matmul(pmean.t[:, h, 0:HALF], lhsT=blk.t[:], rhs=yf.t[:, h * HALF:(h + 1) * HALF],
                                                        start=True, stop=True), reads=[blk.r, yf.r], writes=[pmean.r])
                S.op("dve", lambda: nc.vector.tensor_tensor(out=yb.t[:].rearrange("p (h t) -> p h t", h=2),
                                                            in0=yf.t[:].rearrange("p (h t) -> p h t", h=2), in1=pmean.t[:, :, 0:HALF],
                                                            op=ALU.subtract), reads=[yf.r, pmean.r], writes=[yb.r])
                TTd(yf, yb, yb, ALU.mult)
                pvar = P[2]
                for h in range(2):
                    S.op("pe", lambda: nc.tensor.matmul(pvar.t[:, h, 0:HALF], lhsT=blk.t[:], rhs=yf.t[:, h * HALF:(h + 1) * HALF],
                                                        start=True, stop=True), reads=[blk.r, yf.r], writes=[pvar.r])
                S.op("act", lambda: nc.scalar.activation(out=yf.t[:].rearrange("p (h t) -> p h t", h=2), in_=pvar.t[:, :, 0:HALF],
                                                         func=AF.Sqrt, bias=gnb.t[:, 0:1]), reads=[pvar.r, gnb.r], writes=[yf.r])
                S.op("dve", lambda: nc.vector.reciprocal(out=yf.t[:], in_=yf.t[:]), reads=[yf.r], writes=[yf.r])
                TTd(yb, yb, yf, ALU.mult)
                S.op("act", lambda: nc.scalar.activation(out=yb.t[:], in_=yb.t[:], func=AF.Identity, scale=lnp.t[:, ck, 0:1],
                                                         bias=lnp.t[:, ck, 1:2]), reads=[yb.r, lnp.r], writes=[yb.r])
                TTd(bf_, bf_, bb_, ALU.add)
                TTd(bf_, bf_, vv, ALU.mult)
                TTd(yb, yb, bf_, ALU.add)
                S.op("dve", lambda: nc.vector.tensor_tensor(out=mixT.t[:, 2 + ck, :], in0=yb.t[:], in1=gg.t[:], op=ALU.mult),
                     reads=[yb.r, gg.r], writes=[mixT.r])
            for half_d in range(2):
                wb = wg[half_d]
                S.dma("pool", wb.t[:], wout_d[:, half_d * 512:(half_d + 1) * 512].rearrange("(k p) n -> p k n", p=128),
                      writes=[wb.r])
                for dd in range(4):
                    d = half_d * 4 + dd
                    pt = P[d % 4]
                    for h in range(2):
                        for k in range(8):
                            S.op("pe", lambda: nc.tensor.matmul(pt.t[:, h, 0:HALF], lhsT=wb.t[:, k, dd * 128:(dd + 1) * 128],
                                                                rhs=mixT.t[:, k, h * HALF:(h + 1) * HALF],
                                                                start=(k == 0), stop=(k == 7)),
                                 reads=[wb.r, mixT.r], writes=[pt.r])
                    S.op("act", lambda: nc.scalar.copy(out=yT.t[:, d, :].rearrange("p (h t) -> p h t", h=2), in_=pt.t[:, :, 0:HALF]),
                         reads=[pt.r], writes=[yT.r])
            emit_rstd(K, yT, sq, P[0], rstd, tmpB)
            residual_update(o_s3)

        emit_rstd(K, hT, sq, P[0], rstd, tmpB)
        emit_modnorm(K, hT, rstd, tmpA, f_s1, f_s2, uT)
        ngrp = (NF + 3) // 4
        for g in range(ngrp):
            f0 = g * 4
            nf = min(4, NF - f0)
            bg, bu = wg[g % 2], wu[g % 2]
            S.dma("pool", bg.t[:, :, 0:nf * 128], wgu_d[:, f0 * 128:(f0 + nf) * 128].rearrange("(k p) n -> p k n", p=128),
                  writes=[bg.r])
            S.dma("pool", bu.t[:, :, 0:nf * 128],
                  wgu_d[:, DFF + f0 * 128:DFF + (f0 + nf) * 128].rearrange("(k p) n -> p k n", p=128), writes=[bu.r])
            for ff in range(nf):
                f = f0 + ff
                pg, pu = (P[0], P[1]) if f % 2 == 0 else (P[2], P[3])
                for h in range(2):
                    for k in range(8):
                        S.op("pe", lambda: nc.tensor.matmul(pg.t[:, h, 0:HALF], lhsT=bg.t[:, k, ff * 128:(ff + 1) * 128],
                                                            rhs=uT.t[:, k, h * HALF:(h + 1) * HALF], start=(k == 0), stop=(k == 7)),
                             reads=[bg.r, uT.r], writes=[pg.r])
                    for k in range(8):
                        S.op("pe", lambda: nc.tensor.matmul(pu.t[:, h, 0:HALF], lhsT=bu.t[:, k, ff * 128:(ff + 1) * 128],
                                                            rhs=uT.t[:, k, h * HALF:(h + 1) * HALF], start=(k == 0), stop=(k == 7)),
                             reads=[bu.r, uT.r], writes=[pu.r])
                sg = tmpA if f % 2 == 0 else tmpB
                S.op("act", lambda: nc.scalar.activation(out=sg.t[:].rearrange("p (h t) -> p h t", h=2), in_=pg.t[:, :, 0:HALF],
                                                         func=AF.Silu), reads=[pg.r], writes=[sg.r])
                S.op("dve", lambda: nc.vector.tensor_tensor(out=hid.t[:, f, :].rearrange("p (h t) -> p h t", h=2),
                                                            in0=sg.t[:].rearrange("p (h t) -> p h t", h=2),
                                                            in1=pu.t[:, :, 0:HALF], op=ALU.mult),
                     reads=[sg.r, pu.r], writes=[hid.r])
        for half_d in range(2):
            wb = wdn[half_d]
            S.dma("pool", wb.t[:], wdn_d[:, half_d * 512:(half_d + 1) * 512].rearrange("(f p) n -> p f n", p=128), writes=[wb.r])
            for dd in range(4):
                d = half_d * 4 + dd
                pt = P[d % 4]
                for h in range(2):
                    for f in range(NF):
                        S.op("pe", lambda: nc.tensor.matmul(pt.t[:, h, 0:HALF], lhsT=wb.t[:, f, dd * 128:(dd + 1) * 128],
                                                            rhs=hid.t[:, f, h * HALF:(h + 1) * HALF],
                                                            start=(f == 0), stop=(f == NF - 1)),
                             reads=[wb.r, hid.r], writes=[pt.r])
                S.op("act", lambda: nc.scalar.copy(out=yT.t[:, d, :].rearrange("p (h t) -> p h t", h=2), in_=pt.t[:, :, 0:HALF]),
                     reads=[pt.r], writes=[yT.r])
        emit_rstd(K, yT, sq, P[0], rstd, tmpB)
        residual_update(f_s3)
        S.dma("sp", hout_d[:, c0:c0 + T].rearrange("(k p) t -> p k t", p=128), hT.t[:], reads=[hT.r], writes=[K_out_res(K)])

        if do_win:
            emit_rstd(K, hT, sq, P[0], rstd, tmpB)
            emit_modnorm(K, hT, rstd, tmpA, x_s1, x_s2, uT)
            ngrp = (NIN + 3) // 4
            for g in range(ngrp):
                f0 = g * 4
                nf = min(4, NIN - f0)
                bg = wg[g % 2]
                S.dma("pool", bg.t[:, :, 0:nf * 128], win_d[:, f0 * 128:(f0 + nf) * 128].rearrange("(k p) n -> p k n", p=128),
                      writes=[bg.r])
                for ff in range(nf):
                    f = f0 + ff
                    pt = P[f % 4]
                    for h in range(2):
                        for k in range(8):
                            S.op("pe", lambda: nc.tensor.matmul(pt.t[:, h, 0:HALF], lhsT=bg.t[:, k, ff * 128:(ff + 1) * 128],
                                                                rhs=uT.t[:, k, h * HALF:(h + 1) * HALF], start=(k == 0), stop=(k == 7)),
                                 reads=[bg.r, uT.r], writes=[pt.r])
                    ob = yT
                    S.op("act" if f % 2 == 0 else "dve",
                         (lambda: nc.scalar.copy(out=ob.t[:, f % 8, :].rearrange("p (h t) -> p h t", h=2), in_=pt.t[:, :, 0:HALF]))
                         if f % 2 == 0 else
                         (lambda: nc.vector.tensor_copy(out=ob.t[:, f % 8, :].rearrange("p (h t) -> p h t", h=2), in_=pt.t[:, :, 0:HALF])),
                         reads=[pt.r], writes=[ob.r])
                    if f % 8 == 7 or f == NIN - 1:
                        fa = (f // 8) * 8
                        n = f - fa + 1
                        S.dma("sp", p_d[fa * 128:(fa + n) * 128, c0:c0 + T].rearrange("(k p) t -> p k t", p=128), ob.t[:, 0:n, :],
                              reads=[ob.r], writes=[K_out_res(K)])
    S.finish("sp", K.outs)
    return K


def K_out_res(K):
    if not hasattr(K, "outs"):
        K.outs = []
    r = Res()
    K.outs.append(r)
    return r


_CACHE = {}


def get_kernel(key, fn):
    if key not in _CACHE:
        _CACHE[key] = fn()
    return _CACHE[key]


def run(K, in_maps):
    res = run_bass_kernel_spmd(K.nc, in_maps, core_ids=list(range(NCORE)))
    return res.results


def tok_layout(h_lat, h_ctx):
    outs = []
    for c in range(NCORE):
        b, q = c // 4, c % 4
        lat = h_lat[b, q * 2048:(q + 1) * 2048].reshape(NPASS, TL, -1)
        cx = h_ctx[b, q * 64:(q + 1) * 64].reshape(NPASS, TC, -1)
        a = np.concatenate([lat, cx], axis=1).reshape(TOK, -1)
        outs.append(np.ascontiguousarray(a.T))
    return outs


def tok_unlayout(per_core):
    F = per_core[0].shape[0]
    lat = np.zeros((2, SEQ, F), per_core[0].dtype)
    cx = np.zeros((2, CTX, F), per_core[0].dtype)
    for c in range(NCORE):
        b, q = c // 4, c % 4
        a = per_core[c].T.reshape(NPASS, T, F)
        lat[b, q * 2048:(q + 1) * 2048] = a[:, :TL].reshape(2048, F)
        cx[b, q * 64:(q + 1) * 64] = a[:, TL:].reshape(64, F)
    return lat, cx


def fm(v, n):
    return np.ascontiguousarray(v.reshape(n, 128).T)


def common_maps(l, m_all, norm_g, j0, nj):
    maps = []
    for core in range(NCORE):
        b = core // 4
        ml = m_all[l, b].reshape(9, 8, 128)[j0:j0 + nj]
        mc = m_all[l, 2].reshape(9, 8, 128)[j0:j0 + nj]
        m = np.stack([ml, mc], axis=-1).transpose(2, 0, 1, 3).reshape(128, nj * 8, 2)
        maps.append({
            "m": np.ascontiguousarray(m, dtype=np.float32),
            "ng": np.ascontiguousarray(norm_g[l].reshape(6, 8, 128).transpose(2, 0, 1)),
        })
    return maps


def build_mod_kernel():
    K = Ctx()
    nc, S = K.nc, K.S
    cc_d = K.din("cc", [128, 8, 3])
    w_d = K.din("w", [4, D, 1152])
    b_d = K.din("b", [128, 36])
    o_d = K.dout("m", [128, 36, 3])
    cc = K.sb([128, 8, 3]); S.dma("sp", cc.t[:], cc_d, writes=[cc.r])
    ccb = K.sb([128, 8, 3], BF16)
    S.op("act", lambda: nc.scalar.activation(out=ccb.t[:], in_=cc.t[:], func=AF.Silu), reads=[cc.r], writes=[ccb.r])
    mb = K.sb([128, 36]); S.dma("sp", mb.t[:], b_d, writes=[mb.r])
    ps = K.ps([128, 36, 3])
    wb = [K.sb([128, 8, 1152], BF16) for _ in range(2)]
    for l in range(4):
        w = wb[l % 2]
        S.dma("pool", w.t[:], w_d[l].rearrange("(k p) n -> p k n", p=128), writes=[w.r])
        for fc in range(9):
            for k in range(8):
                S.op("pe", lambda: nc.tensor.matmul(ps.t[:, l * 9 + fc, :], lhsT=w.t[:, k, fc * 128:(fc + 1) * 128], rhs=ccb.t[:, k, :],
                                                    start=(k == 0), stop=(k == 7)), reads=[w.r, ccb.r], writes=[ps.r])
    ms = K.sb([128, 36, 3])
    for s_ in range(3):
        S.op("dve", lambda: nc.vector.tensor_tensor(out=ms.t[:, :, s_], in0=ps.t[:, :, s_], in1=mb.t[:], op=ALU.add),
             reads=[ps.r, mb.r], writes=[ms.r])
    r_ = Res()
    S.dma("sp", o_d, ms.t[:], reads=[ms.r], writes=[r_])
    S.finish("sp", [r_])
    return K


def mod_maps(c, c_ctx, mod_w, mod_b):
    cc = np.stack([c[0].reshape(8, 128).T, c[1].reshape(8, 128).T, c_ctx.reshape(8, 128).T], axis=-1).astype(np.float32)
    maps = []
    for core in range(NCORE):
        cs = slice(core * 1152, (core + 1) * 1152)
        b = mod_b[:, cs].reshape(4, 9, 128).transpose(2, 0, 1).reshape(128, 36)
        maps.append({"cc": np.ascontiguousarray(cc), "w": np.ascontiguousarray(mod_w[:, :, cs]), "b": np.ascontiguousarray(b)})
    return maps


def mod_unpack(res):
    m_all = np.zeros((4, 3, 9 * D), np.float32)
    for core in range(NCORE):
        mm = res[core]["m"].reshape(128, 4, 9, 3)
        m_all[:, :, core * 1152:(core + 1) * 1152] = mm.transpose(1, 3, 2, 0).reshape(4, 3, 1152)
    return m_all


RN = CTX + SEQ
RC = 64
RBLK = 256
RNB = RN // RBLK
RROWS = 9 * 64 + 64 + 64 + 128


def build_rwkv_kernel(nblk=RNB):
    K = Ctx()
    nc, S = K.nc, K.S
    pin_d = K.din("pin", [RROWS, RN + 4])
    mu_d = K.din("mu", [128, 12, 2])
    hp_d = K.din("hp", [64, 3, 5])
    w2_d = K.din("w2", [64, 3, 64])
    a2_d = K.din("a2", [64, 3, 64])
    g2_d = K.din("g2", [128, 3, 64])
    cs_d = K.din("cs", [64, 2, RN])
    cm_d = K.din("cm", [64, 6, 256])
    y_d = K.dout("y", [64, 3, RN // RC, 64])
    bon_d = K.dout("bon", [64, 3, RN])
    v_d = K.dout("vout", [64, 3, RN])
    g_d = K.dout("gout", [64, 3, RN])

    def A(tb, ap=None):
        return (tb, tb.t[:] if ap is None else ap)

    def TT(e, o, a, b, op):
        eng = nc.vector if e == "dve" else nc.gpsimd
        S.op(e, lambda: eng.tensor_tensor(out=o[1], in0=a[1], in1=b[1], op=op), reads=[a[0].r, b[0].r], writes=[o[0].r])

    def STT(o, a, sc, b, op0, op1, extra=()):
        S.op("dve", lambda: nc.vector.scalar_tensor_tensor(out=o[1], in0=a[1], scalar=sc, in1=b[1], op0=op0, op1=op1),
             reads=[a[0].r, b[0].r] + list(extra), writes=[o[0].r])

    def TS(o, a, s1, s2, op0, op1=None, extra=()):
        if op1 is None:
            S.op("dve", lambda: nc.vector.tensor_scalar(out=o[1], in0=a[1], scalar1=s1, scalar2=None, op0=op0),
                 reads=[a[0].r] + list(extra), writes=[o[0].r])
        else:
            S.op("dve", lambda: nc.vector.tensor_scalar(out=o[1], in0=a[1], scalar1=s1, scalar2=s2, op0=op0, op1=op1),
                 reads=[a[0].r] + list(extra), writes=[o[0].r])

    def ACT(o, a, func, scale=1.0, bias=None, extra=()):
        if bias is None:
            S.op("act", lambda: nc.scalar.activation(out=o[1], in_=a[1], func=func, scale=scale),
                 reads=[a[0].r] + list(extra), writes=[o[0].r])
        else:
            S.op("act", lambda: nc.scalar.activation(out=o[1], in_=a[1], func=func, scale=scale, bias=bias),
                 reads=[a[0].r] + list(extra), writes=[o[0].r])

    def MM(o, l, r, start=True, stop=True):
        S.op("pe", lambda: nc.tensor.matmul(o[1], lhsT=l[1], rhs=r[1], start=start, stop=stop),
             reads=[l[0].r, r[0].r], writes=[o[0].r])

    cm = K.sb([64, 6, 256]); S.dma("sp", cm.t[:], cm_d, writes=[cm.r])
    cmb = K.sb([64, 6, 256], BF16); S.dma("pool", cmb.t[:], cm_d, writes=[cmb.r])
    mu = K.sb([128, 12, 2]); S.dma("sp", mu.t[:], mu_d, writes=[mu.r])
    muc = K.sb([128, 12, 1])
    STT(A(muc), A(mu, mu.t[:, :, 0:1]), -1.0, A(mu, mu.t[:, :, 1:2]), ALU.mult, ALU.subtract)
    TS(A(muc), A(muc), 1.0, None, ALU.add)
    hp = K.sb([64, 3, 5]); S.dma("sp", hp.t[:], hp_d, writes=[hp.r])
    omka = K.sb([64, 3, 1])
    TS(A(omka), A(hp, hp.t[:, :, 1:2]), -1.0, 1.0, ALU.mult, ALU.add)
    w2 = K.sb([64, 3, 64], BF16); S.dma("pool", w2.t[:], w2_d, writes=[w2.r])
    a2 = K.sb([64, 3, 64], BF16); S.dma("pool", a2.t[:], a2_d, writes=[a2.r])
    g2 = K.sb([128, 3, 64], BF16); S.dma("pool", g2.t[:], g2_d, writes=[g2.r])
    ones = K.sb([64, 64]); S.op("pool", lambda: nc.gpsimd.memset(ones.t[:], 1.0), writes=[ones.r])
    rkb = K.sb([64, 3, 64])
    for h in range(3):
        TS(A(rkb, rkb.t[:, h, :]), A(ones), hp.t[:, h, 2:3], None, ALU.mult, extra=[hp.r])
    m01 = A(cm, cm.t[:, 0, :])
    m_su = A(cm, cm.t[:, 1, :].rearrange("p (c t) -> p c t", c=4))
    m_ui = A(cm, cm.t[:, 2, :].rearrange("p (c t) -> p c t", c=4))
    m_sl = A(cm, cm.t[:, 3, :].rearrange("p (c t) -> p c t", c=4))
    i4 = A(cm, cm.t[:, 4, :].rearrange("p (c t) -> p c t", c=4))
    identb = A(cmb, cmb.t[:, 4, 0:64])
    ropeT = A(cm, cm.t[:, 5, 0:64])

    pm = [K.ps([128, 512]) for _ in range(3)]
    pc = [K.ps([128, 512]) for _ in range(3)]
    ptb = K.ps([128, 1024], BF16)
    pseq = K.ps([128, 512])
    cnt = {"pm": 0, "pc": 0}

    def PM():
        cnt["pm"] += 1
        tb = pm[cnt["pm"] % 3]
        return (tb, tb.t[0:64, 0:256])

    def PC():
        cnt["pc"] += 1
        tb = pc[cnt["pc"] % 3]
        return (tb, tb.t[0:64, 0:256].rearrange("p (c t) -> p c t", c=4))

    def c4(tb):
        return (tb, tb.t[:].rearrange("p (c t) -> p c t", c=4))

    def mk(n, dt=F32, p=64, w=256):
        return [K.sb([p, w], dt) for _ in range(n)]

    NB2 = 2
    raw = [[K.sb([128, 3, 256]) for _ in range(12)] for _ in range(NB2)]
    sh = [dict(twd=K.sb([64, 256], BF16), adb=K.sb([64, 256], BF16), sgd=K.sb([128, 256], BF16), tmp=K.sb([128, 256]),
               cs=K.sb([64, 2, 256])) for _ in range(NB2)]
    keep_f = ["pt", "U0"]
    keep_b = ["Rt", "Bhtok", "Khtok", "Vtok", "NrbT", "NrkT", "WT"]
    scr_f = ["xr", "xk", "a", "kk", "kd", "bb", "t1", "t2", "t3", "ld", "cl", "pinv", "pprev", "pend", "rs", "kks", "kds", "bbs"]
    scr_b = ["At", "Bt", "Kt", "Bh", "Kh", "vb", "Atok", "Xb", "Mb", "TT", "MakT", "G"]
    scratch = dict([(n, K.sb([64, 256])) for n in scr_f] + [(n, K.sb([64, 256], BF16)) for n in scr_b])
    hd = [[dict([(n, K.sb([64, 256])) for n in keep_f] + [(n, K.sb([64, 256], BF16)) for n in keep_b] + list(scratch.items()))
           for _ in range(3)] for _ in range(NB2)]
    ost = [dict(y=K.sb([64, 3, 4, 64]), bon=K.sb([64, 3, 256]), v=K.sb([64, 3, 256]), g=K.sb([64, 3, 256])) for _ in range(NB2)]
    H = [K.sb([64, 64]) for _ in range(3)]
    Hb = [[K.sb([64, 64], BF16) for _ in range(2)] for _ in range(3)]
    Ub = [[K.sb([64, 64], BF16) for _ in range(2)] for _ in range(3)]
    for h in range(3):
        S.op("pool", lambda: nc.gpsimd.memset(H[h].t[:], 0.0), writes=[H[h].r])
        S.op("pool", lambda: nc.gpsimd.memset(Hb[h][0].t[:], 0.0), writes=[Hb[h][0].r])
    outs = []
    gchunk = 0
    for blk in range(nblk):
        par = blk % NB2
        t0 = blk * RBLK
        cb = t0 + 1 if t0 < CTX else t0 + 3
        R, SH, O = raw[par], sh[par], ost[par]
        for gi in range(12):
            rows = 128 if gi == 11 else 64
            r0 = gi * 64
            for s_ in range(3):
                S.dma("sp" if (gi + s_) % 2 == 0 else "act", R[gi].t[0:rows, s_, :], pin_d[r0:r0 + rows, cb - 1 + s_:cb - 1 + s_ + RBLK],
                      writes=[R[gi].r])
        S.dma("sp", SH["cs"].t[:], cs_d[:, :, t0:t0 + RBLK], writes=[SH["cs"].r])

        def shift(gi, out, rows=64):
            tmp = A(SH["tmp"], SH["tmp"].t[0:rows, :])
            ACT(tmp, A(R[gi], R[gi].t[0:rows, 1, :]), AF.Identity, scale=muc.t[0:rows, gi, :], extra=[muc.r])
            STT(tmp, A(R[gi], R[gi].t[0:rows, 0, :]), mu.t[0:rows, gi, 0:1], tmp, ALU.mult, ALU.add, extra=[mu.r])
            STT(out, A(R[gi], R[gi].t[0:rows, 2, :]), mu.t[0:rows, gi, 1:2], tmp, ALU.mult, ALU.add, extra=[mu.r])

        tq = A(SH["tmp"], SH["tmp"].t[0:64, :])
        x9 = K_tmp64(K, "x9")
        shift(9, A(x9)); ACT(A(SH["twd"]), A(x9), AF.Tanh)
        shift(10, A(x9)); ACT(A(SH["adb"]), A(x9), AF.Copy)
        x11 = K_tmp64(K, "x11", 128)
        shift(11, A(x11), rows=128); ACT(A(SH["sgd"]), A(x11), AF.Sigmoid)
        cosb = A(SH["cs"], SH["cs"].t[:, 0, :])
        sinb = A(SH["cs"], SH["cs"].t[:, 1, :])
        for h in range(3):
            Dh = hd[par][h]
            shift(3 * h + 0, A(Dh["xr"]))
            shift(3 * h + 1, A(Dh["xk"]))
            shift(3 * h + 2, A(O["v"], O["v"].t[:, h, :]))
            xv = A(O["v"], O["v"].t[:, h, :])
            p1 = PM(); MM(p1, A(w2, w2.t[:, h, :]), A(SH["twd"]))
            ACT(A(Dh["ld"]), p1, AF.Sigmoid, bias=hp.t[:, h, 3:4], extra=[hp.r])
            TS(A(Dh["ld"]), A(Dh["ld"]), -0.6065306597126334, None, ALU.mult)
            p2 = PM(); MM(p2, A(a2, a2.t[:, h, :]), A(SH["adb"]))
            ACT(A(Dh["a"]), p2, AF.Sigmoid, bias=hp.t[:, h, 4:5], extra=[hp.r])
            p3 = PM(); MM(p3, A(g2, g2.t[:, h, :]), A(SH["sgd"]))
            ACT(A(O["g"], O["g"].t[:, h, :]), p3, AF.Copy)
            TS(A(Dh["t1"]), A(Dh["xk"]), hp.t[:, h, 0:1], None, ALU.mult, extra=[hp.r])
            TT("dve", A(Dh["t2"]), A(Dh["t1"]), A(Dh["t1"]), ALU.mult)
            p4 = PM(); MM(p4, A(ones), A(Dh["t2"]))
            ACT(A(Dh["t3"]), p4, AF.Sqrt)
            TS(A(Dh["t3"]), A(Dh["t3"]), 1e-12, None, ALU.max)
            S.op("dve", lambda: nc.vector.reciprocal(out=Dh["t3"].t[:], in_=Dh["t3"].t[:]), reads=[Dh["t3"].r], writes=[Dh["t3"].r])
            TT("dve", A(Dh["kk"]), A(Dh["t1"]), A(Dh["t3"]), ALU.mult)
            ACT(A(Dh["t1"]), A(Dh["a"]), AF.Identity, scale=hp.t[:, h, 1:2], bias=omka.t[:, h, :], extra=[hp.r, omka.r])
            TT("dve", A(Dh["kd"]), A(Dh["xk"]), A(Dh["t1"]), ALU.mult)
            TT("dve", A(Dh["bb"]), A(Dh["kk"]), A(Dh["a"]), ALU.mult)
            TT("dve", A(Dh["t2"]), A(Dh["xr"]), A(Dh["kd"]), ALU.mult)
            p5 = PM(); MM(p5, A(rkb, rkb.t[:, h, :]), A(Dh["t2"]))
            ACT(A(O["bon"], O["bon"].t[:, h, :]), p5, AF.Copy)
            for (src, dst) in (("xr", "rs"), ("kk", "kks"), ("kd", "kds"), ("bb", "bbs")):
                pr = PM(); MM(pr, ropeT, A(Dh[src]))
                TT("dve", A(Dh["t1"]), A(Dh[src]), cosb, ALU.mult)
                TT("dve", A(Dh["t2"]), pr, sinb, ALU.mult)
                TT("dve", A(Dh[dst]), A(Dh["t1"]), A(Dh["t2"]), ALU.add)
            S.op("dve", lambda: nc.vector.tensor_tensor_scan(out=Dh["cl"].t[:], data0=m01[1], data1=Dh["ld"].t[:], initial=0.0,
                                                             op0=ALU.mult, op1=ALU.add),
                 reads=[cm.r, Dh["ld"].r], writes=[Dh["cl"].r])
            ACT(A(Dh["pt"]), A(Dh["cl"]), AF.Exp)
            ACT(A(Dh["pinv"]), A(Dh["cl"]), AF.Exp, scale=-1.0)
            TT("dve", A(Dh["t1"]), A(Dh["cl"]), A(Dh["ld"]), ALU.subtract)
            ACT(A(Dh["pprev"]), A(Dh["t1"]), AF.Exp)
            cl3 = Dh["cl"].t[:].rearrange("p (c t) -> p c t", c=4)
            TT("dve", c4(Dh["t2"]), A(Dh["cl"], cl3[:, :, 63:64].to_broadcast([64, 4, 64])), A(Dh["cl"], cl3), ALU.subtract)
            ACT(A(Dh["pend"]), A(Dh["t2"]), AF.Exp)
            STT(A(Dh["At"]), A(Dh["kks"]), -1.0, A(Dh["pprev"]), ALU.mult, ALU.mult)
            TT("dve", A(Dh["Bt"]), A(Dh["bbs"]), A(Dh["pinv"]), ALU.mult)
            TT("dve", A(Dh["Kt"]), A(Dh["kds"]), A(Dh["pinv"]), ALU.mult)
            TT("dve", A(Dh["Rt"]), A(Dh["rs"]), A(Dh["pt"]), ALU.mult)
            TT("dve", A(Dh["Bh"]), A(Dh["bbs"]), A(Dh["pend"]), ALU.mult)
            TT("dve", A(Dh["Kh"]), A(Dh["kds"]), A(Dh["pend"]), ALU.mult)
            ACT(A(Dh["vb"]), xv, AF.Copy)
            for i_, (src, dst) in enumerate((("At", "Atok"), ("Bh", "Bhtok"), ("Kh", "Khtok"), ("vb", "Vtok"))):
                pt_ = (ptb, ptb.t[0:64, i_ * 256:(i_ + 1) * 256])
                for c in range(4):
                    S.op("pe", lambda: nc.tensor.transpose(out=ptb.t[0:64, i_ * 256 + c * 64:i_ * 256 + (c + 1) * 64],
                                                           in_=Dh[src].t[:, c * 64:(c + 1) * 64], identity=identb[1]),
                         reads=[Dh[src].r, cmb.r], writes=[ptb.r])
                ACT(A(Dh[dst]), pt_, AF.Copy)

            def chunk_mm(l, r):
                p_ = PC()
                for c in range(4):
                    MM((p_[0], p_[1][:, c, :]), (l, l.t[:, c * 64:(c + 1) * 64]), (r, r.t[:, c * 64:(c + 1) * 64]))
                return p_

            px = chunk_mm(Dh["Bt"], Dh["At"]); TT("dve", c4(Dh["Xb"]), px, m_su, ALU.mult)
            pmm = chunk_mm(Dh["At"], Dh["Bt"]); TT("dve", c4(Dh["Mb"]), pmm, m_sl, ALU.mult)
            pq = chunk_mm(Dh["Kt"], Dh["At"]); TT("dve", c4(Dh["MakT"]), pq, m_su, ALU.mult)
            pq = chunk_mm(Dh["Bt"], Dh["Rt"]); TT("dve", c4(Dh["NrbT"]), pq, m_ui, ALU.mult)
            pq = chunk_mm(Dh["Kt"], Dh["Rt"]); TT("dve", c4(Dh["NrkT"]), pq, m_ui, ALU.mult)
            TT("dve", c4(Dh["TT"]), c4(Dh["Xb"]), i4, ALU.add)
            for lev in range(5):
                pM2 = chunk_mm(Dh["Xb"], Dh["Mb"])
                if lev < 4:
                    pX2 = chunk_mm(Dh["Mb"], Dh["Xb"])
                    ACT(c4(Dh["Xb"]), pX2, AF.Copy)
                S.op("dve", lambda: nc.vector.tensor_copy(out=Dh["Mb"].t[:].rearrange("p (c t) -> p c t", c=4), in_=pM2[1]),
                     reads=[pM2[0].r], writes=[Dh["Mb"].r])
                pT = PC()
                for c in range(4):
                    MM((pT[0], pT[1][:, c, :]), identb, (Dh["TT"], Dh["TT"].t[:, c * 64:(c + 1) * 64]), start=True, stop=False)
                    MM((pT[0], pT[1][:, c, :]), (Dh["Mb"], Dh["Mb"].t[:, c * 64:(c + 1) * 64]),
                       (Dh["TT"], Dh["TT"].t[:, c * 64:(c + 1) * 64]), start=False, stop=True)
                ACT(c4(Dh["TT"]), pT, AF.Copy)
            pw = chunk_mm(Dh["Atok"], Dh["TT"]); ACT(c4(Dh["WT"]), pw, AF.Copy)
            pg_ = chunk_mm(Dh["MakT"], Dh["Vtok"]); ACT(c4(Dh["G"]), pg_, AF.Copy)
            pu0 = chunk_mm(Dh["TT"], Dh["G"])
            S.op("dve", lambda: nc.vector.tensor_copy(out=Dh["U0"].t[:].rearrange("p (c t) -> p c t", c=4), in_=pu0[1]),
                 reads=[pu0[0].r], writes=[Dh["U0"].r])
        for c in range(4):
            cur, nxt = gchunk % 2, (gchunk + 1) % 2
            sl = slice(c * 64, (c + 1) * 64)
            for h in range(3):
                Dh = hd[par][h]
                pU = (pseq, pseq.t[0:64, h * 64:(h + 1) * 64])
                MM(pU, (Dh["WT"], Dh["WT"].t[:, sl]), A(Hb[h][cur]))
                TT("dve", A(Ub[h][cur]), pU, (Dh["U0"], Dh["U0"].t[:, sl]), ALU.add)
            for h in range(3):
                Dh = hd[par][h]
                pH = (pseq, pseq.t[0:64, 192 + h * 64:192 + (h + 1) * 64])
                MM(pH, (Dh["Khtok"], Dh["Khtok"].t[:, sl]), (Dh["Vtok"], Dh["Vtok"].t[:, sl]), start=True, stop=False)
                MM(pH, (Dh["Bhtok"], Dh["Bhtok"].t[:, sl]), A(Ub[h][cur]), start=False, stop=True)
                pY = (pm[2], pm[2].t[0:64, 256 + h * 64:256 + (h + 1) * 64])
                MM(pY, (Dh["Rt"], Dh["Rt"].t[:, sl]), A(Hb[h][cur]), start=True, stop=False)
                MM(pY, (Dh["NrbT"], Dh["NrbT"].t[:, sl]), A(Ub[h][cur]), start=False, stop=False)
                MM(pY, (Dh["NrkT"], Dh["NrkT"].t[:, sl]), (Dh["Vtok"], Dh["Vtok"].t[:, sl]), start=False, stop=True)
                ACT((O["y"], O["y"].t[:, h, c, :]), pY, AF.Copy)
                STT(A(H[h]), A(H[h]), Dh["pt"].t[:, c * 64 + 63:c * 64 + 64], pH, ALU.mult, ALU.add, extra=[Dh["pt"].r])
                ACT(A(Hb[h][nxt]), A(H[h]), AF.Copy)
            gchunk += 1
        ch0 = blk * 4
        r_ = Res(); outs.append(r_)
        S.dma("sp", y_d[:, :, ch0:ch0 + 4, :], O["y"].t[:], reads=[O["y"].r], writes=[r_])
        for (nm, dd) in (("bon", bon_d), ("v", v_d), ("g", g_d)):
            r_ = Res(); outs.append(r_)
            S.dma("act", dd[:, :, t0:t0 + RBLK], O[nm].t[:], reads=[O[nm].r], writes=[r_])
    S.finish("sp", outs)
    return K


def K_tmp64(K, name, p=64):
    if not hasattr(K, "_tmps"):
        K._tmps = {}
    if name not in K._tmps:
        K._tmps[name] = K.sb([p, 256])
    return K._tmps[name]


def rope_tables():
    nf = 16
    t = np.arange(SEQ)
    row = (t // 64).astype(np.float32)
    col = (t % 64).astype(np.float32)
    inv = (np.float32(10000.0) ** (-np.arange(nf, dtype=np.float32) / np.float32(nf))).astype(np.float32)
    ang_r = row[:, None] * inv
    ang_c = col[:, None] * inv
    ang = np.concatenate([ang_r, ang_r, ang_c, ang_c], axis=-1).astype(np.float32)
    return np.cos(ang).astype(np.float32), np.sin(ang).astype(np.float32)


def rwkv_consts():
    cm = np.zeros((64, 6, 256), np.float32)
    tt = np.arange(256)
    cm[:, 0, :] = (tt % 64 != 0).astype(np.float32)[None, :]
    s = np.arange(64)[:, None]
    t = np.arange(64)[None, :]
    for c in range(4):
        cm[:, 1, c * 64:(c + 1) * 64] = (t > s)
        cm[:, 2, c * 64:(c + 1) * 64] = (t >= s)
        cm[:, 3, c * 64:(c + 1) * 64] = (t < s)
        cm[:, 4, c * 64:(c + 1) * 64] = (t == s)
    Rm = np.zeros((64, 64), np.float32)
    for hf in range(2):
        for m in range(16):
            Rm[hf * 32 + m, hf * 32 + 16 + m] = -1.0
            Rm[hf * 32 + 16 + m, hf * 32 + m] = 1.0
    cm[:, 5, 0:64] = Rm.T
    return cm


def rwkv_maps(l, p_lat, p_ctx, rw_mu, rw_w0, rw_w2, rw_a0, rw_a2, rw_g2, rw_k_k, rw_k_a, rw_r_k):
    cos, sin = rope_tables()
    cm = rwkv_consts()
    maps = []
    o = 768
    for core in range(NCORE):
        b, d, g = core // 4, (core // 2) % 2, core % 2
        lat = p_lat[b, :, o:o + 1536]
        cx = p_ctx[b, :, o:o + 1536]
        cs = np.zeros((64, 2, RN), np.float32)
        cs[:, 0, :CTX] = 1.0
        if d == 0:
            cs[:, 0, CTX:] = cos.T
            cs[:, 1, CTX:] = sin.T
        else:
            lat = lat[::-1]
            cx = cx[::-1]
            cs[:, 0, CTX:] = cos[::-1].T
            cs[:, 1, CTX:] = sin[::-1].T
        feats = []
        for hh in range(3):
            hd_ = 3 * g + hh
            feats += [np.arange(hd_ * 64, hd_ * 64 + 64), 384 + np.arange(hd_ * 64, hd_ * 64 + 64),
                      768 + np.arange(hd_ * 64, hd_ * 64 + 64)]
        feats += [1152 + d * 64 + np.arange(64), 1280 + d * 64 + np.arange(64), 1408 + np.arange(128)]
        fidx = np.concatenate(feats)
        pin = np.zeros((RROWS, RN + 4), np.float32)
        pin[:, 1:1 + CTX] = cx[:, fidx].T
        pin[:, 3 + CTX:3 + CTX + SEQ] = lat[:, fidx].T
        mu = np.zeros((128, 12, 2), np.float32)
        for gi in range(12):
            f = feats[gi]
            mp, mn = rw_mu[l][0][f], rw_mu[l][1][f]
            if d == 1:
                mp, mn = mn, mp
            mu[:len(f), gi, 0] = mp
            mu[:len(f), gi, 1] = mn
        hp = np.zeros((64, 3, 5), np.float32)
        w2 = np.zeros((64, 3, 64), np.float32)
        a2 = np.zeros((64, 3, 64), np.float32)
        g2 = np.zeros((128, 3, 64), np.float32)
        for hh in range(3):
            hd_ = 3 * g + hh
            cs_ = slice(hd_ * 64, hd_ * 64 + 64)
            hp[:, hh, 0] = rw_k_k[l][cs_]
            hp[:, hh, 1] = rw_k_a[l][cs_]
            hp[:, hh, 2] = rw_r_k[l][hd_]
            hp[:, hh, 3] = rw_w0[l][d][cs_]
            hp[:, hh, 4] = rw_a0[l][d][cs_]
            w2[:, hh, :] = rw_w2[l][d][:, cs_]
            a2[:, hh, :] = rw_a2[l][d][:, cs_]
            g2[:, hh, :] = rw_g2[l][:, cs_]
        maps.append({"pin": pin, "mu": mu, "hp": hp, "w2": w2, "a2": a2, "g2": g2, "cs": cs, "cm": cm})
    return maps


def rwkv_unpack(res):
    y = np.zeros((2, 2, RN, 384), np.float32)
    bon = np.zeros((2, 2, RN, 384), np.float32)
    v = np.zeros((2, RN, 384), np.float32)
    g = np.zeros((2, RN, 384), np.float32)
    for core in range(NCORE):
        b, d, gg = core // 4, (core // 2) % 2, core % 2
        r = res[core]
        yy = r["y"].transpose(1, 2, 0, 3).reshape(3, RN, 64)
        bb = r["bon"].transpose(1, 2, 0)
        vv = r["vout"].transpose(1, 2, 0)
        gq = r["gout"].transpose(1, 2, 0)

        def unrev(a):
            if d == 0:
                return a
            return np.concatenate([a[:, :CTX][:, ::-1], a[:, CTX:][:, ::-1]], axis=1)
        yy, bb, vv, gq = unrev(yy), unrev(bb), unrev(vv), unrev(gq)
        for hh in range(3):
            hd_ = 3 * gg + hh
            y[d, b, :, hd_ * 64:(hd_ + 1) * 64] = yy[hh]
            bon[d, b, :, hd_ * 64:(hd_ + 1) * 64] = bb[hh]
            if d == 0:
                v[b, :, hd_ * 64:(hd_ + 1) * 64] = vv[hh]
                g[b, :, hd_ * 64:(hd_ + 1) * 64] = gq[hh]
    return y, bon, v, g


def build_natten_kernel(do_ctx=True):
    K = Ctx()
    nc, S = K.nc, K.S
    q_d = K.din("q", [64, 6, 2048])
    k_d = K.din("k", [64, 6, 2560])
    v_d = K.din("v", [64, 6, 40, 65])
    kc_d = K.din("kc", [64, 6, 256])
    vc_d = K.din("vc", [128, 6, 2, 65])
    qc_d = K.din("qc", [64, 6, 64])
    bt_d = K.din("bt", [64, 6, 15, 64])
    bsp_d = K.din("bsp", [64, 6, 8, 12, 64])
    ol_d = K.dout("ol", [64, 6, 32, 64])
    oc_d = K.dout("oc", [64, 6, 64])
    qT = K.sb([64, 6, 2048], BF16); S.dma("pool", qT.t[:], q_d, writes=[qT.r])
    kT = K.sb([64, 6, 2560], BF16); S.dma("pool", kT.t[:], k_d, writes=[kT.r])
    va = K.sb([64, 6, 40, 65], BF16); S.dma("pool", va.t[:], v_d, writes=[va.r])
    kcT = K.sb([64, 6, 256], BF16); S.dma("pool", kcT.t[:], kc_d, writes=[kcT.r])
    vca = K.sb([128, 6, 2, 65], BF16); S.dma("pool", vca.t[:], vc_d, writes=[vca.r])
    qcT = K.sb([64, 6, 64], BF16); S.dma("pool", qcT.t[:], qc_d, writes=[qcT.r])
    bt = K.sb([64, 6, 15, 64]); S.dma("sp", bt.t[:], bt_d, writes=[bt.r])
    ost = K.sb([64, 6, 32, 64])
    ocs = K.sb([64, 6, 64])
    pS = [K.ps([128, 512]) for _ in range(2)]
    pC = [K.ps([128, 512]) for _ in range(2)]
    pO = [K.ps([128, 512]) for _ in range(2)]
    sbt = [K.sb([64, 12, 64]) for _ in range(2)]
    Et = [K.sb([64, 12, 64], BF16) for _ in range(2)]
    pS2 = K.ps([128, 512])
    spb = [K.sb([64, 12, 64]) for _ in range(2)]
    Ect = [K.sb([128, 2, 64], BF16) for _ in range(2)]
    rd = [K.sb([64, 1]) for _ in range(2)]
    return K, dict(qT=qT, kT=kT, va=va, kcT=kcT, vca=vca, qcT=qcT, bt=bt, ost=ost, ocs=ocs, pS=pS, pC=pC, pO=pO, sbt=sbt, Et=Et,
                   Ect=Ect, rd=rd, ol_d=ol_d, oc_d=oc_d, pS2=pS2, spb=spb, bsp_d=bsp_d)


def emit_natten(K, T_, do_ctx):
    nc, S = K.nc, K.S
    qT, kT, va, kcT, vca, qcT, bt, ost, ocs = (T_[n] for n in ("qT", "kT", "va", "kcT", "vca", "qcT", "bt", "ost", "ocs"))
    it = 0
    for h in range(6):
        for il in range(32 + (1 if do_ctx else 0)):
            par = it % 2
            it += 1
            ps_, pc_, po_ = T_["pS"][par], T_["pC"][par], T_["pO"][par]
            sb_, E_, Ec_, rd_ = T_["sbt"][par], T_["Et"][par], T_["Ect"][par], T_["rd"][par]
            is_ctx = il == 32
            qv = (qcT, qcT.t[:, h, :]) if is_ctx else (qT, qT.t[:, h, il * 64:(il + 1) * 64])
            special = (not is_ctx) and (il < 4 or il >= 28)
            if not is_ctx:
                if il < 4:
                    lo, hi, sp = il, 12, il
                elif il >= 28:
                    lo, hi, sp = 28, il + 8, il - 24
                else:
                    lo, hi, sp = il, il + 8, None
                nr = hi - lo
                ps2 = T_["pS2"]
                for r in range(nr):
                    pt_ = ps_ if r < 8 else ps2
                    rr = r % 8
                    S.op("pe", lambda: nc.tensor.matmul(pt_.t[0:64, rr * 64:(rr + 1) * 64], lhsT=kT.t[:, h, (lo + r) * 64:(lo + r + 1) * 64],
                                                        rhs=qv[1], start=True, stop=True), reads=[kT.r, qv[0].r], writes=[pt_.r])
            for tci in range(2):
                S.op("pe", lambda: nc.tensor.matmul(pc_.t[:, tci * 64:(tci + 1) * 64], lhsT=kcT.t[:, h, tci * 128:(tci + 1) * 128],
                                                    rhs=qv[1], start=True, stop=True), reads=[kcT.r, qv[0].r], writes=[pc_.r])
            if not is_ctx:
                if special:
                    sb_b = T_["spb"][sp % 2]
                    S.dma("sp", sb_b.t[:, 0:nr, :], T_["bsp_d"][:, h, sp, 0:nr, :], writes=[sb_b.r])
                    bias_a = (sb_b, sb_b.t[:, 0:min(nr, 8), :])
                    bias_b = (sb_b, sb_b.t[:, 8:nr, :]) if nr > 8 else None
                else:
                    bias_a = (bt, bt.t[:, h, 3:11, :])
                    bias_b = None
                n1 = min(nr, 8)
                S.op("dve", lambda: nc.vector.scalar_tensor_tensor(out=sb_.t[:, 0:n1, :],
                                                                    in0=ps_.t[0:64, 0:n1 * 64].rearrange("p (r c) -> p r c", r=n1),
                                                                    scalar=0.125, in1=bias_a[1], op0=ALU.mult, op1=ALU.add),
                     reads=[ps_.r, bias_a[0].r], writes=[sb_.r])
                if bias_b is not None:
                    n2 = nr - 8
                    S.op("dve", lambda: nc.vector.scalar_tensor_tensor(out=sb_.t[:, 8:nr, :],
                                                                        in0=ps2.t[0:64, 0:n2 * 64].rearrange("p (r c) -> p r c", r=n2),
                                                                        scalar=0.125, in1=bias_b[1], op0=ALU.mult, op1=ALU.add),
                         reads=[ps2.r, bias_b[0].r], writes=[sb_.r])
                S.op("act", lambda: nc.scalar.activation(out=E_.t[:, 0:nr, :], in_=sb_.t[:, 0:nr, :], func=AF.Exp), reads=[sb_.r], writes=[E_.r])
            S.op("act", lambda: nc.scalar.activation(out=Ec_.t[:], in_=pc_.t[:, 0:128].rearrange("p (a c) -> p a c", a=2), func=AF.Exp,
                                                     scale=0.125), reads=[pc_.r], writes=[Ec_.r])
            first = True
            if not is_ctx:
                for r in range(nr):
                    S.op("pe", lambda: nc.tensor.matmul(po_.t[0:64, 0:65], lhsT=E_.t[:, r, :], rhs=va.t[:, h, lo + r, :],
                                                        start=first, stop=False), reads=[E_.r, va.r], writes=[po_.r])
                    first = False
            for tci in range(2):
                S.op("pe", lambda: nc.tensor.matmul(po_.t[0:64, 0:65], lhsT=Ec_.t[:, tci, :], rhs=vca.t[:, h, tci, :],
                                                    start=first, stop=(tci == 1)), reads=[Ec_.r, vca.r], writes=[po_.r])
                first = False
            S.op("dve", lambda: nc.vector.reciprocal(out=rd_.t[:], in_=po_.t[0:64, 64:65]), reads=[po_.r], writes=[rd_.r])
            dst = (ocs, ocs.t[:, h, :]) if is_ctx else (ost, ost.t[:, h, il, :])
            S.op("dve", lambda: nc.vector.tensor_scalar(out=dst[1], in0=po_.t[0:64, 0:64], scalar1=rd_.t[:, 0:1], scalar2=None, op0=ALU.mult),
                 reads=[po_.r, rd_.r], writes=[dst[0].r])
    r1, r2 = Res(), Res()
    S.dma("sp", T_["ol_d"], ost.t[:], reads=[ost.r], writes=[r1])
    if do_ctx:
        S.dma("sp", T_["oc_d"], ocs.t[:], reads=[ocs.r], writes=[r2])
    else:
        S.op("pool", lambda: nc.gpsimd.memset(ocs.t[:], 0.0), writes=[ocs.r])
        S.dma("sp", T_["oc_d"], ocs.t[:], reads=[ocs.r], writes=[r2])
    S.finish("sp", [r1, r2])


def natten_bias_table(rpb_l):
    c = np.arange(64)[None, :]
    ck = np.arange(64)[:, None]
    win0 = np.clip(c - 8, 0, 48)
    valid = (ck >= win0) & (ck < win0 + 16)
    off = np.clip(ck - c + 15, 0, 30)
    g = rpb_l[:, :, off]
    g = np.where(valid[None, None], g, np.float32(-30000.0)).astype(np.float32)
    return np.ascontiguousarray(g.transpose(2, 0, 1, 3))


def natten_maps(l, p_lat, p_ctx, na_rpb):
    o = 768 + 1536
    bt = natten_bias_table(na_rpb[l])
    maps = []
    for core in range(NCORE):
        b, qq = core // 4, core % 4
        na_l = p_lat[b, :, o:o + 1152].reshape(128, 64, 3, 6, 64)
        na_c = p_ctx[b, :, o:o + 1152].reshape(256, 3, 6, 64)
        r0 = 32 * qq
        q = na_l[r0:r0 + 32, :, 0].reshape(2048, 6, 64).transpose(2, 1, 0)
        kh = np.zeros((40, 64, 6, 64), np.float32)
        vh = np.zeros((40, 64, 6, 64), np.float32)
        lo, hi = max(r0 - 4, 0), min(r0 + 36, 128)
        kh[lo - (r0 - 4):hi - (r0 - 4)] = na_l[lo:hi, :, 1]
        vh[lo - (r0 - 4):hi - (r0 - 4)] = na_l[lo:hi, :, 2]
        k = kh.reshape(2560, 6, 64).transpose(2, 1, 0)
        v = np.ones((64, 6, 40, 65), np.float32)
        v[:, :, :, :64] = vh.transpose(1, 2, 0, 3)
        kc = na_c[:, 1].transpose(2, 1, 0)
        vc = np.ones((128, 6, 2, 65), np.float32)
        vc[:, :, :, :64] = na_c[:, 2].reshape(2, 128, 6, 64).transpose(1, 2, 0, 3)
        qc = na_c[qq * 64:(qq + 1) * 64, 0].transpose(2, 1, 0)
        bsp = np.full((64, 6, 8, 12, 64), -30000.0, np.float32)
        for sp in range(8):
            il = sp if sp < 4 else sp + 24
            lo = il if il < 4 else 28
            hi = 12 if il < 4 else il + 8
            i = r0 + il
            start = min(max(i - 4, 0), 120)
            for j in range(hi - lo):
                ar = r0 - 4 + lo + j
                if start <= ar < start + 8:
                    bsp[:, :, sp, j, :] = bt[:, :, ar - i + 7, :]
        maps.append({"q": np.ascontiguousarray(q), "k": np.ascontiguousarray(k), "v": v, "kc": np.ascontiguousarray(kc),
                     "vc": vc, "qc": np.ascontiguousarray(qc), "bt": bt, "bsp": bsp})
    return maps


def natten_unpack(res):
    o_lat = np.zeros((2, SEQ, 384), np.float32)
    o_ctx = np.zeros((2, CTX, 384), np.float32)
    for core in range(NCORE):
        b, qq = core // 4, core % 4
        ol = res[core]["ol"]
        o_lat[b, qq * 2048:(qq + 1) * 2048] = ol.transpose(2, 0, 1, 3).reshape(2048, 384)
        oc = res[core]["oc"]
        o_ctx[b, qq * 64:(qq + 1) * 64] = oc.reshape(64, 384)
    return o_lat, o_ctx


MAGIC = 12582912.0
TWO_PI = 6.283185307179586


HY_STAGE = [3]
HY_ONLY = [""]
HY_FLAGS = set()


def hyena_seq(K, tag, L, do_it, P_, params):
    nc, S = K.nc, K.S
    NJ = L // 128
    E2 = 2 * L
    ph_d = K.din("ph" + tag, [3, 128, 96, NJ, 2])
    zx_d = K.din("zx" + tag, [2, 64, E2])
    wx_d = K.din("wx" + tag, [2, 64, E2])
    out_d = K.dout("z" + tag, [128, 32, NJ, 2])
    kext = K.nc.dram_tensor("kext" + tag, [64, E2], BF16, kind="Internal")
    kext_ap = kext.ap()
    kres = Res()
    outs = []
    if not do_it:
        zt = K.sb([128, 32 * NJ * 2])
        S.op("pool", lambda: nc.gpsimd.memset(zt.t[:], 0.0), writes=[zt.r])
        r_ = Res()
        S.dma("sp", out_d.rearrange("p a b c -> p (a b c)"), zt.t[:], reads=[zt.r], writes=[r_])
        return [r_]
    w1, w2, w3, cw, fq, fb1, fb2, hb, ident = (params[n] for n in ("w1", "w2", "w3", "cw", "fq", "fb1", "fb2", "hb", "ident"))
    CH = 512 if E2 >= 512 else E2
    nchunk = E2 // CH
    zx = [K.sb([64, CH]) for _ in range(2)]
    wx = [K.sb([64, CH]) for _ in range(2)]
    ta = K.sb([64, CH]); tb = K.sb([64, CH]); h1 = K.sb([64, CH]); h2 = K.sb([64, CH])
    fk = [K.sb([64, CH], BF16) for _ in range(2)]
    ff = K.sb([64, CH])

    def sin_layer(ps, fbias, dst):
        S.op("dve", lambda: nc.vector.tensor_scalar(out=ta.t[:], in0=ps[1], scalar1=fq.t[:, 0:1], scalar2=fbias.t[:, 0:1],
                                                    op0=ALU.mult, op1=ALU.add), reads=[ps[0].r, fq.r, fbias.r], writes=[ta.r])
        S.op("dve", lambda: nc.vector.tensor_scalar(out=tb.t[:], in0=ta.t[:], scalar1=1.0 / TWO_PI, scalar2=MAGIC,
                                                    op0=ALU.mult, op1=ALU.add), reads=[ta.r], writes=[tb.r])
        S.op("dve", lambda: nc.vector.tensor_scalar(out=tb.t[:], in0=tb.t[:], scalar1=MAGIC, scalar2=-TWO_PI,
                                                    op0=ALU.subtract, op1=ALU.mult), reads=[tb.r], writes=[tb.r])
        S.op("dve", lambda: nc.vector.tensor_tensor(out=ta.t[:], in0=ta.t[:], in1=tb.t[:], op=ALU.add), reads=[ta.r, tb.r], writes=[ta.r])
        S.op("dve", lambda: nc.vector.tensor_scalar(out=ta.t[:], in0=ta.t[:], scalar1=3.141592, scalar2=-3.141592,
                                                    op0=ALU.min, op1=ALU.max), reads=[ta.r], writes=[ta.r])
        S.op("act", lambda: nc.scalar.activation(out=dst.t[:], in_=ta.t[:], func=AF.Sin), reads=[ta.r], writes=[dst.r])

    for lay in range(2):
      for ci in range(nchunk if "nofilt" not in HY_FLAGS else 0):
        e0 = ci * CH
        zt_, wt_ = zx[ci % 2], wx[ci % 2]
        S.dma("sp", zt_.t[:], zx_d[lay, :, e0:e0 + CH], writes=[zt_.r])
        S.dma("act", wt_.t[:], wx_d[lay, :, e0:e0 + CH], writes=[wt_.r])
        p1 = P_[ci % 2]
        S.op("pe", lambda: nc.tensor.matmul(p1.t[0:64, 0:CH], lhsT=w1.t[:], rhs=zt_.t[:], start=True, stop=True),
             reads=[w1.r, zt_.r], writes=[p1.r])
        sin_layer((p1, p1.t[0:64, 0:CH]), fb1, h1)
        p2 = P_[2 + ci % 2]
        S.op("pe", lambda: nc.tensor.matmul(p2.t[0:64, 0:CH], lhsT=w2.t[:], rhs=h1.t[:], start=True, stop=True),
             reads=[w2.r, h1.r], writes=[p2.r])
        sin_layer((p2, p2.t[0:64, 0:CH]), fb2, h2)
        X0 = L if lay == 0 else L + 1
        d_first, d_second = (1, 0) if lay == 0 else (0, 1)
        if e0 + CH <= X0:
            segs = [(0, CH, d_first)]
        elif e0 >= X0:
            segs = [(0, CH, d_second)]
        else:
            segs = [(0, X0 - e0, d_first), (X0 - e0, CH, d_second)]
        p3 = P_[4 + ci % 2]
        for (a, b_, dr) in segs:
            S.op("pe", lambda: nc.tensor.matmul(p3.t[0:64, a:b_], lhsT=w3.t[:, dr, :], rhs=h2.t[:, a:b_], start=True, stop=True),
                 reads=[w3.r, h2.r], writes=[p3.r])
        S.op("dve", lambda: nc.vector.tensor_tensor(out=ff.t[:], in0=p3.t[0:64, 0:CH], in1=wt_.t[:], op=ALU.mult),
             reads=[p3.r, wt_.r], writes=[ff.r])
        if e0 <= L < e0 + CH:
            S.op("dve", lambda: nc.vector.tensor_tensor(out=ff.t[:, L - e0:L - e0 + 1], in0=ff.t[:, L - e0:L - e0 + 1], in1=hb.t[:, 0:1],
                                                        op=ALU.add), reads=[ff.r, hb.r], writes=[ff.r])
        fkt = fk[ci % 2]
        S.op("act", lambda: nc.scalar.copy(out=fkt.t[:], in_=ff.t[:]), reads=[ff.r], writes=[fkt.r])
        S.dma("sp", kext_ap[lay * 32:lay * 32 + 32, e0:e0 + CH], fkt.t[lay * 32:lay * 32 + 32, :], reads=[fkt.r], writes=[kres])
    if "kdbg" in HY_FLAGS:
        kd_d = K.dout("kd" + tag, [64, E2], BF16)
        r_ = Res(); outs.append(r_)
        S.dma("sp", kd_d, kext_ap, reads=[kres], writes=[r_])
    if HY_STAGE[0] < 2:
        zt = K.sb([128, 32 * NJ * 2])
        S.op("pool", lambda: nc.gpsimd.memset(zt.t[:], 0.0), writes=[zt.r])
        r_ = Res()
        S.dma("sp", out_d.rearrange("p a b c -> p (a b c)"), zt.t[:], reads=[zt.r], writes=[r_])
        return [r_, kres]
    G = K.sb([128, 64, NJ, 2])
    Ub = K.sb([128, 32, NJ, 2], BF16)
    Z1b = K.sb([128, 32, NJ, 2], BF16)
    xs = [K.sb([128, 32, NJ * 2]) for _ in range(3)]
    Z2 = TB(xs[1].t[:].rearrange("p r (j b) -> p r j b", b=2))
    Z2.r = xs[1].r
    cwb = params["cwb"]
    for g in range(3):
        for s_ in range(3):
            S.dma("sp" if s_ != 1 else "act", xs[s_].t[:], ph_d[s_, :, g * 32:(g + 1) * 32].rearrange("p r j b -> p r (j b)"),
                  writes=[xs[s_].r])

        def wb(k_):
            return cwb.t[:, g * 32:(g + 1) * 32, k_:k_ + 1].to_broadcast([128, 32, NJ * 2])
        for s_ in range(3):
            S.op("dve", lambda: nc.vector.tensor_tensor(out=xs[s_].t[:], in0=xs[s_].t[:], in1=wb(s_), op=ALU.mult),
                 reads=[xs[s_].r, cwb.r], writes=[xs[s_].r])
        S.op("dve", lambda: nc.vector.tensor_tensor(out=xs[0].t[:], in0=xs[0].t[:], in1=xs[1].t[:], op=ALU.add),
             reads=[xs[0].r, xs[1].r], writes=[xs[0].r])
        S.op("dve", lambda: nc.vector.tensor_tensor(out=xs[0].t[:], in0=xs[0].t[:], in1=xs[2].t[:], op=ALU.add),
             reads=[xs[0].r, xs[2].r], writes=[xs[0].r])
        dst = (Ub, Ub.t[:].rearrange("p r j b -> p r (j b)")) if g == 0 else \
              (G, G.t[:, (g - 1) * 32:g * 32].rearrange("p r j b -> p r (j b)"))
        S.op("dve", lambda: nc.vector.tensor_tensor(out=dst[1], in0=xs[0].t[:], in1=wb(3), op=ALU.add),
             reads=[xs[0].r, cwb.r], writes=[dst[0].r])
    if HY_STAGE[0] < 3:
        r_ = Res()
        S.dma("sp", out_d, G.t[:, 0:32], reads=[G.r, Ub.r], writes=[r_])
        return [r_, kres]
    TW = E2 - 128
    Tz = [K.sb([128, TW], BF16) for _ in range(2)]
    it = 0
    for o in range(2):
        src_t = Ub if o == 0 else Z1b
        for c in range(32):
            tz = Tz[it % 2]
            py = P_[it % 4]
            it += 1
            row = o * 32 + c
            srcap = bass.AP(kext_ap.tensor, row * E2 + 1, [[1, 128], [1, TW]])
            S.dma("sp" if it % 2 == 0 else "act", tz.t[:], srcap, reads=[kres], writes=[tz.r])
            if "tzdbg" in HY_FLAGS and o == 0 and c in (0, 1):
                td_d = K.dout("tzd%d" % c + tag, [128, TW], BF16)
                r_ = Res(); outs.append(r_)
                S.dma("sp", td_d, tz.t[:], reads=[tz.r], writes=[r_])
            deltas = [0] + [d for k_ in range(1, NJ) for d in (k_, -k_)]
            for n_, dl in enumerate(deltas):
                Jlo, Jhi = max(0, -dl), min(NJ - 1, NJ - 1 - dl)
                nJ = Jhi - Jlo + 1
                m0 = L + 128 * (dl if o == 0 else -dl) - 128
                S.op("pe", lambda: nc.tensor.matmul(py.t[:, (Jlo + dl) * 2:(Jlo + dl + nJ) * 2], lhsT=tz.t[:, m0:m0 + 128],
                                                    rhs=src_t.t[:, c, Jlo:Jlo + nJ, :], start=(n_ == 0), stop=(n_ == len(deltas) - 1),
                                                    skip_group_check=True),
                     reads=[tz.r, src_t.r], writes=[py.r])
            if "tzdbg" in HY_FLAGS and o == 0 and c in (0, 1):
                pyd_d = K.dout("pyd%d" % c + tag, [128, NJ * 2])
                pys = K.sb([128, NJ * 2])
                S.op("act", lambda: nc.scalar.copy(out=pys.t[:], in_=py.t[:, 0:NJ * 2]), reads=[py.r], writes=[pys.r])
                r_ = Res(); outs.append(r_)
                S.dma("sp", pyd_d, pys.t[:], reads=[pys.r], writes=[r_])
            if o == 0:
                S.op("dve", lambda: nc.vector.tensor_tensor(out=Z1b.t[:, c, :, :], in0=py.t[:, 0:NJ * 2].rearrange("p (j b) -> p j b", b=2),
                                                            in1=G.t[:, c, :, :], op=ALU.mult), reads=[py.r, G.r], writes=[Z1b.r])
            else:
                S.op("dve", lambda: nc.vector.tensor_tensor(out=Z2.t[:, c, :, :], in0=py.t[:, 0:NJ * 2].rearrange("p (j b) -> p j b", b=2),
                                                            in1=G.t[:, 32 + c, :, :], op=ALU.mult), reads=[py.r, G.r], writes=[Z2.r])
    r_ = Res()
    S.dma("sp", out_d, Z2.t, reads=[Z2.r], writes=[r_])
    return [r_] + outs


def build_hyena_kernel(do_ctx=True):
    K = Ctx()
    nc, S = K.nc, K.S
    prm_d = K.din("prm", [128, 8])
    w1_d = K.din("w1", [64, 64])
    w2_d = K.din("w2", [64, 64])
    w3_d = K.din("w3", [64, 2, 64])
    id_d = K.din("ident", [128, 128])
    prm = K.sb([128, 8]); S.dma("sp", prm.t[:], prm_d, writes=[prm.r])
    w1 = K.sb([64, 64]); S.dma("sp", w1.t[:], w1_d, writes=[w1.r])
    w2 = K.sb([64, 64]); S.dma("sp", w2.t[:], w2_d, writes=[w2.r])
    w3 = K.sb([64, 2, 64]); S.dma("sp", w3.t[:], w3_d, writes=[w3.r])
    ident = K.sb([128, 128]); S.dma("sp", ident.t[:], id_d, writes=[ident.r])
    cwb_d = K.din("cwb", [128, 96, 4])
    cwb = K.sb([128, 96, 4]); S.dma("sp", cwb.t[:], cwb_d, writes=[cwb.r])
    fb = K.sb([64, 2])
    S.op("dve", lambda: nc.vector.tensor_scalar(out=fb.t[:], in0=prm.t[0:64, 5:7], scalar1=prm.t[0:64, 4:5], scalar2=None, op0=ALU.mult),
         reads=[prm.r], writes=[fb.r])

    class V:
        def __init__(s, tb, ap):
            s.t, s.r = ap, tb.r
    params = dict(w1=w1, w2=w2, w3=w3, cw=V(prm, prm.t[0:96, 0:4]), fq=V(prm, prm.t[0:64, 4:5]), fb1=V(fb, fb.t[:, 0:1]),
                  fb2=V(fb, fb.t[:, 1:2]), hb=V(prm, prm.t[0:64, 7:8]), ident=ident, cwb=cwb)
    P_ = [K.ps([128, 512]) for _ in range(8)]
    outs = hyena_seq(K, "l", SEQ, HY_ONLY[0] != "c", P_, params)
    outs += hyena_seq(K, "c", CTX, do_ctx and HY_ONLY[0] != "l", P_, params)
    S.finish("sp", outs)
    return K


def hyena_consts(L):
    f32 = np.float32
    t = np.linspace(0.0, 1.0, L, dtype=f32)
    ang = (f32(2.0 * np.pi) * np.arange(L, dtype=f32) / f32(L)).astype(f32)
    fr = np.linspace(1e-4, 15.0, 16, dtype=f32)
    z = np.concatenate([t[:, None], np.cos(fr[None, :] * ang[:, None]), -np.sin(fr[None, :] * ang[:, None])], axis=-1).astype(f32)
    deltas = np.abs(np.linspace(np.log(1e-2) / 1.5, np.log(1e-2) / 0.3, 256, dtype=f32))
    win = np.exp(-t[:, None] * deltas[None, :]).astype(f32)
    e = np.arange(2 * L)
    pos0 = np.zeros(2 * L, np.int64)
    pos0[1:L] = L - e[1:L]
    pos0[L:] = e[L:] - L
    pos1 = np.zeros(2 * L, np.int64)
    pos1[1:L + 1] = L - e[1:L + 1]
    pos1[L + 1:] = e[L + 1:] - L
    zx = np.zeros((2, 64, 2 * L), np.float32)
    zx[0, :33] = z[pos0].T
    zx[1, :33] = z[pos1].T
    w0_ = win[pos0]; w0_[0] = 0.0
    w1_ = win[pos1]; w1_[0] = 0.0
    return zx, np.stack([w0_, w1_])


def hyena_maps(l, p_lat, p_ctx, hy_conv_w, hy_conv_b, hy_f_w1, hy_f_b1, hy_f_w2, hy_f_b2, hy_f_w3, hy_freq, hy_bias):
    zxl, wl = hyena_consts(SEQ)
    zxc, wc = hyena_consts(CTX)
    maps = []
    ident = np.eye(128, dtype=np.float32)
    for core in range(NCORE):
        ch = np.arange(32 * core, 32 * core + 32)
        rows = np.concatenate([ch, 256 + ch, 512 + ch])
        def blocked(p_, L):
            NJ = L // 128
            x = np.zeros((96, 2, L + 2), np.float32)
            x[:, :, 1:1 + L] = p_[:, :, rows].transpose(2, 0, 1)
            out = np.zeros((3, 128, 96, NJ, 2), np.float32)
            p = np.arange(128)[:, None]
            J = np.arange(NJ)[None, :]
            t_n = 128 * J + p
            t_r = 128 * J + 127 - p
            for s_ in range(3):
                out[s_, :, 32:64] = x[32:64][:, :, 1 + t_n + (s_ - 1)].transpose(2, 0, 3, 1)
                out[s_, :, 0:32] = x[0:32][:, :, 1 + t_r + (s_ - 1)].transpose(2, 0, 3, 1)
                out[s_, :, 64:96] = x[64:96][:, :, 1 + t_r + (s_ - 1)].transpose(2, 0, 3, 1)
            return out
        phl = blocked(p_lat, SEQ)
        phc = blocked(p_ctx, CTX)
        prm = np.zeros((128, 8), np.float32)
        prm[:96, 0:3] = hy_conv_w[l][:, rows].T
        prm[:96, 3] = hy_conv_b[l][rows]
        prm[:64, 4] = hy_freq[l]
        prm[:64, 5] = hy_f_b1[l]
        prm[:64, 6] = hy_f_b2[l]
        prm[:64, 7] = hy_bias[l][:, ch].reshape(64)
        cwb = np.ascontiguousarray(np.broadcast_to(prm[None, :96, 0:4], (128, 96, 4)))
        w3 = hy_f_w3[l].reshape(64, 2, 2, 256)[:, :, :, ch]
        w3 = np.ascontiguousarray(w3.transpose(0, 2, 1, 3).reshape(64, 2, 64))
        wxl = np.ascontiguousarray(np.tile(wl[:, :, ch].transpose(0, 2, 1), (1, 2, 1)))
        wxc = np.ascontiguousarray(np.tile(wc[:, :, ch].transpose(0, 2, 1), (1, 2, 1)))
        maps.append({"prm": prm, "w1": np.concatenate([hy_f_w1[l], np.zeros((31, 64), np.float32)], axis=0), "w2": hy_f_w2[l], "w3": w3, "ident": ident, "cwb": cwb,
                     "phl": phl, "zxl": zxl, "wxl": wxl, "phc": phc, "zxc": zxc, "wxc": wxc})
    return maps


def hyena_unpack(res):
    o_lat = np.zeros((2, SEQ, 256), np.float32)
    o_ctx = np.zeros((2, CTX, 256), np.float32)
    for core in range(NCORE):
        zl = res[core]["zl"][::-1]
        o_lat[:, :, 32 * core:32 * core + 32] = zl.transpose(3, 2, 0, 1).reshape(2, SEQ, 32)
        zc = res[core]["zc"][::-1]
        o_ctx[:, :, 32 * core:32 * core + 32] = zc.transpose(3, 2, 0, 1).reshape(2, CTX, 32)
    return o_lat, o_ctx


def gn_block():
    blk = np.zeros((128, 128), np.float32)
    blk[:64, :64] = 1.0 / 64
    blk[64:, 64:] = 1.0 / 64
    return blk


def run_front(l, h_lat, h_ctx, m_all, inp):
    K = get_kernel("A", lambda: build_token_kernel(do_wout=False, do_win=True))
    hs = tok_layout(h_lat, h_ctx)
    maps = common_maps(l, m_all, inp["norm_g"], 0, 5)
    for c in range(NCORE):
        maps[c].update({"h": hs[c], "wgu": inp["ffn1_wgu"][l], "wdn": inp["ffn1_wdn"][l], "win": inp["w_in"][l]})
    res = run(K, maps)
    hl, hc = tok_unlayout([r["hout"] for r in res])
    pl, pc = tok_unlayout([r["p"] for r in res])
    return hl, hc, pl, pc


def run_mixers(l, pl, pc, inp):
    Kh = get_kernel("H", lambda: build_hyena_kernel(True))
    names = ['hy_conv_w', 'hy_conv_b', 'hy_f_w1', 'hy_f_b1', 'hy_f_w2', 'hy_f_b2', 'hy_f_w3', 'hy_freq', 'hy_bias']
    hy_l, hy_c = hyena_unpack(run(Kh, hyena_maps(l, pl, pc, *[inp[n] for n in names])))
    Kr = get_kernel("R", lambda: build_rwkv_kernel())
    y, bon, v, g = rwkv_unpack(run(Kr, rwkv_maps(l, pl, pc, inp['rw_mu'], inp['rw_w0'], inp['rw_w2'], inp['rw_a0'], inp['rw_a2'],
                                                 inp['rw_g2'], inp['rw_k_k'], inp['rw_k_a'], inp['rw_r_k'])))

    def mkn():
        K, T_ = build_natten_kernel(True)
        emit_natten(K, T_, True)
        return K
    Kn = get_kernel("N", mkn)
    na_l, na_c = natten_unpack(run(Kn, natten_maps(l, pl, pc, inp['na_rpb'])))
    return dict(hy_l=hy_l, hy_c=hy_c, y=y, bon=bon, v=v, g=g, na_l=na_l, na_c=na_c)


def run_back(l, h_lat, h_ctx, mx, m_all, inp):
    K = get_kernel("C", lambda: build_token_kernel(do_wout=True, do_win=False))
    hs = tok_layout(h_lat, h_ctx)
    mixh = tok_layout(mx["hy_l"], mx["hy_c"])
    mixn = tok_layout(mx["na_l"], mx["na_c"])
    rws = []
    for arr in (mx["y"][0], mx["y"][1], mx["bon"][0], mx["bon"][1], mx["v"], mx["g"]):
        rws.append(tok_layout(arr[:, CTX:], arr[:, :CTX]))
    maps = common_maps(l, m_all, inp["norm_g"], 5, 4)
    lnp = np.stack([inp["rw_ln_w"][l].reshape(3, 128).T, inp["rw_ln_b"][l].reshape(3, 128).T], axis=-1).astype(np.float32)
    blk = gn_block()
    for c in range(NCORE):
        maps[c].update({"h": hs[c], "wgu": inp["ffn2_wgu"][l], "wdn": inp["ffn2_wdn"][l], "wout": inp["w_out"][l],
                        "mixh": mixh[c], "mixn": mixn[c], "rwin": np.stack([r[c] for r in rws]), "lnp": np.ascontiguousarray(lnp),
                        "blk": blk})
    res = run(K, maps)
    return tok_unlayout([r["hout"] for r in res])


def run_mod(inp):
    K = get_kernel("M", build_mod_kernel)
    return mod_unpack(run(K, mod_maps(inp["c"], inp["c_ctx"], inp["mod_w"], inp["mod_b"])))


def kernel(**inputs):
    inp = {k: np.asarray(v, dtype=np.float32) for k, v in inputs.items()}
    m_all = run_mod(inp)
    h_lat, h_ctx = inp["x"], inp["ctx"]
    for l in range(4):
        h_lat, h_ctx, pl, pc = run_front(l, h_lat, h_ctx, m_all, inp)
        mx = run_mixers(l, pl, pc, inp)
        h_lat, h_ctx = run_back(l, h_lat, h_ctx, mx, m_all, inp)
    return np.ascontiguousarray(h_lat, dtype=np.float32)
```

```python
import numpy as np
import concourse.bass as bass
import concourse.mybir as mybir
from concourse.bass_utils import run_bass_kernel_spmd

F32 = mybir.dt.float32
BF16 = mybir.dt.bfloat16
AF = mybir.ActivationFunctionType
ALU = mybir.AluOpType

D = 1024
DFF = 2816
NF = DFF // 128
SEQ = 8192
CTX = 256
NCORE = 8
TL = 512
TC = 16
T = TL + TC
HALF = T // 2
NPASS = 4
TOK = NPASS * T
IN_W = 3456
NIN = IN_W // 128
EPS = 1e-6


class Res:
    __slots__ = ("w", "r")

    def __init__(self):
        self.w = None
        self.r = []


class Sched:
    def __init__(self, nc, n_dma_sems=6):
        self.nc = nc
        self.eng = {"pe": nc.tensor, "dve": nc.vector, "act": nc.scalar, "pool": nc.gpsimd, "sp": nc.sync}
        self.sems = {}
        self.cnt = {}
        for e in ["pe", "dve", "act", "pool"]:
            self.sems[e] = nc.alloc_semaphore(name="s_" + e)
            self.cnt[e] = 0
        self.dma_sems = {}
        for q in ["sp", "act", "pool"]:
            lst = []
            for i in range(n_dma_sems):
                k = "d_%s_%d" % (q, i)
                self.sems[k] = nc.alloc_semaphore(name=k)
                self.cnt[k] = 0
                lst.append(k)
            self.dma_sems[q] = [lst, 0]
        self.seen = {e: {} for e in self.eng}
        self.ninst = 0

    def _wait(self, e, deps):
        mx = {}
        for d in deps:
            if d is None:
                continue
            k, v = d
            if mx.get(k, 0) < v:
                mx[k] = v
        for k, v in mx.items():
            if k == e and e == "pe":
                continue
            if self.seen[e].get(k, 0) >= v:
                continue
            self.eng[e].wait_ge(self.sems[k], v)
            self.seen[e][k] = v

    def _deps(self, e, reads, writes):
        deps = []
        for r in reads:
            deps.append(r.w)
        for w in writes:
            deps.append(w.w)
            for rr in w.r:
                if rr[0] != e:
                    deps.append(rr)
        return deps

    def _mark(self, ev, reads, writes):
        for r in reads:
            r.r.append(ev)
            if len(r.r) > 64:
                mx = {}
                for k, v in r.r:
                    if mx.get(k, 0) < v:
                        mx[k] = v
                r.r = list(mx.items())
        for w in writes:
            w.w = ev
            w.r = []

    def op(self, e, fn, reads=(), writes=()):
        self._wait(e, self._deps(e, reads, writes))
        inst = fn()
        self.cnt[e] += 1
        inst.then_inc(self.sems[e], 1)
        ev = (e, self.cnt[e])
        self._mark(ev, reads, writes)
        self.ninst += 1
        return ev

    def dma(self, q, out, in_, reads=(), writes=(), **kw):
        lst, idx = self.dma_sems[q]
        k = lst[idx % len(lst)]
        self.dma_sems[q][1] = idx + 1
        deps = self._deps(q, reads, writes)
        if self.cnt[k] > 0:
            deps.append((k, self.cnt[k]))
        self._wait(q, deps)
        inst = self.eng[q].dma_start(out=out, in_=in_, **kw)
        self.cnt[k] += 16
        inst.then_inc(self.sems[k], 16)
        ev = (k, self.cnt[k])
        self._mark(ev, reads, writes)
        self.ninst += 1
        return ev

    def finish(self, e, resources):
        deps = []
        for r in resources:
            deps.append(r.w)
            deps.extend(r.r)
        self._wait(e, deps)


class TB:
    def __init__(self, t):
        self.t = t
        self.r = Res()


class Ctx:
    def __init__(self):
        self.nc = bass.Bass("TRN2", target_bir_lowering=False)
        self.S = Sched(self.nc)
        self.n = 0

    def sb(self, shape, dt=F32):
        self.n += 1
        return TB(self.nc.alloc_sbuf_tensor("sb%d" % self.n, list(shape), dt))

    def ps(self, shape, dt=F32):
        self.n += 1
        return TB(self.nc.alloc_psum_tensor("ps%d" % self.n, list(shape), dt))

    def din(self, name, shape, dt=F32):
        return self.nc.dram_tensor(name, list(shape), dt, kind="ExternalInput").ap()

    def dout(self, name, shape, dt=F32):
        return self.nc.dram_tensor(name, list(shape), dt, kind="ExternalOutput").ap()


def cols(lo, hi):
    return [(lo, hi)]


def emit_modulation(K, cc_d, modw_d, modb_d, j0, nj, pbank):
    nc, S = K.nc, K.S
    cc = K.sb([128, 8, 2])
    ccb = K.sb([128, 8, 2], BF16)
    S.dma("sp", cc.t[:], cc_d, writes=[cc.r])
    S.op("act", lambda: nc.scalar.activation(out=ccb.t[:], in_=cc.t[:], func=AF.Silu), reads=[cc.r], writes=[ccb.r])
    mb = K.sb([128, nj * 8])
    S.dma("sp", mb.t[:], modb_d[:, j0 * 8:(j0 + nj) * 8], writes=[mb.r])
    mps = TB(pbank.t[:, 0, 0:nj * 16].rearrange("p (a b) -> p a b", b=2))
    mps.r = pbank.r
    msb = K.sb([128, nj * 8, 2])
    wbuf = [K.sb([128, 8, 1024], BF16) for _ in range(2)]
    for j in range(nj):
        wb = wbuf[j % 2]
        S.dma("pool", wb.t[:], modw_d[:, (j0 + j) * 1024:(j0 + j + 1) * 1024].rearrange("(k p) n -> p k n", p=128),
              writes=[wb.r])
        for fc in range(8):
            for k in range(8):
                S.op("pe", lambda: nc.tensor.matmul(mps.t[:, j * 8 + fc, :], lhsT=wb.t[:, k, fc * 128:(fc + 1) * 128],
                                                    rhs=ccb.t[:, k, :], start=(k == 0), stop=(k == 7)),
                     reads=[wb.r, ccb.r], writes=[mps.r])
    for s in range(2):
        S.op("dve", lambda: nc.vector.tensor_tensor(out=msb.t[:, :, s], in0=mps.t[:, :, s], in1=mb.t[:], op=ALU.add),
             reads=[mps.r, mb.r], writes=[msb.r])
    return msb


def emit_rstd(K, src, sq, ssp, rstd, tmp, from_list=None):
    nc, S = K.nc, K.S
    for k in range(8):
        S.op("act", lambda: nc.scalar.activation(out=sq.t[:, k, :], in_=src.t[:, k, :], func=AF.Square),
             reads=[src.r], writes=[sq.r])
    for h in range(2):
        for k in range(8):
            S.op("pe", lambda: nc.tensor.matmul(ssp.t[:, h, 0:HALF], lhsT=K.ones.t[:], rhs=sq.t[:, k, h * HALF:(h + 1) * HALF],
                                                start=(k == 0), stop=(k == 7)),
                 reads=[K.ones.r, sq.r], writes=[ssp.r])
    for h in range(2):
        S.op("act", lambda: nc.scalar.activation(out=tmp.t[:, h * HALF:(h + 1) * HALF], in_=ssp.t[:, h, 0:HALF], func=AF.Sqrt,
                                                 scale=1.0 / D, bias=K.epsb.t[:, 0:1]),
             reads=[ssp.r, K.epsb.r], writes=[tmp.r])
    S.op("dve", lambda: nc.vector.reciprocal(out=rstd.t[:], in_=tmp.t[:]), reads=[tmp.r], writes=[rstd.r])


def emit_modnorm(K, src, rstd, tmp, s1, s2, uT):
    nc, S = K.nc, K.S
    for k in range(8):
        S.op("dve", lambda: nc.vector.tensor_tensor(out=tmp.t[:], in0=src.t[:, k, :], in1=rstd.t[:], op=ALU.mult),
             reads=[src.r, rstd.r], writes=[tmp.r])
        for (lo, hi, s) in ((0, TL, 0), (TL, T, 1)):
            S.op("act", lambda: nc.scalar.activation(out=uT.t[:, k, lo:hi], in_=tmp.t[:, lo:hi], func=AF.Identity,
                                                     scale=s1.t[:, k, s:s + 1], bias=s2.t[:, k, s:s + 1]),
                 reads=[tmp.r, s1.r, s2.r], writes=[uT.r])


def build_token_kernel(do_wout, do_win):
    K = Ctx()
    nc, S = K.nc, K.S
    h_d = K.din("h", [D, TOK])
    j0, nj = (0, 5) if do_win else (5, 4)
    m_d = K.din("m", [128, nj * 8, 2])
    ng_d = K.din("ng", [128, 6, 8])
    wgu_d = K.din("wgu", [D, 2 * DFF])
    wdn_d = K.din("wdn", [DFF, D])
    hout_d = K.dout("hout", [D, TOK])
    if do_wout:
        mixh_d = K.din("mixh", [256, TOK])
        mixn_d = K.din("mixn", [384, TOK])
        rw_d = K.din("rwin", [6, 384, TOK])
        lnp_d = K.din("lnp", [128, 3, 2])
        blk_d = K.din("blk", [128, 128])
        wout_d = K.din("wout", [D, D])
    if do_win:
        win_d = K.din("win", [D, IN_W])
        p_d = K.dout("p", [IN_W, TOK])

    K.ones = K.sb([128, 128], BF16)
    S.op("pool", lambda: nc.gpsimd.memset(K.ones.t[:], 1.0), writes=[K.ones.r])
    K.epsb = K.sb([128, 1])
    S.op("pool", lambda: nc.gpsimd.memset(K.epsb.t[:], EPS), writes=[K.epsb.r])

    P = [K.ps([128, 2, 512]) for _ in range(4)]
    m = K.sb([128, nj * 8, 2])
    S.dma("sp", m.t[:], m_d, writes=[m.r])
    ng = K.sb([128, 6, 8])
    S.dma("sp", ng.t[:], ng_d, writes=[ng.r])

    def mslice(j):
        return m.t[:, (j - j0) * 8:(j - j0 + 1) * 8, :]

    def mk_scale(gidx, jscale, mul=None):
        s = K.sb([128, 8, 2])
        for c in range(2):
            if mul is None:
                S.op("dve", lambda: nc.vector.scalar_tensor_tensor(out=s.t[:, :, c], in0=mslice(jscale)[:, :, c], scalar=1.0,
                                                                    in1=ng.t[:, gidx, :], op0=ALU.add, op1=ALU.mult),
                     reads=[m.r, ng.r], writes=[s.r])
            else:
                S.op("dve", lambda: nc.vector.scalar_tensor_tensor(out=s.t[:, :, c], in0=mslice(jscale)[:, :, c], scalar=float(mul),
                                                                    in1=ng.t[:, gidx, :], op0=ALU.mult, op1=ALU.mult),
                     reads=[m.r, ng.r], writes=[s.r])
        return s

    def mk_copy(j):
        s = K.sb([128, 8, 2])
        S.op("dve", lambda: nc.vector.tensor_copy(out=s.t[:], in_=mslice(j)), reads=[m.r], writes=[s.r])
        return s

    if do_win:
        f_s1 = mk_scale(0, 1)
        f_s2 = mk_copy(0)
        f_s3 = mk_scale(1, 2, mul=0.5)
        x_s1 = mk_scale(2, 4)
        x_s2 = mk_copy(3)
    else:
        o_s3 = mk_scale(3, 5, mul=1.0)
        f_s1 = mk_scale(4, 7)
        f_s2 = mk_copy(6)
        f_s3 = mk_scale(5, 8, mul=0.5)

    hT = K.sb([128, 8, T])
    sq = K.sb([128, 8, T], BF16)
    uT = K.sb([128, 8, T], BF16)
    yT = K.sb([128, 8, T])
    hid = K.sb([128, NF, T], BF16)
    tmpA = K.sb([128, T])
    tmpB = K.sb([128, T])
    rstd = K.sb([128, T])
    wg = [K.sb([128, 8, 512], BF16) for _ in range(2)]
    wu = [K.sb([128, 8, 512], BF16) for _ in range(2)]
    wdn = [K.sb([128, NF, 512], BF16) for _ in range(2)]
    if do_wout:
        mixT = K.sb([128, 8, T], BF16)
        lnp = K.sb([128, 3, 2]); S.dma("sp", lnp.t[:], lnp_d, writes=[lnp.r])
        blk = K.sb([128, 128]); S.dma("sp", blk.t[:], blk_d, writes=[blk.r])
        gnb = K.sb([128, 1]); S.op("pool", lambda: nc.gpsimd.memset(gnb.t[:], 64e-5), writes=[gnb.r])
        rwt = [K.sb([128, T]) for _ in range(6)]

    def residual_update(s3):
        for d in range(8):
            S.op("dve", lambda: nc.vector.tensor_tensor(out=tmpA.t[:], in0=yT.t[:, d, :], in1=rstd.t[:], op=ALU.mult),
                 reads=[yT.r, rstd.r], writes=[tmpA.r])
            for (lo, hi, s) in ((0, TL, 0), (TL, T, 1)):
                S.op("dve", lambda: nc.vector.scalar_tensor_tensor(out=hT.t[:, d, lo:hi], in0=tmpA.t[:, lo:hi],
                                                                    scalar=s3.t[:, d, s:s + 1], in1=hT.t[:, d, lo:hi],
                                                                    op0=ALU.mult, op1=ALU.add),
                     reads=[tmpA.r, s3.r, hT.r], writes=[hT.r])

    for ps_ in range(NPASS):
        c0 = ps_ * T
        S.dma("sp", hT.t[:], h_d[:, c0:c0 + T].rearrange("(k p) t -> p k t", p=128), writes=[hT.r])
        if do_wout:
            S.dma("pool", mixT.t[:, 0:2, :], mixh_d[:, c0:c0 + T].rearrange("(k p) t -> p k t", p=128), writes=[mixT.r])
            S.dma("pool", mixT.t[:, 5:8, :], mixn_d[:, c0:c0 + T].rearrange("(k p) t -> p k t", p=128), writes=[mixT.r])
            for ck in range(3):
                for i_ in range(6):
                    S.dma("sp" if i_ % 2 == 0 else "act", rwt[i_].t[:], rw_d[i_, ck * 128:(ck + 1) * 128, c0:c0 + T], writes=[rwt[i_].r])
                yf, yb, bf_, bb_, vv, gg = rwt

                def TTd(o, a, b, op):
                    S.op("dve", lambda: nc.vector.tensor_tensor(out=o.t[:], in0=a.t[:], in1=b.t[:], op=op), reads=[a.r, b.r], writes=[o.r])
                TTd(yf, yf, yb, ALU.add)
                pmean = P[1]
                for h in range(2):
                    S.op("pe", lambda: nc.tensor.matmul(pmean.t[:, h, 0:HALF], lhsT=blk.t[:], rhs=yf.t[:, h * HALF:(h + 1) * HALF],
                                                        start=True, stop=True), reads=[blk.r, yf.r], writes=[pmean.r])
                S.op("dve", lambda: nc.vector.tensor_tensor(out=yb.t[:].rearrange("p (h t) -> p h t", h=2),
                                                            in0=yf.t[:].rearrange("p (h t) -> p h t", h=2), in1=pmean.t[:, :, 0:HALF],
                                                            op=ALU.subtract), reads=[yf.r, pmean.r], writes=[yb.r])
                TTd(yf, yb, yb, ALU.mult)
                pvar = P[2]
                for h in range(2):
                    S.op("pe", lambda: nc.tensor.matmul(pvar.t[:, h, 0:HALF], lhsT=blk.t[:], rhs=yf.t[:, h * HALF:(h + 1) * HALF],
                                                        start=True, stop=True), reads=[blk.r, yf.r], writes=[pvar.r])
                S.op("act", lambda: nc.scalar.activation(out=yf.t[:].rearrange("p (h t) -> p h t", h=2), in_=pvar.t[:, :, 0:HALF],
                                                         func=AF.Sqrt, bias=gnb.t[:, 0:1]), reads=[pvar.r, gnb.r], writes=[yf.r])
                S.op("dve", lambda: nc.vector.reciprocal(out=yf.t[:], in_=yf.t[:]), reads=[yf.r], writes=[yf.r])
                TTd(yb, yb, yf, ALU.mult)
                S.op("act", lambda: nc.scalar.activation(out=yb.t[:], in_=yb.t[:], func=AF.Identity, scale=lnp.t[:, ck, 0:1],
                                                         bias=lnp.t[:, ck, 1:2]), reads=[yb.r, lnp.r], writes=[yb.r])
                TTd(bf_, bf_, bb_, ALU.add)
                TTd(bf_, bf_, vv, ALU.mult)
                TTd(yb, yb, bf_, ALU.add)
                S.op("dve", lambda: nc.vector.tensor_tensor(out=mixT.t[:, 2 + ck, :], in0=yb.t[:], in1=gg.t[:], op=ALU.mult),
                     reads=[yb.r, gg.r], writes=[mixT.r])
            for half_d in range(2):
                wb = wg[half_d]
                S.dma("pool", wb.t[:], wout_d[:, half_d * 512:(half_d + 1) * 512].rearrange("(k p) n -> p k n", p=128),
                      writes=[wb.r])
                for dd in range(4):
                    d = half_d * 4 + dd
                    pt = P[d % 4]
                    for h in range(2):
                        for k in range(8):
                            S.op("pe", lambda: nc.tensor.matmul(pt.t[:, h, 0:HALF], lhsT=wb.t[:, k, dd * 128:(dd + 1) * 128],
                                                                rhs=mixT.t[:, k, h * HALF:(h + 1) * HALF],
                                                                start=(k == 0), stop=(k == 7)),
                                 reads=[wb.r, mixT.r], writes=[pt.r])
                    S.op("act", lambda: nc.scalar.copy(out=yT.t[:, d, :].rearrange("p (h t) -> p h t", h=2), in_=pt.t[:, :, 0:HALF]),
                         reads=[pt.r], writes=[yT.r])
            emit_rstd(K, yT, sq, P[0], rstd, tmpB)
            residual_update(o_s3)

        emit_rstd(K, hT, sq, P[0], rstd, tmpB)
        emit_modnorm(K, hT, rstd, tmpA, f_s1, f_s2, uT)
        ngrp = (NF + 3) // 4
        for g in range(ngrp):
            f0 = g * 4
            nf = min(4, NF - f0)
            bg, bu = wg[g % 2], wu[g % 2]
            S.dma("pool", bg.t[:, :, 0:nf * 128], wgu_d[:, f0 * 128:(f0 + nf) * 128].rearrange("(k p) n -> p k n", p=128),
                  writes=[bg.r])
            S.dma("pool", bu.t[:, :, 0:nf * 128],
                  wgu_d[:, DFF + f0 * 128:DFF + (f0 + nf) * 128].rearrange("(k p) n -> p k n", p=128), writes=[bu.r])
            for ff in range(nf):
                f = f0 + ff
                pg, pu = (P[0], P[1]) if f % 2 == 0 else (P[2], P[3])
                for h in range(2):
                    for k in range(8):
                        S.op("pe", lambda: nc.tensor.matmul(pg.t[:, h, 0:HALF], lhsT=bg.t[:, k, ff * 128:(ff + 1) * 128],
                                                            rhs=uT.t[:, k, h * HALF:(h + 1) * HALF], start=(k == 0), stop=(k == 7)),
                             reads=[bg.r, uT.r], writes=[pg.r])
                    for k in range(8):
                        S.op("pe", lambda: nc.tensor.matmul(pu.t[:, h, 0:HALF], lhsT=bu.t[:, k, ff * 128:(ff + 1) * 128],
                                                            rhs=uT.t[:, k, h * HALF:(h + 1) * HALF], start=(k == 0), stop=(k == 7)),
                             reads=[bu.r, uT.r], writes=[pu.r])
                sg = tmpA if f % 2 == 0 else tmpB
                S.op("act", lambda: nc.scalar.activation(out=sg.t[:].rearrange("p (h t) -> p h t", h=2), in_=pg.t[:, :, 0:HALF],
                                                         func=AF.Silu), reads=[pg.r], writes=[sg.r])
                S.op("dve", lambda: nc.vector.tensor_tensor(out=hid.t[:, f, :].rearrange("p (h t) -> p h t", h=2),
                                                            in0=sg.t[:].rearrange("p (h t) -> p h t", h=2),
                                                            in1=pu.t[:, :, 0:HALF], op=ALU.mult),
                     reads=[sg.r, pu.r], writes=[hid.r])
        for half_d in range(2):
            wb = wdn[half_d]
            S.dma("pool", wb.t[:], wdn_d[:, half_d * 512:(half_d + 1) * 512].rearrange("(f p) n -> p f n", p=128), writes=[wb.r])
            for dd in range(4):
                d = half_d * 4 + dd
                pt = P[d % 4]
                for h in range(2):
                    for f in range(NF):
                        S.op("pe", lambda: nc.tensor.matmul(pt.t[:, h, 0:HALF], lhsT=wb.t[:, f, dd * 128:(dd + 1) * 128],
                                                            rhs=hid.t[:, f, h * HALF:(h + 1) * HALF],
                                                            start=(f == 0), stop=(f == NF - 1)),
                             reads=[wb.r, hid.r], writes=[pt.r])
                S.op("act", lambda: nc.scalar.copy(out=yT.t[:, d, :].rearrange("p (h t) -> p h t", h=2), in_=pt.t[:, :, 0:HALF]),
                     reads=[pt.r], writes=[yT.r])
        emit_rstd(K, yT, sq, P[0], rstd, tmpB)
        residual_update(f_s3)
        S.dma("sp", hout_d[:, c0:c0 + T].rearrange("(k p) t -> p k t", p=128), hT.t[:], reads=[hT.r], writes=[K_out_res(K)])

        if do_win:
            emit_rstd(K, hT, sq, P[0], rstd, tmpB)
            emit_modnorm(K, hT, rstd, tmpA, x_s1, x_s2, uT)
            ngrp = (NIN + 3) // 4
            for g in range(ngrp):
                f0 = g * 4
                nf = min(4, NIN - f0)
                bg = wg[g % 2]
                S.dma("pool", bg.t[:, :, 0:nf * 128], win_d[:, f0 * 128:(f0 + nf) * 128].rearrange("(k p) n -> p k n", p=128),
                      writes=[bg.r])
                for ff in range(nf):
                    f = f0 + ff
                    pt = P[f % 4]
                    for h in range(2):
                        for k in range(8):
                            S.op("pe", lambda: nc.tensor.matmul(pt.t[:, h, 0:HALF], lhsT=bg.t[:, k, ff * 128:(ff + 1) * 128],
                                                                rhs=uT.t[:, k, h * HALF:(h + 1) * HALF], start=(k == 0), stop=(k == 7)),
                                 reads=[bg.r, uT.r], writes=[pt.r])
                    ob = yT
                    S.op("act" if f % 2 == 0 else "dve",
                         (lambda: nc.scalar.copy(out=ob.t[:, f % 8, :].rearrange("p (h t) -> p h t", h=2), in_=pt.t[:, :, 0:HALF]))
                         if f % 2 == 0 else
                         (lambda: nc.vector.tensor_copy(out=ob.t[:, f % 8, :].rearrange("p (h t) -> p h t", h=2), in_=pt.t[:, :, 0:HALF])),
                         reads=[pt.r], writes=[ob.r])
                    if f % 8 == 7 or f == NIN - 1:
                        fa = (f // 8) * 8
                        n = f - fa + 1
                        S.dma("sp", p_d[fa * 128:(fa + n) * 128, c0:c0 + T].rearrange("(k p) t -> p k t", p=128), ob.t[:, 0:n, :],
                              reads=[ob.r], writes=[K_out_res(K)])
    S.finish("sp", K.outs)
    return K


def K_out_res(K):
    if not hasattr(K, "outs"):
        K.outs = []
    r = Res()
    K.outs.append(r)
    return r


_CACHE = {}


def get_kernel(key, fn):
    if key not in _CACHE:
        _CACHE[key] = fn()
    return _CACHE[key]


TRACE = [False]


def run(K, in_maps):
    if TRACE[0]:
        res = run_bass_kernel_spmd(K.nc, in_maps, core_ids=list(range(NCORE)), trace=True)
        print("EXEC_NS", res.exec_time_ns)
        return res.results
    res = run_bass_kernel_spmd(K.nc, in_maps, core_ids=list(range(NCORE)))
    return res.results


def tok_layout(h_lat, h_ctx):
    outs = []
    for c in range(NCORE):
        b, q = c // 4, c % 4
        lat = h_lat[b, q * 2048:(q + 1) * 2048].reshape(NPASS, TL, -1)
        cx = h_ctx[b, q * 64:(q + 1) * 64].reshape(NPASS, TC, -1)
        a = np.concatenate([lat, cx], axis=1).reshape(TOK, -1)
        outs.append(np.ascontiguousarray(a.T))
    return outs


def tok_unlayout(per_core):
    F = per_core[0].shape[0]
    lat = np.zeros((2, SEQ, F), per_core[0].dtype)
    cx = np.zeros((2, CTX, F), per_core[0].dtype)
    for c in range(NCORE):
        b, q = c // 4, c % 4
        a = per_core[c].T.reshape(NPASS, T, F)
        lat[b, q * 2048:(q + 1) * 2048] = a[:, :TL].reshape(2048, F)
        cx[b, q * 64:(q + 1) * 64] = a[:, TL:].reshape(64, F)
    return lat, cx


def fm(v, n):
    return np.ascontiguousarray(v.reshape(n, 128).T)


def common_maps(l, m_all, norm_g, j0, nj):
    maps = []
    for core in range(NCORE):
        b = core // 4
        ml = m_all[l, b].reshape(9, 8, 128)[j0:j0 + nj]
        mc = m_all[l, 2].reshape(9, 8, 128)[j0:j0 + nj]
        m = np.stack([ml, mc], axis=-1).transpose(2, 0, 1, 3).reshape(128, nj * 8, 2)
        maps.append({
            "m": np.ascontiguousarray(m, dtype=np.float32),
            "ng": np.ascontiguousarray(norm_g[l].reshape(6, 8, 128).transpose(2, 0, 1)),
        })
    return maps


def build_mod_kernel():
    K = Ctx()
    nc, S = K.nc, K.S
    cc_d = K.din("cc", [128, 8, 3])
    w_d = K.din("w", [4, D, 1152])
    b_d = K.din("b", [128, 36])
    o_d = K.dout("m", [128, 36, 3])
    cc = K.sb([128, 8, 3]); S.dma("sp", cc.t[:], cc_d, writes=[cc.r])
    ccb = K.sb([128, 8, 3], BF16)
    S.op("act", lambda: nc.scalar.activation(out=ccb.t[:], in_=cc.t[:], func=AF.Silu), reads=[cc.r], writes=[ccb.r])
    mb = K.sb([128, 36]); S.dma("sp", mb.t[:], b_d, writes=[mb.r])
    ps = K.ps([128, 36, 3])
    wb = [K.sb([128, 8, 1152], BF16) for _ in range(2)]
    for l in range(4):
        w = wb[l % 2]
        S.dma("pool", w.t[:], w_d[l].rearrange("(k p) n -> p k n", p=128), writes=[w.r])
        for fc in range(9):
            for k in range(8):
                S.op("pe", lambda: nc.tensor.matmul(ps.t[:, l * 9 + fc, :], lhsT=w.t[:, k, fc * 128:(fc + 1) * 128], rhs=ccb.t[:, k, :],
                                                    start=(k == 0), stop=(k == 7)), reads=[w.r, ccb.r], writes=[ps.r])
    ms = K.sb([128, 36, 3])
    for s_ in range(3):
        S.op("dve", lambda: nc.vector.tensor_tensor(out=ms.t[:, :, s_], in0=ps.t[:, :, s_], in1=mb.t[:], op=ALU.add),
             reads=[ps.r, mb.r], writes=[ms.r])
    r_ = Res()
    S.dma("sp", o_d, ms.t[:], reads=[ms.r], writes=[r_])
    S.finish("sp", [r_])
    return K


def mod_maps(c, c_ctx, mod_w, mod_b):
    cc = np.stack([c[0].reshape(8, 128).T, c[1].reshape(8, 128).T, c_ctx.reshape(8, 128).T], axis=-1).astype(np.float32)
    maps = []
    for core in range(NCORE):
        cs = slice(core * 1152, (core + 1) * 1152)
        b = mod_b[:, cs].reshape(4, 9, 128).transpose(2, 0, 1).reshape(128, 36)
        maps.append({"cc": np.ascontiguousarray(cc), "w": np.ascontiguousarray(mod_w[:, :, cs]), "b": np.ascontiguousarray(b)})
    return maps


def mod_unpack(res):
    m_all = np.zeros((4, 3, 9 * D), np.float32)
    for core in range(NCORE):
        mm = res[core]["m"].reshape(128, 4, 9, 3)
        m_all[:, :, core * 1152:(core + 1) * 1152] = mm.transpose(1, 3, 2, 0).reshape(4, 3, 1152)
    return m_all


RW_EVERY = [6]
RN = CTX + SEQ
RC = 64
RBLK = 256
RNB = RN // RBLK
RROWS = 9 * 64 + 64 + 64 + 128


def build_rwkv_kernel(nblk=RNB):
    K = Ctx()
    nc, S = K.nc, K.S
    pin_d = K.din("pin", [RROWS, RN + 4])
    mu_d = K.din("mu", [128, 12, 2])
    hp_d = K.din("hp", [64, 3, 5])
    w2_d = K.din("w2", [64, 3, 64])
    a2_d = K.din("a2", [64, 3, 64])
    g2_d = K.din("g2", [128, 3, 64])
    cs_d = K.din("cs", [64, 2, RN])
    cm_d = K.din("cm", [64, 6, 256])
    y_d = K.dout("y", [64, 3, RN // RC, 64])
    bon_d = K.dout("bon", [64, 3, RN])
    v_d = K.dout("vout", [64, 3, RN])
    g_d = K.dout("gout", [64, 3, RN])

    def A(tb, ap=None):
        return (tb, tb.t[:] if ap is None else ap)

    def TT(e, o, a, b, op):
        eng = nc.vector if e == "dve" else nc.gpsimd
        S.op(e, lambda: eng.tensor_tensor(out=o[1], in0=a[1], in1=b[1], op=op), reads=[a[0].r, b[0].r], writes=[o[0].r])

    def STT(o, a, sc, b, op0, op1, extra=()):
        S.op("dve", lambda: nc.vector.scalar_tensor_tensor(out=o[1], in0=a[1], scalar=sc, in1=b[1], op0=op0, op1=op1),
             reads=[a[0].r, b[0].r] + list(extra), writes=[o[0].r])

    def TS(o, a, s1, s2, op0, op1=None, extra=()):
        if op1 is None:
            S.op("dve", lambda: nc.vector.tensor_scalar(out=o[1], in0=a[1], scalar1=s1, scalar2=None, op0=op0),
                 reads=[a[0].r] + list(extra), writes=[o[0].r])
        else:
            S.op("dve", lambda: nc.vector.tensor_scalar(out=o[1], in0=a[1], scalar1=s1, scalar2=s2, op0=op0, op1=op1),
                 reads=[a[0].r] + list(extra), writes=[o[0].r])

    def ACT(o, a, func, scale=1.0, bias=None, extra=()):
        if bias is None:
            S.op("act", lambda: nc.scalar.activation(out=o[1], in_=a[1], func=func, scale=scale),
                 reads=[a[0].r] + list(extra), writes=[o[0].r])
        else:
            S.op("act", lambda: nc.scalar.activation(out=o[1], in_=a[1], func=func, scale=scale, bias=bias),
                 reads=[a[0].r] + list(extra), writes=[o[0].r])

    def MM(o, l, r, start=True, stop=True):
        S.op("pe", lambda: nc.tensor.matmul(o[1], lhsT=l[1], rhs=r[1], start=start, stop=stop),
             reads=[l[0].r, r[0].r], writes=[o[0].r])

    cm = K.sb([64, 6, 256]); S.dma("sp", cm.t[:], cm_d, writes=[cm.r])
    cmb = K.sb([64, 6, 256], BF16); S.dma("pool", cmb.t[:], cm_d, writes=[cmb.r])
    mu = K.sb([128, 12, 2]); S.dma("sp", mu.t[:], mu_d, writes=[mu.r])
    muc = K.sb([128, 12, 1])
    STT(A(muc), A(mu, mu.t[:, :, 0:1]), -1.0, A(mu, mu.t[:, :, 1:2]), ALU.mult, ALU.subtract)
    TS(A(muc), A(muc), 1.0, None, ALU.add)
    hp = K.sb([64, 3, 5]); S.dma("sp", hp.t[:], hp_d, writes=[hp.r])
    omka = K.sb([64, 3, 1])
    TS(A(omka), A(hp, hp.t[:, :, 1:2]), -1.0, 1.0, ALU.mult, ALU.add)
    w2 = K.sb([64, 3, 64], BF16); S.dma("pool", w2.t[:], w2_d, writes=[w2.r])
    a2 = K.sb([64, 3, 64], BF16); S.dma("pool", a2.t[:], a2_d, writes=[a2.r])
    g2 = K.sb([128, 3, 64], BF16); S.dma("pool", g2.t[:], g2_d, writes=[g2.r])
    ones = K.sb([64, 64]); S.op("pool", lambda: nc.gpsimd.memset(ones.t[:], 1.0), writes=[ones.r])
    rkb = K.sb([64, 3, 64])
    for h in range(3):
        TS(A(rkb, rkb.t[:, h, :]), A(ones), hp.t[:, h, 2:3], None, ALU.mult, extra=[hp.r])
    m01 = A(cm, cm.t[:, 0, :])
    m_su = A(cm, cm.t[:, 1, :].rearrange("p (c t) -> p c t", c=4))
    m_ui = A(cm, cm.t[:, 2, :].rearrange("p (c t) -> p c t", c=4))
    m_sl = A(cm, cm.t[:, 3, :].rearrange("p (c t) -> p c t", c=4))
    i4 = A(cm, cm.t[:, 4, :].rearrange("p (c t) -> p c t", c=4))
    identb = A(cmb, cmb.t[:, 4, 0:64])
    ropeT = A(cm, cm.t[:, 5, 0:64])

    pm = [K.ps([128, 512]) for _ in range(3)]
    pc = [K.ps([128, 512]) for _ in range(3)]
    ptb = K.ps([128, 1024], BF16)
    pseq = K.ps([128, 512])
    cnt = {"pm": 0, "pc": 0}

    def PM():
        cnt["pm"] += 1
        tb = pm[cnt["pm"] % 3]
        return (tb, tb.t[0:64, 0:256])

    def PC():
        cnt["pc"] += 1
        tb = pc[cnt["pc"] % 3]
        return (tb, tb.t[0:64, 0:256].rearrange("p (c t) -> p c t", c=4))

    def c4(tb):
        return (tb, tb.t[:].rearrange("p (c t) -> p c t", c=4))

    def mk(n, dt=F32, p=64, w=256):
        return [K.sb([p, w], dt) for _ in range(n)]

    NB2 = 2
    raw = [K.sb([128, 3, 256]) for _ in range(12)]
    SH = dict(twd=K.sb([64, 256], BF16), adb=K.sb([64, 256], BF16), sgd=K.sb([128, 256], BF16), tmp=K.sb([128, 256]),
              cs=K.sb([64, 2, 256]))
    NB3 = 3
    keep_f = ["pt", "U0"]
    keep_b = ["Rt", "Bhtok", "Khtok", "Vtok", "NrbT", "NrkT", "WT"]
    eo_b = ["At", "Bt", "Kt", "Bh", "Kh", "vb"]
    scr_f = ["xr", "xk", "a", "kk", "kd", "bb", "t1", "t2", "t3", "ld", "cl"]
    scr_b = ["Atok", "Xb", "Mb", "TT", "MakT", "G"]
    scratch = [dict([(n, K.sb([64, 256])) for n in scr_f] + [(n, K.sb([64, 256], BF16)) for n in scr_b]) for _ in range(3)]
    eout = [[dict([(n, K.sb([64, 256], BF16)) for n in eo_b]) for h in range(3)] for _ in range(2)]
    keep = [[dict([(n, K.sb([64, 256])) for n in keep_f] + [(n, K.sb([64, 256], BF16)) for n in keep_b]) for h in range(3)]
            for _ in range(NB3)]

    def HD(blk, h):
        d = dict(scratch[h])
        d.update(eout[blk % 2][h])
        d.update(keep[blk % NB3][h])
        for a_, b_ in (("rs", "xr"), ("kks", "kk"), ("kds", "kd"), ("bbs", "bb"), ("tmp", "t3")):
            d[a_] = d[b_]
        return d
    ost = [dict(y=K.sb([64, 3, 4, 64]), bon=K.sb([64, 3, 256]), v=K.sb([64, 3, 256]), g=K.sb([64, 3, 256])) for _ in range(NB3)]
    H = [K.sb([64, 64]) for _ in range(3)]
    Hb = [[K.sb([64, 64], BF16) for _ in range(2)] for _ in range(3)]
    Ub = [[K.sb([64, 64], BF16) for _ in range(2)] for _ in range(3)]
    for h in range(3):
        S.op("pool", lambda: nc.gpsimd.memset(H[h].t[:], 0.0), writes=[H[h].r])
        S.op("pool", lambda: nc.gpsimd.memset(Hb[h][0].t[:], 0.0), writes=[Hb[h][0].r])
    outs = []
    x9 = K.sb([64, 256])
    x11 = K.sb([128, 256])
    ptbs = [K.ps([128, 1024], BF16)] if False else [ptb]

    def shift(gi, out, tmp_tb, rows=64):
        tmp = (tmp_tb, tmp_tb.t[0:rows, :])
        ACT(tmp, A(raw[gi], raw[gi].t[0:rows, 1, :]), AF.Identity, scale=muc.t[0:rows, gi, :], extra=[muc.r])
        STT(tmp, A(raw[gi], raw[gi].t[0:rows, 0, :]), mu.t[0:rows, gi, 0:1], tmp, ALU.mult, ALU.add, extra=[mu.r])
        STT(out, A(raw[gi], raw[gi].t[0:rows, 2, :]), mu.t[0:rows, gi, 1:2], tmp, ALU.mult, ALU.add, extra=[mu.r])

    def pre_shared(blk):
        t0 = blk * RBLK
        cb = t0 + 1 if t0 < CTX else t0 + 3
        for gi in range(12):
            rows = 128 if gi == 11 else 64
            r0 = gi * 64
            for s_ in range(3):
                S.dma("sp" if (gi + s_) % 2 == 0 else "act", raw[gi].t[0:rows, s_, :],
                      pin_d[r0:r0 + rows, cb - 1 + s_:cb - 1 + s_ + RBLK], writes=[raw[gi].r])
        S.dma("sp", SH["cs"].t[:], cs_d[:, :, t0:t0 + RBLK], writes=[SH["cs"].r])
        shift(9, A(x9), SH["tmp"]); ACT(A(SH["twd"]), A(x9), AF.Tanh)
        shift(10, A(x9), SH["tmp"]); ACT(A(SH["adb"]), A(x9), AF.Copy)
        shift(11, A(x11), SH["tmp"], rows=128); ACT(A(SH["sgd"]), A(x11), AF.Sigmoid)

    def pre_E(blk, h):
        O = ost[blk % NB3]
        Dh = HD(blk, h)
        cosb = A(SH["cs"], SH["cs"].t[:, 0, :])
        sinb = A(SH["cs"], SH["cs"].t[:, 1, :])
        shift(3 * h + 0, A(Dh["xr"]), Dh["tmp"]); yield
        shift(3 * h + 1, A(Dh["xk"]), Dh["tmp"]); yield
        shift(3 * h + 2, A(O["v"], O["v"].t[:, h, :]), Dh["tmp"]); yield
        xv = A(O["v"], O["v"].t[:, h, :])
        p1 = PM(); MM(p1, A(w2, w2.t[:, h, :]), A(SH["twd"]))
        ACT(A(Dh["ld"]), p1, AF.Sigmoid, bias=hp.t[:, h, 3:4], extra=[hp.r])
        TS(A(Dh["ld"]), A(Dh["ld"]), -0.6065306597126334, None, ALU.mult); yield
        p2 = PM(); MM(p2, A(a2, a2.t[:, h, :]), A(SH["adb"]))
        ACT(A(Dh["a"]), p2, AF.Sigmoid, bias=hp.t[:, h, 4:5], extra=[hp.r]); yield
        p3 = PM(); MM(p3, A(g2, g2.t[:, h, :]), A(SH["sgd"]))
        ACT(A(O["g"], O["g"].t[:, h, :]), p3, AF.Copy); yield
        TS(A(Dh["t1"]), A(Dh["xk"]), hp.t[:, h, 0:1], None, ALU.mult, extra=[hp.r])
        TT("pool", A(Dh["t2"]), A(Dh["t1"]), A(Dh["t1"]), ALU.mult)
        p4 = PM(); MM(p4, A(ones), A(Dh["t2"])); yield
        ACT(A(Dh["t3"]), p4, AF.Sqrt)
        TS(A(Dh["t3"]), A(Dh["t3"]), 1e-12, None, ALU.max)
        S.op("dve", lambda: nc.vector.reciprocal(out=Dh["t3"].t[:], in_=Dh["t3"].t[:]), reads=[Dh["t3"].r], writes=[Dh["t3"].r])
        TT("pool", A(Dh["kk"]), A(Dh["t1"]), A(Dh["t3"]), ALU.mult); yield
        ACT(A(Dh["t1"]), A(Dh["a"]), AF.Identity, scale=hp.t[:, h, 1:2], bias=omka.t[:, h, :], extra=[hp.r, omka.r])
        TT("pool", A(Dh["kd"]), A(Dh["xk"]), A(Dh["t1"]), ALU.mult)
        TT("dve", A(Dh["bb"]), A(Dh["kk"]), A(Dh["a"]), ALU.mult); yield
        TT("pool", A(Dh["t2"]), A(Dh["xr"]), A(Dh["kd"]), ALU.mult)
        p5 = PM(); MM(p5, A(rkb, rkb.t[:, h, :]), A(Dh["t2"]))
        ACT(A(O["bon"], O["bon"].t[:, h, :]), p5, AF.Copy); yield
        for i_, (src, dst) in enumerate((("xr", "rs"), ("kk", "kks"), ("kd", "kds"), ("bb", "bbs"))):
            pr = PM(); MM(pr, ropeT, A(Dh[src]))
            TT("pool", A(Dh["t1"]), A(Dh[src]), cosb, ALU.mult)
            TT("dve", A(Dh["t2"]), pr, sinb, ALU.mult)
            TT("pool" if i_ % 2 else "dve", A(Dh[dst]), A(Dh["t1"]), A(Dh["t2"]), ALU.add); yield
        S.op("dve", lambda: nc.vector.tensor_tensor_scan(out=Dh["cl"].t[:], data0=m01[1], data1=Dh["ld"].t[:], initial=0.0,
                                                         op0=ALU.mult, op1=ALU.add),
             reads=[cm.r, Dh["ld"].r], writes=[Dh["cl"].r])
        ACT(A(Dh["pt"]), A(Dh["cl"]), AF.Exp); yield
        ACT(A(Dh["t3"]), A(Dh["cl"]), AF.Exp, scale=-1.0)
        TT("pool", A(Dh["Bt"]), A(Dh["bbs"]), A(Dh["t3"]), ALU.mult)
        TT("dve", A(Dh["Kt"]), A(Dh["kds"]), A(Dh["t3"]), ALU.mult); yield
        TT("pool", A(Dh["t1"]), A(Dh["cl"]), A(Dh["ld"]), ALU.subtract)
        ACT(A(Dh["t1"]), A(Dh["t1"]), AF.Exp)
        STT(A(Dh["At"]), A(Dh["kks"]), -1.0, A(Dh["t1"]), ALU.mult, ALU.mult); yield
        cl3 = Dh["cl"].t[:].rearrange("p (c t) -> p c t", c=4)
        TT("dve", c4(Dh["t2"]), A(Dh["cl"], cl3[:, :, 63:64].to_broadcast([64, 4, 64])), A(Dh["cl"], cl3), ALU.subtract)
        ACT(A(Dh["t2"]), A(Dh["t2"]), AF.Exp)
        TT("pool", A(Dh["Rt"]), A(Dh["rs"]), A(Dh["pt"]), ALU.mult)
        TT("dve", A(Dh["Bh"]), A(Dh["bbs"]), A(Dh["t2"]), ALU.mult)
        TT("pool", A(Dh["Kh"]), A(Dh["kds"]), A(Dh["t2"]), ALU.mult)
        ACT(A(Dh["vb"]), xv, AF.Copy); yield

    def pre_I(blk, h):
        Dh = HD(blk, h)
        for i_, (src, dst) in enumerate((("At", "Atok"), ("Bh", "Bhtok"), ("Kh", "Khtok"), ("vb", "Vtok"))):
            pt_ = (ptb, ptb.t[0:64, i_ * 256:(i_ + 1) * 256])
            for c in range(4):
                S.op("pe", lambda: nc.tensor.transpose(out=ptb.t[0:64, i_ * 256 + c * 64:i_ * 256 + (c + 1) * 64],
                                                       in_=Dh[src].t[:, c * 64:(c + 1) * 64], identity=identb[1]),
                     reads=[Dh[src].r, cmb.r], writes=[ptb.r])
            ACT(A(Dh[dst]), pt_, AF.Copy); yield

        def chunk_mm(l, r):
            p_ = PC()
            for c in range(4):
                MM((p_[0], p_[1][:, c, :]), (l, l.t[:, c * 64:(c + 1) * 64]), (r, r.t[:, c * 64:(c + 1) * 64]))
            return p_

        px = chunk_mm(Dh["Bt"], Dh["At"]); TT("dve", c4(Dh["Xb"]), px, m_su, ALU.mult); yield
        pmm = chunk_mm(Dh["At"], Dh["Bt"]); TT("dve", c4(Dh["Mb"]), pmm, m_sl, ALU.mult); yield
        pq = chunk_mm(Dh["Kt"], Dh["At"]); TT("dve", c4(Dh["MakT"]), pq, m_su, ALU.mult); yield
        pq = chunk_mm(Dh["Bt"], Dh["Rt"]); TT("dve", c4(Dh["NrbT"]), pq, m_ui, ALU.mult); yield
        pq = chunk_mm(Dh["Kt"], Dh["Rt"]); TT("dve", c4(Dh["NrkT"]), pq, m_ui, ALU.mult); yield
        TT("pool", c4(Dh["TT"]), c4(Dh["Xb"]), i4, ALU.add)
        for lev in range(5):
            pM2 = chunk_mm(Dh["Xb"], Dh["Mb"])
            if lev < 4:
                pX2 = chunk_mm(Dh["Mb"], Dh["Xb"])
                ACT(c4(Dh["Xb"]), pX2, AF.Copy)
            TT("dve", c4(Dh["G"]), pM2, i4, ALU.add)
            if lev < 4:
                S.op("dve", lambda: nc.vector.tensor_copy(out=Dh["Mb"].t[:].rearrange("p (c t) -> p c t", c=4), in_=pM2[1]),
                     reads=[pM2[0].r], writes=[Dh["Mb"].r])
            yield
            pT = chunk_mm(Dh["G"], Dh["TT"])
            ACT(c4(Dh["TT"]), pT, AF.Copy); yield
        pw = chunk_mm(Dh["Atok"], Dh["TT"]); ACT(c4(Dh["WT"]), pw, AF.Copy); yield
        pg_ = chunk_mm(Dh["MakT"], Dh["Vtok"]); ACT(c4(Dh["G"]), pg_, AF.Copy); yield
        pu0 = chunk_mm(Dh["TT"], Dh["G"])
        S.op("dve", lambda: nc.vector.tensor_copy(out=Dh["U0"].t[:].rearrange("p (c t) -> p c t", c=4), in_=pu0[1]),
             reads=[pu0[0].r], writes=[Dh["U0"].r])
        yield

    def seq_block(blk):
        O = ost[blk % NB3]
        for c in range(4):
            gch = blk * 4 + c
            cur, nxt = gch % 2, (gch + 1) % 2
            sl = slice(c * 64, (c + 1) * 64)
            for h in range(3):
                Dh = HD(blk, h)
                pU = (pseq, pseq.t[0:64, h * 64:(h + 1) * 64])
                MM(pU, (Dh["WT"], Dh["WT"].t[:, sl]), A(Hb[h][cur]))
                TT("dve", A(Ub[h][cur]), pU, (Dh["U0"], Dh["U0"].t[:, sl]), ALU.add)
                yield
            for h in range(3):
                Dh = HD(blk, h)
                pH = (pseq, pseq.t[0:64, 192 + h * 64:192 + (h + 1) * 64])
                MM(pH, (Dh["Khtok"], Dh["Khtok"].t[:, sl]), (Dh["Vtok"], Dh["Vtok"].t[:, sl]), start=True, stop=False)
                MM(pH, (Dh["Bhtok"], Dh["Bhtok"].t[:, sl]), A(Ub[h][cur]), start=False, stop=True)
                pY = (pm[2], pm[2].t[0:64, 256 + h * 64:256 + (h + 1) * 64])
                MM(pY, (Dh["Rt"], Dh["Rt"].t[:, sl]), A(Hb[h][cur]), start=True, stop=False)
                MM(pY, (Dh["NrbT"], Dh["NrbT"].t[:, sl]), A(Ub[h][cur]), start=False, stop=False)
                MM(pY, (Dh["NrkT"], Dh["NrkT"].t[:, sl]), (Dh["Vtok"], Dh["Vtok"].t[:, sl]), start=False, stop=True)
                STT(A(H[h]), A(H[h]), Dh["pt"].t[:, c * 64 + 63:c * 64 + 64], pH, ALU.mult, ALU.add, extra=[Dh["pt"].r])
                ACT(A(Hb[h][nxt]), A(H[h]), AF.Copy)
                ACT((O["y"], O["y"].t[:, h, c, :]), pY, AF.Copy)
                yield

    def drain(gens, seq=None, every=1):
        gens = list(gens) + ([seq] if seq is not None else [])
        while gens:
            for g_ in list(gens):
                try:
                    next(g_)
                except StopIteration:
                    gens.remove(g_)

    def block_out(blk):
        O = ost[blk % NB3]
        t0 = blk * RBLK
        ch0 = blk * 4
        r_ = Res(); outs.append(r_)
        S.dma("sp", y_d[:, :, ch0:ch0 + 4, :], O["y"].t[:], reads=[O["y"].r], writes=[r_])
        for (nm, dd) in (("bon", bon_d), ("v", v_d), ("g", g_d)):
            r_ = Res(); outs.append(r_)
            S.dma("act", dd[:, :, t0:t0 + RBLK], O[nm].t[:], reads=[O[nm].r], writes=[r_])

    pre_shared(0)
    drain([pre_E(0, h) for h in range(3)])
    g0 = [pre_I(0, h) for h in range(3)]
    if nblk > 1:
        pre_shared(1)
        g0 += [pre_E(1, h) for h in range(3)]
    drain(g0)
    for blk in range(nblk):
        gens = []
        if blk + 1 < nblk:
            gens += [pre_I(blk + 1, h) for h in range(3)]
        if blk + 2 < nblk:
            pre_shared(blk + 2)
            gens += [pre_E(blk + 2, h) for h in range(3)]
        drain(gens, seq=seq_block(blk), every=RW_EVERY[0])
        block_out(blk)
    S.finish("sp", outs)
    return K


def K_tmp64(K, name, p=64):
    if not hasattr(K, "_tmps"):
        K._tmps = {}
    if name not in K._tmps:
        K._tmps[name] = K.sb([p, 256])
    return K._tmps[name]


def rope_tables():
    nf = 16
    t = np.arange(SEQ)
    row = (t // 64).astype(np.float32)
    col = (t % 64).astype(np.float32)
    inv = (np.float32(10000.0) ** (-np.arange(nf, dtype=np.float32) / np.float32(nf))).astype(np.float32)
    ang_r = row[:, None] * inv
    ang_c = col[:, None] * inv
    ang = np.concatenate([ang_r, ang_r, ang_c, ang_c], axis=-1).astype(np.float32)
    return np.cos(ang).astype(np.float32), np.sin(ang).astype(np.float32)


def rwkv_consts():
    cm = np.zeros((64, 6, 256), np.float32)
    tt = np.arange(256)
    cm[:, 0, :] = (tt % 64 != 0).astype(np.float32)[None, :]
    s = np.arange(64)[:, None]
    t = np.arange(64)[None, :]
    for c in range(4):
        cm[:, 1, c * 64:(c + 1) * 64] = (t > s)
        cm[:, 2, c * 64:(c + 1) * 64] = (t >= s)
        cm[:, 3, c * 64:(c + 1) * 64] = (t < s)
        cm[:, 4, c * 64:(c + 1) * 64] = (t == s)
    Rm = np.zeros((64, 64), np.float32)
    for hf in range(2):
        for m in range(16):
            Rm[hf * 32 + m, hf * 32 + 16 + m] = -1.0
            Rm[hf * 32 + 16 + m, hf * 32 + m] = 1.0
    cm[:, 5, 0:64] = Rm.T
    return cm


def rwkv_maps(l, p_lat, p_ctx, rw_mu, rw_w0, rw_w2, rw_a0, rw_a2, rw_g2, rw_k_k, rw_k_a, rw_r_k):
    cos, sin = rope_tables()
    cm = rwkv_consts()
    maps = []
    o = 768
    for core in range(NCORE):
        b, d, g = core // 4, (core // 2) % 2, core % 2
        lat = p_lat[b, :, o:o + 1536]
        cx = p_ctx[b, :, o:o + 1536]
        cs = np.zeros((64, 2, RN), np.float32)
        cs[:, 0, :CTX] = 1.0
        if d == 0:
            cs[:, 0, CTX:] = cos.T
            cs[:, 1, CTX:] = sin.T
        else:
            lat = lat[::-1]
            cx = cx[::-1]
            cs[:, 0, CTX:] = cos[::-1].T
            cs[:, 1, CTX:] = sin[::-1].T
        feats = []
        for hh in range(3):
            hd_ = 3 * g + hh
            feats += [np.arange(hd_ * 64, hd_ * 64 + 64), 384 + np.arange(hd_ * 64, hd_ * 64 + 64),
                      768 + np.arange(hd_ * 64, hd_ * 64 + 64)]
        feats += [1152 + d * 64 + np.arange(64), 1280 + d * 64 + np.arange(64), 1408 + np.arange(128)]
        fidx = np.concatenate(feats)
        pin = np.zeros((RROWS, RN + 4), np.float32)
        pin[:, 1:1 + CTX] = cx[:, fidx].T
        pin[:, 3 + CTX:3 + CTX + SEQ] = lat[:, fidx].T
        mu = np.zeros((128, 12, 2), np.float32)
        for gi in range(12):
            f = feats[gi]
            mp, mn = rw_mu[l][0][f], rw_mu[l][1][f]
            if d == 1:
                mp, mn = mn, mp
            mu[:len(f), gi, 0] = mp
            mu[:len(f), gi, 1] = mn
        hp = np.zeros((64, 3, 5), np.float32)
        w2 = np.zeros((64, 3, 64), np.float32)
        a2 = np.zeros((64, 3, 64), np.float32)
        g2 = np.zeros((128, 3, 64), np.float32)
        for hh in range(3):
            hd_ = 3 * g + hh
            cs_ = slice(hd_ * 64, hd_ * 64 + 64)
            hp[:, hh, 0] = rw_k_k[l][cs_]
            hp[:, hh, 1] = rw_k_a[l][cs_]
            hp[:, hh, 2] = rw_r_k[l][hd_]
            hp[:, hh, 3] = rw_w0[l][d][cs_]
            hp[:, hh, 4] = rw_a0[l][d][cs_]
            w2[:, hh, :] = rw_w2[l][d][:, cs_]
            a2[:, hh, :] = rw_a2[l][d][:, cs_]
            g2[:, hh, :] = rw_g2[l][:, cs_]
        maps.append({"pin": pin, "mu": mu, "hp": hp, "w2": w2, "a2": a2, "g2": g2, "cs": cs, "cm": cm})
    return maps


def rwkv_unpack(res):
    y = np.zeros((2, 2, RN, 384), np.float32)
    bon = np.zeros((2, 2, RN, 384), np.float32)
    v = np.zeros((2, RN, 384), np.float32)
    g = np.zeros((2, RN, 384), np.float32)
    for core in range(NCORE):
        b, d, gg = core // 4, (core // 2) % 2, core % 2
        r = res[core]
        yy = r["y"].transpose(1, 2, 0, 3).reshape(3, RN, 64)
        bb = r["bon"].transpose(1, 2, 0)
        vv = r["vout"].transpose(1, 2, 0)
        gq = r["gout"].transpose(1, 2, 0)

        def unrev(a):
            if d == 0:
                return a
            return np.concatenate([a[:, :CTX][:, ::-1], a[:, CTX:][:, ::-1]], axis=1)
        yy, bb, vv, gq = unrev(yy), unrev(bb), unrev(vv), unrev(gq)
        for hh in range(3):
            hd_ = 3 * gg + hh
            y[d, b, :, hd_ * 64:(hd_ + 1) * 64] = yy[hh]
            bon[d, b, :, hd_ * 64:(hd_ + 1) * 64] = bb[hh]
            if d == 0:
                v[b, :, hd_ * 64:(hd_ + 1) * 64] = vv[hh]
                g[b, :, hd_ * 64:(hd_ + 1) * 64] = gq[hh]
    return y, bon, v, g


def build_natten_kernel(do_ctx=True):
    K = Ctx()
    nc, S = K.nc, K.S
    q_d = K.din("q", [64, 6, 2048])
    k_d = K.din("k", [64, 6, 2560])
    v_d = K.din("v", [64, 6, 40, 65])
    kc_d = K.din("kc", [64, 6, 256])
    vc_d = K.din("vc", [128, 6, 2, 65])
    qc_d = K.din("qc", [64, 6, 64])
    bt_d = K.din("bt", [64, 6, 15, 64])
    bsp_d = K.din("bsp", [64, 6, 8, 12, 64])
    ol_d = K.dout("ol", [64, 6, 32, 64])
    oc_d = K.dout("oc", [64, 6, 64])
    qT = K.sb([64, 6, 2048], BF16); S.dma("pool", qT.t[:], q_d, writes=[qT.r])
    kT = K.sb([64, 6, 2560], BF16); S.dma("pool", kT.t[:], k_d, writes=[kT.r])
    va = K.sb([64, 6, 40, 65], BF16); S.dma("pool", va.t[:], v_d, writes=[va.r])
    kcT = K.sb([64, 6, 256], BF16); S.dma("pool", kcT.t[:], kc_d, writes=[kcT.r])
    vca = K.sb([128, 6, 2, 65], BF16); S.dma("pool", vca.t[:], vc_d, writes=[vca.r])
    qcT = K.sb([64, 6, 64], BF16); S.dma("pool", qcT.t[:], qc_d, writes=[qcT.r])
    bt = K.sb([64, 6, 15, 64]); S.dma("sp", bt.t[:], bt_d, writes=[bt.r])
    ost = K.sb([64, 6, 32, 64])
    ocs = K.sb([64, 6, 64])
    pS = [K.ps([128, 512]) for _ in range(2)]
    pC = [K.ps([128, 512]) for _ in range(2)]
    pO = [K.ps([128, 512]) for _ in range(2)]
    sbt = [K.sb([64, 12, 64]) for _ in range(2)]
    Et = [K.sb([64, 12, 64], BF16) for _ in range(2)]
    pS2 = K.ps([128, 512])
    spb = [K.sb([64, 12, 64]) for _ in range(2)]
    Ect = [K.sb([128, 2, 64], BF16) for _ in range(2)]
    rd = [K.sb([64, 1]) for _ in range(2)]
    return K, dict(qT=qT, kT=kT, va=va, kcT=kcT, vca=vca, qcT=qcT, bt=bt, ost=ost, ocs=ocs, pS=pS, pC=pC, pO=pO, sbt=sbt, Et=Et,
                   Ect=Ect, rd=rd, ol_d=ol_d, oc_d=oc_d, pS2=pS2, spb=spb, bsp_d=bsp_d)


def emit_natten(K, T_, do_ctx):
    nc, S = K.nc, K.S
    qT, kT, va, kcT, vca, qcT, bt, ost, ocs = (T_[n] for n in ("qT", "kT", "va", "kcT", "vca", "qcT", "bt", "ost", "ocs"))
    it = 0
    for h in range(6):
        for il in range(32 + (1 if do_ctx else 0)):
            par = it % 2
            it += 1
            ps_, pc_, po_ = T_["pS"][par], T_["pC"][par], T_["pO"][par]
            sb_, E_, Ec_, rd_ = T_["sbt"][par], T_["Et"][par], T_["Ect"][par], T_["rd"][par]
            is_ctx = il == 32
            qv = (qcT, qcT.t[:, h, :]) if is_ctx else (qT, qT.t[:, h, il * 64:(il + 1) * 64])
            special = (not is_ctx) and (il < 4 or il >= 28)
            if not is_ctx:
                if il < 4:
                    lo, hi, sp = il, 12, il
                elif il >= 28:
                    lo, hi, sp = 28, il + 8, il - 24
                else:
                    lo, hi, sp = il, il + 8, None
                nr = hi - lo
                ps2 = T_["pS2"]
                for r in range(nr):
                    pt_ = ps_ if r < 8 else ps2
                    rr = r % 8
                    S.op("pe", lambda: nc.tensor.matmul(pt_.t[0:64, rr * 64:(rr + 1) * 64], lhsT=kT.t[:, h, (lo + r) * 64:(lo + r + 1) * 64],
                                                        rhs=qv[1], start=True, stop=True), reads=[kT.r, qv[0].r], writes=[pt_.r])
            for tci in range(2):
                S.op("pe", lambda: nc.tensor.matmul(pc_.t[:, tci * 64:(tci + 1) * 64], lhsT=kcT.t[:, h, tci * 128:(tci + 1) * 128],
                                                    rhs=qv[1], start=True, stop=True), reads=[kcT.r, qv[0].r], writes=[pc_.r])
            if not is_ctx:
                if special:
                    sb_b = T_["spb"][sp % 2]
                    S.dma("sp", sb_b.t[:, 0:nr, :], T_["bsp_d"][:, h, sp, 0:nr, :], writes=[sb_b.r])
                    bias_a = (sb_b, sb_b.t[:, 0:min(nr, 8), :])
                    bias_b = (sb_b, sb_b.t[:, 8:nr, :]) if nr > 8 else None
                else:
                    bias_a = (bt, bt.t[:, h, 3:11, :])
                    bias_b = None
                n1 = min(nr, 8)
                S.op("dve", lambda: nc.vector.scalar_tensor_tensor(out=sb_.t[:, 0:n1, :],
                                                                    in0=ps_.t[0:64, 0:n1 * 64].rearrange("p (r c) -> p r c", r=n1),
                                                                    scalar=0.125, in1=bias_a[1], op0=ALU.mult, op1=ALU.add),
                     reads=[ps_.r, bias_a[0].r], writes=[sb_.r])
                if bias_b is not None:
                    n2 = nr - 8
                    S.op("dve", lambda: nc.vector.scalar_tensor_tensor(out=sb_.t[:, 8:nr, :],
                                                                        in0=ps2.t[0:64, 0:n2 * 64].rearrange("p (r c) -> p r c", r=n2),
                                                                        scalar=0.125, in1=bias_b[1], op0=ALU.mult, op1=ALU.add),
                         reads=[ps2.r, bias_b[0].r], writes=[sb_.r])
                S.op("act", lambda: nc.scalar.activation(out=E_.t[:, 0:nr, :], in_=sb_.t[:, 0:nr, :], func=AF.Exp), reads=[sb_.r], writes=[E_.r])
            S.op("act", lambda: nc.scalar.activation(out=Ec_.t[:], in_=pc_.t[:, 0:128].rearrange("p (a c) -> p a c", a=2), func=AF.Exp,
                                                     scale=0.125), reads=[pc_.r], writes=[Ec_.r])
            first = True
            if not is_ctx:
                for r in range(nr):
                    S.op("pe", lambda: nc.tensor.matmul(po_.t[0:64, 0:65], lhsT=E_.t[:, r, :], rhs=va.t[:, h, lo + r, :],
                                                        start=first, stop=False), reads=[E_.r, va.r], writes=[po_.r])
                    first = False
            for tci in range(2):
                S.op("pe", lambda: nc.tensor.matmul(po_.t[0:64, 0:65], lhsT=Ec_.t[:, tci, :], rhs=vca.t[:, h, tci, :],
                                                    start=first, stop=(tci == 1)), reads=[Ec_.r, vca.r], writes=[po_.r])
                first = False
            S.op("dve", lambda: nc.vector.reciprocal(out=rd_.t[:], in_=po_.t[0:64, 64:65]), reads=[po_.r], writes=[rd_.r])
            dst = (ocs, ocs.t[:, h, :]) if is_ctx else (ost, ost.t[:, h, il, :])
            S.op("dve", lambda: nc.vector.tensor_scalar(out=dst[1], in0=po_.t[0:64, 0:64], scalar1=rd_.t[:, 0:1], scalar2=None, op0=ALU.mult),
                 reads=[po_.r, rd_.r], writes=[dst[0].r])
    r1, r2 = Res(), Res()
    S.dma("sp", T_["ol_d"], ost.t[:], reads=[ost.r], writes=[r1])
    if do_ctx:
        S.dma("sp", T_["oc_d"], ocs.t[:], reads=[ocs.r], writes=[r2])
    else:
        S.op("pool", lambda: nc.gpsimd.memset(ocs.t[:], 0.0), writes=[ocs.r])
        S.dma("sp", T_["oc_d"], ocs.t[:], reads=[ocs.r], writes=[r2])
    S.finish("sp", [r1, r2])


def natten_bias_table(rpb_l):
    c = np.arange(64)[None, :]
    ck = np.arange(64)[:, None]
    win0 = np.clip(c - 8, 0, 48)
    valid = (ck >= win0) & (ck < win0 + 16)
    off = np.clip(ck - c + 15, 0, 30)
    g = rpb_l[:, :, off]
    g = np.where(valid[None, None], g, np.float32(-30000.0)).astype(np.float32)
    return np.ascontiguousarray(g.transpose(2, 0, 1, 3))


def natten_maps(l, p_lat, p_ctx, na_rpb):
    o = 768 + 1536
    bt = natten_bias_table(na_rpb[l])
    maps = []
    for core in range(NCORE):
        b, qq = core // 4, core % 4
        na_l = p_lat[b, :, o:o + 1152].reshape(128, 64, 3, 6, 64)
        na_c = p_ctx[b, :, o:o + 1152].reshape(256, 3, 6, 64)
        r0 = 32 * qq
        q = na_l[r0:r0 + 32, :, 0].reshape(2048, 6, 64).transpose(2, 1, 0)
        kh = np.zeros((40, 64, 6, 64), np.float32)
        vh = np.zeros((40, 64, 6, 64), np.float32)
        lo, hi = max(r0 - 4, 0), min(r0 + 36, 128)
        kh[lo - (r0 - 4):hi - (r0 - 4)] = na_l[lo:hi, :, 1]
        vh[lo - (r0 - 4):hi - (r0 - 4)] = na_l[lo:hi, :, 2]
        k = kh.reshape(2560, 6, 64).transpose(2, 1, 0)
        v = np.ones((64, 6, 40, 65), np.float32)
        v[:, :, :, :64] = vh.transpose(1, 2, 0, 3)
        kc = na_c[:, 1].transpose(2, 1, 0)
        vc = np.ones((128, 6, 2, 65), np.float32)
        vc[:, :, :, :64] = na_c[:, 2].reshape(2, 128, 6, 64).transpose(1, 2, 0, 3)
        qc = na_c[qq * 64:(qq + 1) * 64, 0].transpose(2, 1, 0)
        bsp = np.full((64, 6, 8, 12, 64), -30000.0, np.float32)
        for sp in range(8):
            il = sp if sp < 4 else sp + 24
            lo = il if il < 4 else 28
            hi = 12 if il < 4 else il + 8
            i = r0 + il
            start = min(max(i - 4, 0), 120)
            for j in range(hi - lo):
                ar = r0 - 4 + lo + j
                if start <= ar < start + 8:
                    bsp[:, :, sp, j, :] = bt[:, :, ar - i + 7, :]
        maps.append({"q": np.ascontiguousarray(q), "k": np.ascontiguousarray(k), "v": v, "kc": np.ascontiguousarray(kc),
                     "vc": vc, "qc": np.ascontiguousarray(qc), "bt": bt, "bsp": bsp})
    return maps


def natten_unpack(res):
    o_lat = np.zeros((2, SEQ, 384), np.float32)
    o_ctx = np.zeros((2, CTX, 384), np.float32)
    for core in range(NCORE):
        b, qq = core // 4, core % 4
        ol = res[core]["ol"]
        o_lat[b, qq * 2048:(qq + 1) * 2048] = ol.transpose(2, 0, 1, 3).reshape(2048, 384)
        oc = res[core]["oc"]
        o_ctx[b, qq * 64:(qq + 1) * 64] = oc.reshape(64, 384)
    return o_lat, o_ctx


MAGIC = 12582912.0
TWO_PI = 6.283185307179586


HY_STAGE = [3]
HY_ONLY = [""]
HY_FLAGS = set()


def hyena_seq(K, tag, L, do_it, P_, params):
    nc, S = K.nc, K.S
    NJ = L // 128
    E2 = 2 * L
    ph_d = K.din("ph" + tag, [3, 128, 96, NJ, 2])
    zx_d = K.din("zx" + tag, [64, E2])
    wx_d = K.din("wx" + tag, [64, E2])
    out_d = K.dout("z" + tag, [128, 32, NJ, 2])
    kext = K.nc.dram_tensor("kext" + tag, [64, E2], BF16, kind="Internal")
    kext_ap = kext.ap()
    kres = Res()
    outs = []
    if not do_it:
        zt = K.sb([128, 32 * NJ * 2])
        S.op("pool", lambda: nc.gpsimd.memset(zt.t[:], 0.0), writes=[zt.r])
        r_ = Res()
        S.dma("sp", out_d.rearrange("p a b c -> p (a b c)"), zt.t[:], reads=[zt.r], writes=[r_])
        return [r_]
    w1, w2, w3, cw, fq, fb1, fb2, hb, ident = (params[n] for n in ("w1", "w2", "w3", "cw", "fq", "fb1", "fb2", "hb", "ident"))
    CH = 512 if E2 >= 512 else E2
    nchunk = E2 // CH
    zx = [K.sb([64, CH]) for _ in range(2)]
    wx = [K.sb([64, CH]) for _ in range(2)]
    ta = K.sb([64, CH]); tb = K.sb([64, CH]); h1 = K.sb([64, CH]); h2 = K.sb([64, CH])
    fk = [K.sb([64, CH], BF16) for _ in range(2)]
    ff = K.sb([64, CH])

    def sin_layer(ps, fbias, dst):
        S.op("dve", lambda: nc.vector.tensor_scalar(out=ta.t[:], in0=ps[1], scalar1=fq.t[:, 0:1], scalar2=fbias.t[:, 0:1],
                                                    op0=ALU.mult, op1=ALU.add), reads=[ps[0].r, fq.r, fbias.r], writes=[ta.r])
        S.op("dve", lambda: nc.vector.tensor_scalar(out=tb.t[:], in0=ta.t[:], scalar1=1.0 / TWO_PI, scalar2=MAGIC,
                                                    op0=ALU.mult, op1=ALU.add), reads=[ta.r], writes=[tb.r])
        S.op("dve", lambda: nc.vector.tensor_scalar(out=tb.t[:], in0=tb.t[:], scalar1=MAGIC, scalar2=-TWO_PI,
                                                    op0=ALU.subtract, op1=ALU.mult), reads=[tb.r], writes=[tb.r])
        S.op("dve", lambda: nc.vector.tensor_tensor(out=ta.t[:], in0=ta.t[:], in1=tb.t[:], op=ALU.add), reads=[ta.r, tb.r], writes=[ta.r])
        S.op("dve", lambda: nc.vector.tensor_scalar(out=ta.t[:], in0=ta.t[:], scalar1=3.141592, scalar2=-3.141592,
                                                    op0=ALU.min, op1=ALU.max), reads=[ta.r], writes=[ta.r])
        S.op("act", lambda: nc.scalar.activation(out=dst.t[:], in_=ta.t[:], func=AF.Sin), reads=[ta.r], writes=[dst.r])

    for ci in range(nchunk if "nofilt" not in HY_FLAGS else 0):
        e0 = ci * CH
        zt_, wt_ = zx[ci % 2], wx[ci % 2]
        S.dma("sp", zt_.t[:], zx_d[:, e0:e0 + CH], writes=[zt_.r])
        S.dma("act", wt_.t[:], wx_d[:, e0:e0 + CH], writes=[wt_.r])
        p1 = P_[ci % 2]
        S.op("pe", lambda: nc.tensor.matmul(p1.t[0:64, 0:CH], lhsT=w1.t[:], rhs=zt_.t[:], start=True, stop=True),
             reads=[w1.r, zt_.r], writes=[p1.r])
        sin_layer((p1, p1.t[0:64, 0:CH]), fb1, h1)
        p2 = P_[2 + ci % 2]
        S.op("pe", lambda: nc.tensor.matmul(p2.t[0:64, 0:CH], lhsT=w2.t[:], rhs=h1.t[:], start=True, stop=True),
             reads=[w2.r, h1.r], writes=[p2.r])
        sin_layer((p2, p2.t[0:64, 0:CH]), fb2, h2)
        segs = []
        if e0 < L:
            segs.append((0, min(CH, L - e0), 0))
        if e0 <= L < e0 + CH:
            segs.append((L - e0, L - e0 + 1, 2))
        if e0 + CH > L + 1:
            segs.append((max(0, L + 1 - e0), CH, 1))
        p3 = P_[4 + ci % 2]
        for (a, b_, wi) in segs:
            S.op("pe", lambda: nc.tensor.matmul(p3.t[0:64, a:b_], lhsT=w3.t[:, wi, :], rhs=h2.t[:, a:b_], start=True, stop=True),
                 reads=[w3.r, h2.r], writes=[p3.r])
        S.op("dve", lambda: nc.vector.tensor_tensor(out=ff.t[:], in0=p3.t[0:64, 0:CH], in1=wt_.t[:], op=ALU.mult),
             reads=[p3.r, wt_.r], writes=[ff.r])
        if e0 <= L < e0 + CH:
            S.op("dve", lambda: nc.vector.tensor_tensor(out=ff.t[:, L - e0:L - e0 + 1], in0=ff.t[:, L - e0:L - e0 + 1], in1=hb.t[:, 0:1],
                                                        op=ALU.add), reads=[ff.r, hb.r], writes=[ff.r])
        fkt = fk[ci % 2]
        S.op("act", lambda: nc.scalar.copy(out=fkt.t[:], in_=ff.t[:]), reads=[ff.r], writes=[fkt.r])
        S.dma("sp", kext_ap[:, e0:e0 + CH], fkt.t[:], reads=[fkt.r], writes=[kres])
    if "kdbg" in HY_FLAGS:
        kd_d = K.dout("kd" + tag, [64, E2], BF16)
        r_ = Res(); outs.append(r_)
        S.dma("sp", kd_d, kext_ap, reads=[kres], writes=[r_])
    if HY_STAGE[0] < 2:
        zt = K.sb([128, 32 * NJ * 2])
        S.op("pool", lambda: nc.gpsimd.memset(zt.t[:], 0.0), writes=[zt.r])
        r_ = Res()
        S.dma("sp", out_d.rearrange("p a b c -> p (a b c)"), zt.t[:], reads=[zt.r], writes=[r_])
        return [r_, kres]
    G = K.sb([128, 64, NJ, 2])
    Ub = K.sb([128, 32, NJ, 2], BF16)
    Z1b = K.sb([128, 32, NJ, 2], BF16)
    xs = [K.sb([128, 32, NJ * 2]) for _ in range(3)]
    Z2 = TB(xs[1].t[:].rearrange("p r (j b) -> p r j b", b=2))
    Z2.r = xs[1].r
    cwb = params["cwb"]
    for g in range(3):
        for s_ in range(3):
            S.dma("sp" if s_ != 1 else "act", xs[s_].t[:], ph_d[s_, :, g * 32:(g + 1) * 32].rearrange("p r j b -> p r (j b)"),
                  writes=[xs[s_].r])

        def wb(k_):
            return cwb.t[:, g * 32:(g + 1) * 32, k_:k_ + 1].to_broadcast([128, 32, NJ * 2])
        for s_ in range(3):
            S.op("dve", lambda: nc.vector.tensor_tensor(out=xs[s_].t[:], in0=xs[s_].t[:], in1=wb(s_), op=ALU.mult),
                 reads=[xs[s_].r, cwb.r], writes=[xs[s_].r])
        S.op("dve", lambda: nc.vector.tensor_tensor(out=xs[0].t[:], in0=xs[0].t[:], in1=xs[1].t[:], op=ALU.add),
             reads=[xs[0].r, xs[1].r], writes=[xs[0].r])
        S.op("dve", lambda: nc.vector.tensor_tensor(out=xs[0].t[:], in0=xs[0].t[:], in1=xs[2].t[:], op=ALU.add),
             reads=[xs[0].r, xs[2].r], writes=[xs[0].r])
        dst = (Ub, Ub.t[:].rearrange("p r j b -> p r (j b)")) if g == 0 else \
              (G, G.t[:, (g - 1) * 32:g * 32].rearrange("p r j b -> p r (j b)"))
        S.op("dve", lambda: nc.vector.tensor_tensor(out=dst[1], in0=xs[0].t[:], in1=wb(3), op=ALU.add),
             reads=[xs[0].r, cwb.r], writes=[dst[0].r])
    if HY_STAGE[0] < 3:
        r_ = Res()
        S.dma("sp", out_d, G.t[:, 0:32], reads=[G.r, Ub.r], writes=[r_])
        return [r_, kres]
    TW = E2 - 128
    Tz = [K.sb([128, TW], BF16) for _ in range(2)]
    it = 0
    for o in range(2):
        src_t = Ub if o == 0 else Z1b
        for c in range(32):
            tz = Tz[it % 2]
            py = P_[it % 4]
            it += 1
            row = o * 32 + c
            srcap = bass.AP(kext_ap.tensor, row * E2 + 1, [[1, 128], [1, TW]])
            S.dma("sp" if it % 2 == 0 else "act", tz.t[:], srcap, reads=[kres], writes=[tz.r])
            if "tzdbg" in HY_FLAGS and o == 0 and c in (0, 1):
                td_d = K.dout("tzd%d" % c + tag, [128, TW], BF16)
                r_ = Res(); outs.append(r_)
                S.dma("sp", td_d, tz.t[:], reads=[tz.r], writes=[r_])
            deltas = [0] + [d for k_ in range(1, NJ) for d in (k_, -k_)]
            for n_, dl in enumerate(deltas):
                Jlo, Jhi = max(0, -dl), min(NJ - 1, NJ - 1 - dl)
                nJ = Jhi - Jlo + 1
                m0 = L + 128 * (dl if o == 0 else -dl) - 128
                S.op("pe", lambda: nc.tensor.matmul(py.t[:, (Jlo + dl) * 2:(Jlo + dl + nJ) * 2], lhsT=tz.t[:, m0:m0 + 128],
                                                    rhs=src_t.t[:, c, Jlo:Jlo + nJ, :], start=(n_ == 0), stop=(n_ == len(deltas) - 1),
                                                    skip_group_check=True),
                     reads=[tz.r, src_t.r], writes=[py.r])
            if "tzdbg" in HY_FLAGS and o == 0 and c in (0, 1):
                pyd_d = K.dout("pyd%d" % c + tag, [128, NJ * 2])
                pys = K.sb([128, NJ * 2])
                S.op("act", lambda: nc.scalar.copy(out=pys.t[:], in_=py.t[:, 0:NJ * 2]), reads=[py.r], writes=[pys.r])
                r_ = Res(); outs.append(r_)
                S.dma("sp", pyd_d, pys.t[:], reads=[pys.r], writes=[r_])
            if o == 0:
                S.op("dve", lambda: nc.vector.tensor_tensor(out=Z1b.t[:, c, :, :], in0=py.t[:, 0:NJ * 2].rearrange("p (j b) -> p j b", b=2),
                                                            in1=G.t[:, c, :, :], op=ALU.mult), reads=[py.r, G.r], writes=[Z1b.r])
            else:
                S.op("dve", lambda: nc.vector.tensor_tensor(out=Z2.t[:, c, :, :], in0=py.t[:, 0:NJ * 2].rearrange("p (j b) -> p j b", b=2),
                                                            in1=G.t[:, 32 + c, :, :], op=ALU.mult), reads=[py.r, G.r], writes=[Z2.r])
    r_ = Res()
    S.dma("sp", out_d, Z2.t, reads=[Z2.r], writes=[r_])
    return [r_] + outs


def build_hyena_kernel(do_ctx=True):
    K = Ctx()
    nc, S = K.nc, K.S
    prm_d = K.din("prm", [128, 8])
    w1_d = K.din("w1", [64, 64])
    w2_d = K.din("w2", [64, 64])
    w3_d = K.din("w3", [64, 3, 64])
    id_d = K.din("ident", [128, 128])
    prm = K.sb([128, 8]); S.dma("sp", prm.t[:], prm_d, writes=[prm.r])
    w1 = K.sb([64, 64]); S.dma("sp", w1.t[:], w1_d, writes=[w1.r])
    w2 = K.sb([64, 64]); S.dma("sp", w2.t[:], w2_d, writes=[w2.r])
    w3 = K.sb([64, 3, 64]); S.dma("sp", w3.t[:], w3_d, writes=[w3.r])
    ident = K.sb([128, 128]); S.dma("sp", ident.t[:], id_d, writes=[ident.r])
    cwb_d = K.din("cwb", [128, 96, 4])
    cwb = K.sb([128, 96, 4]); S.dma("sp", cwb.t[:], cwb_d, writes=[cwb.r])
    fb = K.sb([64, 2])
    S.op("dve", lambda: nc.vector.tensor_scalar(out=fb.t[:], in0=prm.t[0:64, 5:7], scalar1=prm.t[0:64, 4:5], scalar2=None, op0=ALU.mult),
         reads=[prm.r], writes=[fb.r])

    class V:
        def __init__(s, tb, ap):
            s.t, s.r = ap, tb.r
    params = dict(w1=w1, w2=w2, w3=w3, cw=V(prm, prm.t[0:96, 0:4]), fq=V(prm, prm.t[0:64, 4:5]), fb1=V(fb, fb.t[:, 0:1]),
                  fb2=V(fb, fb.t[:, 1:2]), hb=V(prm, prm.t[0:64, 7:8]), ident=ident, cwb=cwb)
    P_ = [K.ps([128, 512]) for _ in range(8)]
    outs = hyena_seq(K, "l", SEQ, HY_ONLY[0] != "c", P_, params)
    outs += hyena_seq(K, "c", CTX, do_ctx and HY_ONLY[0] != "l", P_, params)
    S.finish("sp", outs)
    return K


def hyena_consts(L):
    f32 = np.float32
    t = np.linspace(0.0, 1.0, L, dtype=f32)
    ang = (f32(2.0 * np.pi) * np.arange(L, dtype=f32) / f32(L)).astype(f32)
    fr = np.linspace(1e-4, 15.0, 16, dtype=f32)
    z = np.concatenate([t[:, None], np.cos(fr[None, :] * ang[:, None]), -np.sin(fr[None, :] * ang[:, None])], axis=-1).astype(f32)
    deltas = np.abs(np.linspace(np.log(1e-2) / 1.5, np.log(1e-2) / 0.3, 256, dtype=f32))
    win = np.exp(-t[:, None] * deltas[None, :]).astype(f32)
    e = np.arange(2 * L)
    pos0 = np.abs(e - L)
    pos0[0] = 0
    zx = np.zeros((64, 2 * L), np.float32)
    zx[:33] = z[pos0].T
    w0_ = win[pos0]; w0_[0] = 0.0
    return zx, w0_


def hyena_maps(l, p_lat, p_ctx, hy_conv_w, hy_conv_b, hy_f_w1, hy_f_b1, hy_f_w2, hy_f_b2, hy_f_w3, hy_freq, hy_bias):
    zxl, wl = hyena_consts(SEQ)
    zxc, wc = hyena_consts(CTX)
    maps = []
    ident = np.eye(128, dtype=np.float32)
    for core in range(NCORE):
        ch = np.arange(32 * core, 32 * core + 32)
        rows = np.concatenate([ch, 256 + ch, 512 + ch])
        def blocked(p_, L):
            NJ = L // 128
            x = np.zeros((96, 2, L + 2), np.float32)
            x[:, :, 1:1 + L] = p_[:, :, rows].transpose(2, 0, 1)
            out = np.zeros((3, 128, 96, NJ, 2), np.float32)
            p = np.arange(128)[:, None]
            J = np.arange(NJ)[None, :]
            t_n = 128 * J + p
            t_r = 128 * J + 127 - p
            for s_ in range(3):
                out[s_, :, 32:64] = x[32:64][:, :, 1 + t_n + (s_ - 1)].transpose(2, 0, 3, 1)
                out[s_, :, 0:32] = x[0:32][:, :, 1 + t_r + (s_ - 1)].transpose(2, 0, 3, 1)
                out[s_, :, 64:96] = x[64:96][:, :, 1 + t_r + (s_ - 1)].transpose(2, 0, 3, 1)
            return out
        phl = blocked(p_lat, SEQ)
        phc = blocked(p_ctx, CTX)
        prm = np.zeros((128, 8), np.float32)
        prm[:96, 0:3] = hy_conv_w[l][:, rows].T
        prm[:96, 3] = hy_conv_b[l][rows]
        prm[:64, 4] = hy_freq[l]
        prm[:64, 5] = hy_f_b1[l]
        prm[:64, 6] = hy_f_b2[l]
        prm[:64, 7] = hy_bias[l][:, ch].reshape(64)
        cwb = np.ascontiguousarray(np.broadcast_to(prm[None, :96, 0:4], (128, 96, 4)))
        w3r = hy_f_w3[l].reshape(64, 2, 2, 256)[:, :, :, ch]
        WA = np.concatenate([w3r[:, 0, 1], w3r[:, 1, 0]], axis=-1)
        WB = np.concatenate([w3r[:, 0, 0], w3r[:, 1, 1]], axis=-1)
        WC = np.concatenate([w3r[:, 0, 0], w3r[:, 1, 0]], axis=-1)
        w3 = np.ascontiguousarray(np.stack([WA, WB, WC], axis=1))
        wxl = np.ascontiguousarray(np.tile(wl[:, ch].T, (2, 1)))
        wxc = np.ascontiguousarray(np.tile(wc[:, ch].T, (2, 1)))
        maps.append({"prm": prm, "w1": np.concatenate([hy_f_w1[l], np.zeros((31, 64), np.float32)], axis=0), "w2": hy_f_w2[l], "w3": w3, "ident": ident, "cwb": cwb,
                     "phl": phl, "zxl": zxl, "wxl": wxl, "phc": phc, "zxc": zxc, "wxc": wxc})
    return maps


def hyena_unpack(res):
    o_lat = np.zeros((2, SEQ, 256), np.float32)
    o_ctx = np.zeros((2, CTX, 256), np.float32)
    for core in range(NCORE):
        zl = res[core]["zl"][::-1]
        o_lat[:, :, 32 * core:32 * core + 32] = zl.transpose(3, 2, 0, 1).reshape(2, SEQ, 32)
        zc = res[core]["zc"][::-1]
        o_ctx[:, :, 32 * core:32 * core + 32] = zc.transpose(3, 2, 0, 1).reshape(2, CTX, 32)
    return o_lat, o_ctx


def gn_block():
    blk = np.zeros((128, 128), np.float32)
    blk[:64, :64] = 1.0 / 64
    blk[64:, 64:] = 1.0 / 64
    return blk


def run_front(l, h_lat, h_ctx, m_all, inp):
    K = get_kernel("A", lambda: build_token_kernel(do_wout=False, do_win=True))
    hs = tok_layout(h_lat, h_ctx)
    maps = common_maps(l, m_all, inp["norm_g"], 0, 5)
    for c in range(NCORE):
        maps[c].update({"h": hs[c], "wgu": inp["ffn1_wgu"][l], "wdn": inp["ffn1_wdn"][l], "win": inp["w_in"][l]})
    res = run(K, maps)
    hl, hc = tok_unlayout([r["hout"] for r in res])
    pl, pc = tok_unlayout([r["p"] for r in res])
    return hl, hc, pl, pc


def run_mixers(l, pl, pc, inp):
    Kh = get_kernel("H", lambda: build_hyena_kernel(True))
    names = ['hy_conv_w', 'hy_conv_b', 'hy_f_w1', 'hy_f_b1', 'hy_f_w2', 'hy_f_b2', 'hy_f_w3', 'hy_freq', 'hy_bias']
    hy_l, hy_c = hyena_unpack(run(Kh, hyena_maps(l, pl, pc, *[inp[n] for n in names])))
    Kr = get_kernel("R", lambda: build_rwkv_kernel())
    y, bon, v, g = rwkv_unpack(run(Kr, rwkv_maps(l, pl, pc, inp['rw_mu'], inp['rw_w0'], inp['rw_w2'], inp['rw_a0'], inp['rw_a2'],
                                                 inp['rw_g2'], inp['rw_k_k'], inp['rw_k_a'], inp['rw_r_k'])))

    def mkn():
        K, T_ = build_natten_kernel(True)
        emit_natten(K, T_, True)
        return K
    Kn = get_kernel("N", mkn)
    na_l, na_c = natten_unpack(run(Kn, natten_maps(l, pl, pc, inp['na_rpb'])))
    return dict(hy_l=hy_l, hy_c=hy_c, y=y, bon=bon, v=v, g=g, na_l=na_l, na_c=na_c)


def run_back(l, h_lat, h_ctx, mx, m_all, inp):
    K = get_kernel("C", lambda: build_token_kernel(do_wout=True, do_win=False))
    hs = tok_layout(h_lat, h_ctx)
    mixh = tok_layout(mx["hy_l"], mx["hy_c"])
    mixn = tok_layout(mx["na_l"], mx["na_c"])
    rws = []
    for arr in (mx["y"][0], mx["y"][1], mx["bon"][0], mx["bon"][1], mx["v"], mx["g"]):
        rws.append(tok_layout(arr[:, CTX:], arr[:, :CTX]))
    maps = common_maps(l, m_all, inp["norm_g"], 5, 4)
    lnp = np.stack([inp["rw_ln_w"][l].reshape(3, 128).T, inp["rw_ln_b"][l].reshape(3, 128).T], axis=-1).astype(np.float32)
    blk = gn_block()
    for c in range(NCORE):
        maps[c].update({"h": hs[c], "wgu": inp["ffn2_wgu"][l], "wdn": inp["ffn2_wdn"][l], "wout": inp["w_out"][l],
                        "mixh": mixh[c], "mixn": mixn[c], "rwin": np.stack([r[c] for r in rws]), "lnp": np.ascontiguousarray(lnp),
                        "blk": blk})
    res = run(K, maps)
    return tok_unlayout([r["hout"] for r in res])


def run_mod(inp):
    K = get_kernel("M", build_mod_kernel)
    return mod_unpack(run(K, mod_maps(inp["c"], inp["c_ctx"], inp["mod_w"], inp["mod_b"])))


def kernel(**inputs):
    inp = {k: np.asarray(v, dtype=np.float32) for k, v in inputs.items()}
    m_all = run_mod(inp)
    h_lat, h_ctx = inp["x"], inp["ctx"]
    for l in range(4):
        h_lat, h_ctx, pl, pc = run_front(l, h_lat, h_ctx, m_all, inp)
        mx = run_mixers(l, pl, pc, inp)
        h_lat, h_ctx = run_back(l, h_lat, h_ctx, mx, m_all, inp)
    return np.ascontiguousarray(h_lat, dtype=np.float32)
```

```python
import numpy as np
import concourse.bass as bass
import concourse.mybir as mybir
from concourse.bass_utils import run_bass_kernel_spmd

F32 = mybir.dt.float32
BF16 = mybir.dt.bfloat16
AF = mybir.ActivationFunctionType
ALU = mybir.AluOpType

D = 1024
DFF = 2816
NF = DFF // 128
SEQ = 8192
CTX = 256
NCORE = 8
TL = 512
TC = 16
T = TL + TC
HALF = T // 2
NPASS = 4
TOK = NPASS * T
IN_W = 3456
NIN = IN_W // 128
EPS = 1e-6


class Res:
    __slots__ = ("w", "r")

    def __init__(self):
        self.w = None
        self.r = []


class Sched:
    def __init__(self, nc, n_dma_sems=6):
        self.nc = nc
        self.eng = {"pe": nc.tensor, "dve": nc.vector, "act": nc.scalar, "pool": nc.gpsimd, "sp": nc.sync}
        self.sems = {}
        self.cnt = {}
        for e in ["pe", "dve", "act", "pool"]:
            self.sems[e] = nc.alloc_semaphore(name="s_" + e)
            self.cnt[e] = 0
        self.dma_sems = {}
        for q in ["sp", "act", "pool"]:
            lst = []
            for i in range(n_dma_sems):
                k = "d_%s_%d" % (q, i)
                self.sems[k] = nc.alloc_semaphore(name=k)
                self.cnt[k] = 0
                lst.append(k)
            self.dma_sems[q] = [lst, 0]
        self.seen = {e: {} for e in self.eng}
        self.ninst = 0

    def _wait(self, e, deps):
        mx = {}
        for d in deps:
            if d is None:
                continue
            k, v = d
            if mx.get(k, 0) < v:
                mx[k] = v
        for k, v in mx.items():
            if k == e and e == "pe":
                continue
            if self.seen[e].get(k, 0) >= v:
                continue
            self.eng[e].wait_ge(self.sems[k], v)
            self.seen[e][k] = v

    def _deps(self, e, reads, writes):
        deps = []
        for r in reads:
            deps.append(r.w)
        for w in writes:
            deps.append(w.w)
            for rr in w.r:
                deps.append(rr)
        return deps

    def _mark(self, ev, reads, writes):
        for r in reads:
            r.r.append(ev)
            if len(r.r) > 64:
                mx = {}
                for k, v in r.r:
                    if mx.get(k, 0) < v:
                        mx[k] = v
                r.r = list(mx.items())
        for w in writes:
            w.w = ev
            w.r = []

    def op(self, e, fn, reads=(), writes=()):
        self._wait(e, self._deps(e, reads, writes))
        inst = fn()
        self.cnt[e] += 1
        inst.then_inc(self.sems[e], 1)
        ev = (e, self.cnt[e])
        self._mark(ev, reads, writes)
        self.ninst += 1
        return ev

    def dma(self, q, out, in_, reads=(), writes=(), **kw):
        lst, idx = self.dma_sems[q]
        k = lst[idx % len(lst)]
        self.dma_sems[q][1] = idx + 1
        deps = self._deps(q, reads, writes)
        if self.cnt[k] > 0:
            deps.append((k, self.cnt[k]))
        self._wait(q, deps)
        inst = self.eng[q].dma_start(out=out, in_=in_, **kw)
        self.cnt[k] += 16
        inst.then_inc(self.sems[k], 16)
        ev = (k, self.cnt[k])
        self._mark(ev, reads, writes)
        self.ninst += 1
        return ev

    def finish(self, e, resources):
        deps = []
        for r in resources:
            deps.append(r.w)
            deps.extend(r.r)
        self._wait(e, deps)


class TB:
    def __init__(self, t):
        self.t = t
        self.r = Res()


class Ctx:
    def __init__(self):
        self.nc = bass.Bass("TRN2", target_bir_lowering=False)
        self.S = Sched(self.nc)
        self.n = 0

    def sb(self, shape, dt=F32):
        self.n += 1
        return TB(self.nc.alloc_sbuf_tensor("sb%d" % self.n, list(shape), dt))

    def ps(self, shape, dt=F32):
        self.n += 1
        return TB(self.nc.alloc_psum_tensor("ps%d" % self.n, list(shape), dt))

    def din(self, name, shape, dt=F32):
        return self.nc.dram_tensor(name, list(shape), dt, kind="ExternalInput").ap()

    def dout(self, name, shape, dt=F32):
        return self.nc.dram_tensor(name, list(shape), dt, kind="ExternalOutput").ap()


def cols(lo, hi):
    return [(lo, hi)]


def emit_modulation(K, cc_d, modw_d, modb_d, j0, nj, pbank):
    nc, S = K.nc, K.S
    cc = K.sb([128, 8, 2])
    ccb = K.sb([128, 8, 2], BF16)
    S.dma("sp", cc.t[:], cc_d, writes=[cc.r])
    S.op("act", lambda: nc.scalar.activation(out=ccb.t[:], in_=cc.t[:], func=AF.Silu), reads=[cc.r], writes=[ccb.r])
    mb = K.sb([128, nj * 8])
    S.dma("sp", mb.t[:], modb_d[:, j0 * 8:(j0 + nj) * 8], writes=[mb.r])
    mps = TB(pbank.t[:, 0, 0:nj * 16].rearrange("p (a b) -> p a b", b=2))
    mps.r = pbank.r
    msb = K.sb([128, nj * 8, 2])
    wbuf = [K.sb([128, 8, 1024], BF16) for _ in range(2)]
    for j in range(nj):
        wb = wbuf[j % 2]
        S.dma("pool", wb.t[:], modw_d[:, (j0 + j) * 1024:(j0 + j + 1) * 1024].rearrange("(k p) n -> p k n", p=128),
              writes=[wb.r])
        for fc in range(8):
            for k in range(8):
                S.op("pe", lambda: nc.tensor.matmul(mps.t[:, j * 8 + fc, :], lhsT=wb.t[:, k, fc * 128:(fc + 1) * 128],
                                                    rhs=ccb.t[:, k, :], start=(k == 0), stop=(k == 7)),
                     reads=[wb.r, ccb.r], writes=[mps.r])
    for s in range(2):
        S.op("dve", lambda: nc.vector.tensor_tensor(out=msb.t[:, :, s], in0=mps.t[:, :, s], in1=mb.t[:], op=ALU.add),
             reads=[mps.r, mb.r], writes=[msb.r])
    return msb


def emit_rstd(K, src, sq, ssp, rstd, tmp, from_list=None):
    nc, S = K.nc, K.S
    for k in range(8):
        S.op("act", lambda: nc.scalar.activation(out=sq.t[:, k, :], in_=src.t[:, k, :], func=AF.Square),
             reads=[src.r], writes=[sq.r])
    for h in range(2):
        for k in range(8):
            S.op("pe", lambda: nc.tensor.matmul(ssp.t[:, h, 0:HALF], lhsT=K.ones.t[:], rhs=sq.t[:, k, h * HALF:(h + 1) * HALF],
                                                start=(k == 0), stop=(k == 7)),
                 reads=[K.ones.r, sq.r], writes=[ssp.r])
    for h in range(2):
        S.op("act", lambda: nc.scalar.activation(out=tmp.t[:, h * HALF:(h + 1) * HALF], in_=ssp.t[:, h, 0:HALF], func=AF.Sqrt,
                                                 scale=1.0 / D, bias=K.epsb.t[:, 0:1]),
             reads=[ssp.r, K.epsb.r], writes=[tmp.r])
    S.op("dve", lambda: nc.vector.reciprocal(out=rstd.t[:], in_=tmp.t[:]), reads=[tmp.r], writes=[rstd.r])


def emit_modnorm(K, src, rstd, tmp, s1, s2, uT):
    nc, S = K.nc, K.S
    for k in range(8):
        S.op("dve", lambda: nc.vector.tensor_tensor(out=tmp.t[:], in0=src.t[:, k, :], in1=rstd.t[:], op=ALU.mult),
             reads=[src.r, rstd.r], writes=[tmp.r])
        for (lo, hi, s) in ((0, TL, 0), (TL, T, 1)):
            S.op("act", lambda: nc.scalar.activation(out=uT.t[:, k, lo:hi], in_=tmp.t[:, lo:hi], func=AF.Identity,
                                                     scale=s1.t[:, k, s:s + 1], bias=s2.t[:, k, s:s + 1]),
                 reads=[tmp.r, s1.r, s2.r], writes=[uT.r])


def build_token_kernel(do_wout, do_win):
    K = Ctx()
    nc, S = K.nc, K.S
    h_d = K.din("h", [D, TOK])
    j0, nj = (0, 5) if do_win else (5, 4)
    m_d = K.din("m", [128, nj * 8, 2])
    ng_d = K.din("ng", [128, 6, 8])
    wgu_d = K.din("wgu", [D, 2 * DFF])
    wdn_d = K.din("wdn", [DFF, D])
    hout_d = K.dout("hout", [D, TOK])
    if do_wout:
        mixh_d = K.din("mixh", [256, TOK])
        mixn_d = K.din("mixn", [384, TOK])
        rw_d = K.din("rwin", [6, 384, TOK])
        lnp_d = K.din("lnp", [128, 3, 2])
        blk_d = K.din("blk", [128, 128])
        wout_d = K.din("wout", [D, D])
    if do_win:
        win_d = K.din("win", [D, IN_W])
        p_d = K.dout("p", [IN_W, TOK])

    K.ones = K.sb([128, 128], BF16)
    S.op("pool", lambda: nc.gpsimd.memset(K.ones.t[:], 1.0), writes=[K.ones.r])
    K.epsb = K.sb([128, 1])
    S.op("pool", lambda: nc.gpsimd.memset(K.epsb.t[:], EPS), writes=[K.epsb.r])

    P = [K.ps([128, 2, 512]) for _ in range(4)]
    m = K.sb([128, nj * 8, 2])
    S.dma("sp", m.t[:], m_d, writes=[m.r])
    ng = K.sb([128, 6, 8])
    S.dma("sp", ng.t[:], ng_d, writes=[ng.r])

    def mslice(j):
        return m.t[:, (j - j0) * 8:(j - j0 + 1) * 8, :]

    def mk_scale(gidx, jscale, mul=None):
        s = K.sb([128, 8, 2])
        for c in range(2):
            if mul is None:
                S.op("dve", lambda: nc.vector.scalar_tensor_tensor(out=s.t[:, :, c], in0=mslice(jscale)[:, :, c], scalar=1.0,
                                                                    in1=ng.t[:, gidx, :], op0=ALU.add, op1=ALU.mult),
                     reads=[m.r, ng.r], writes=[s.r])
            else:
                S.op("dve", lambda: nc.vector.scalar_tensor_tensor(out=s.t[:, :, c], in0=mslice(jscale)[:, :, c], scalar=float(mul),
                                                                    in1=ng.t[:, gidx, :], op0=ALU.mult, op1=ALU.mult),
                     reads=[m.r, ng.r], writes=[s.r])
        return s

    def mk_copy(j):
        s = K.sb([128, 8, 2])
        S.op("dve", lambda: nc.vector.tensor_copy(out=s.t[:], in_=mslice(j)), reads=[m.r], writes=[s.r])
        return s

    if do_win:
        f_s1 = mk_scale(0, 1)
        f_s2 = mk_copy(0)
        f_s3 = mk_scale(1, 2, mul=0.5)
        x_s1 = mk_scale(2, 4)
        x_s2 = mk_copy(3)
    else:
        o_s3 = mk_scale(3, 5, mul=1.0)
        f_s1 = mk_scale(4, 7)
        f_s2 = mk_copy(6)
        f_s3 = mk_scale(5, 8, mul=0.5)

    hT = K.sb([128, 8, T])
    sq = K.sb([128, 8, T], BF16)
    uT = K.sb([128, 8, T], BF16)
    yT = K.sb([128, 8, T])
    hid = K.sb([128, NF, T], BF16)
    tmpA = K.sb([128, T])
    tmpB = K.sb([128, T])
    rstd = K.sb([128, T])
    wg = [K.sb([128, 8, 512], BF16) for _ in range(2)]
    wu = [K.sb([128, 8, 512], BF16) for _ in range(2)]
    wdn = [K.sb([128, NF, 512], BF16) for _ in range(2)]
    if do_wout:
        mixT = K.sb([128, 8, T], BF16)
        lnp = K.sb([128, 3, 2]); S.dma("sp", lnp.t[:], lnp_d, writes=[lnp.r])
        blk = K.sb([128, 128]); S.dma("sp", blk.t[:], blk_d, writes=[blk.r])
        gnb = K.sb([128, 1]); S.op("pool", lambda: nc.gpsimd.memset(gnb.t[:], 64e-5), writes=[gnb.r])
        rwt = [K.sb([128, T]) for _ in range(6)]

    def residual_update(s3):
        for d in range(8):
            S.op("dve", lambda: nc.vector.tensor_tensor(out=tmpA.t[:], in0=yT.t[:, d, :], in1=rstd.t[:], op=ALU.mult),
                 reads=[yT.r, rstd.r], writes=[tmpA.r])
            for (lo, hi, s) in ((0, TL, 0), (TL, T, 1)):
                S.op("dve", lambda: nc.vector.scalar_tensor_tensor(out=hT.t[:, d, lo:hi], in0=tmpA.t[:, lo:hi],
                                                                    scalar=s3.t[:, d, s:s + 1], in1=hT.t[:, d, lo:hi],
                                                                    op0=ALU.mult, op1=ALU.add),
                     reads=[tmpA.r, s3.r, hT.r], writes=[hT.r])

    for ps_ in range(NPASS):
        c0 = ps_ * T
        S.dma("sp", hT.t[:], h_d[:, c0:c0 + T].rearrange("(k p) t -> p k t", p=128), writes=[hT.r])
        if do_wout:
            S.dma("pool", mixT.t[:, 0:2, :], mixh_d[:, c0:c0 + T].rearrange("(k p) t -> p k t", p=128), writes=[mixT.r])
            S.dma("pool", mixT.t[:, 5:8, :], mixn_d[:, c0:c0 + T].rearrange("(k p) t -> p k t", p=128), writes=[mixT.r])
            for ck in range(3):
                for i_ in range(6):
                    S.dma("sp" if i_ % 2 == 0 else "act", rwt[i_].t[:], rw_d[i_, ck * 128:(ck + 1) * 128, c0:c0 + T], writes=[rwt[i_].r])
                yf, yb, bf_, bb_, vv, gg = rwt

                def TTd(o, a, b, op):
                    S.op("dve", lambda: nc.vector.tensor_tensor(out=o.t[:], in0=a.t[:], in1=b.t[:], op=op), reads=[a.r, b.r], writes=[o.r])
                TTd(yf, yf, yb, ALU.add)
                pmean = P[1]
                for h in range(2):
                    S.op("pe", lambda: nc.tensor.matmul(pmean.t[:, h, 0:HALF], lhsT=blk.t[:], rhs=yf.t[:, h * HALF:(h + 1) * HALF],
                                                        start=True, stop=True), reads=[blk.r, yf.r], writes=[pmean.r])
                S.op("dve", lambda: nc.vector.tensor_tensor(out=yb.t[:].rearrange("p (h t) -> p h t", h=2),
                                                            in0=yf.t[:].rearrange("p (h t) -> p h t", h=2), in1=pmean.t[:, :, 0:HALF],
                                                            op=ALU.subtract), reads=[yf.r, pmean.r], writes=[yb.r])
                TTd(yf, yb, yb, ALU.mult)
                pvar = P[2]
                for h in range(2):
                    S.op("pe", lambda: nc.tensor.matmul(pvar.t[:, h, 0:HALF], lhsT=blk.t[:], rhs=yf.t[:, h * HALF:(h + 1) * HALF],
                                                        start=True, stop=True), reads=[blk.r, yf.r], writes=[pvar.r])
                S.op("act", lambda: nc.scalar.activation(out=yf.t[:].rearrange("p (h t) -> p h t", h=2), in_=pvar.t[:, :, 0:HALF],
                                                         func=AF.Sqrt, bias=gnb.t[:, 0:1]), reads=[pvar.r, gnb.r], writes=[yf.r])
                S.op("dve", lambda: nc.vector.reciprocal(out=yf.t[:], in_=yf.t[:]), reads=[yf.r], writes=[yf.r])
                TTd(yb, yb, yf, ALU.mult)
                S.op("act", lambda: nc.scalar.activation(out=yb.t[:], in_=yb.t[:], func=AF.Identity, scale=lnp.t[:, ck, 0:1],
                                                         bias=lnp.t[:, ck, 1:2]), reads=[yb.r, lnp.r], writes=[yb.r])
                TTd(bf_, bf_, bb_, ALU.add)
                TTd(bf_, bf_, vv, ALU.mult)
                TTd(yb, yb, bf_, ALU.add)
                S.op("dve", lambda: nc.vector.tensor_tensor(out=mixT.t[:, 2 + ck, :], in0=yb.t[:], in1=gg.t[:], op=ALU.mult),
                     reads=[yb.r, gg.r], writes=[mixT.r])
            for half_d in range(2):
                wb = wg[half_d]
                S.dma("pool", wb.t[:], wout_d[:, half_d * 512:(half_d + 1) * 512].rearrange("(k p) n -> p k n", p=128),
                      writes=[wb.r])
                for dd in range(4):
                    d = half_d * 4 + dd
                    pt = P[d % 4]
                    for h in range(2):
                        for k in range(8):
                            S.op("pe", lambda: nc.tensor.matmul(pt.t[:, h, 0:HALF], lhsT=wb.t[:, k, dd * 128:(dd + 1) * 128],
                                                                rhs=mixT.t[:, k, h * HALF:(h + 1) * HALF],
                                                                start=(k == 0), stop=(k == 7)),
                                 reads=[wb.r, mixT.r], writes=[pt.r])
                    S.op("act", lambda: nc.scalar.copy(out=yT.t[:, d, :].rearrange("p (h t) -> p h t", h=2), in_=pt.t[:, :, 0:HALF]),
                         reads=[pt.r], writes=[yT.r])
            emit_rstd(K, yT, sq, P[0], rstd, tmpB)
            residual_update(o_s3)

        emit_rstd(K, hT, sq, P[0], rstd, tmpB)
        emit_modnorm(K, hT, rstd, tmpA, f_s1, f_s2, uT)
        ngrp = (NF + 3) // 4
        for g in range(ngrp):
            f0 = g * 4
            nf = min(4, NF - f0)
            bg, bu = wg[g % 2], wu[g % 2]
            S.dma("pool", bg.t[:, :, 0:nf * 128], wgu_d[:, f0 * 128:(f0 + nf) * 128].rearrange("(k p) n -> p k n", p=128),
                  writes=[bg.r])
            S.dma("pool", bu.t[:, :, 0:nf * 128],
                  wgu_d[:, DFF + f0 * 128:DFF + (f0 + nf) * 128].rearrange("(k p) n -> p k n", p=128), writes=[bu.r])
            for ff in range(nf):
                f = f0 + ff
                pg, pu = (P[0], P[1]) if f % 2 == 0 else (P[2], P[3])
                for h in range(2):
                    for k in range(8):
                        S.op("pe", lambda: nc.tensor.matmul(pg.t[:, h, 0:HALF], lhsT=bg.t[:, k, ff * 128:(ff + 1) * 128],
                                                            rhs=uT.t[:, k, h * HALF:(h + 1) * HALF], start=(k == 0), stop=(k == 7)),
                             reads=[bg.r, uT.r], writes=[pg.r])
                    for k in range(8):
                        S.op("pe", lambda: nc.tensor.matmul(pu.t[:, h, 0:HALF], lhsT=bu.t[:, k, ff * 128:(ff + 1) * 128],
                                                            rhs=uT.t[:, k, h * HALF:(h + 1) * HALF], start=(k == 0), stop=(k == 7)),
                             reads=[bu.r, uT.r], writes=[pu.r])
                sg = tmpA if f % 2 == 0 else tmpB
                S.op("act", lambda: nc.scalar.activation(out=sg.t[:].rearrange("p (h t) -> p h t", h=2), in_=pg.t[:, :, 0:HALF],
                                                         func=AF.Silu), reads=[pg.r], writes=[sg.r])
                S.op("dve", lambda: nc.vector.tensor_tensor(out=hid.t[:, f, :].rearrange("p (h t) -> p h t", h=2),
                                                            in0=sg.t[:].rearrange("p (h t) -> p h t", h=2),
                                                            in1=pu.t[:, :, 0:HALF], op=ALU.mult),
                     reads=[sg.r, pu.r], writes=[hid.r])
        for half_d in range(2):
            wb = wdn[half_d]
            S.dma("pool", wb.t[:], wdn_d[:, half_d * 512:(half_d + 1) * 512].rearrange("(f p) n -> p f n", p=128), writes=[wb.r])
            for dd in range(4):
                d = half_d * 4 + dd
                pt = P[d % 4]
                for h in range(2):
                    for f in range(NF):
                        S.op("pe", lambda: nc.tensor.matmul(pt.t[:, h, 0:HALF], lhsT=wb.t[:, f, dd * 128:(dd + 1) * 128],
                                                            rhs=hid.t[:, f, h * HALF:(h + 1) * HALF],
                                                            start=(f == 0), stop=(f == NF - 1)),
                             reads=[wb.r, hid.r], writes=[pt.r])
                S.op("act", lambda: nc.scalar.copy(out=yT.t[:, d, :].rearrange("p (h t) -> p h t", h=2), in_=pt.t[:, :, 0:HALF]),
                     reads=[pt.r], writes=[yT.r])
        emit_rstd(K, yT, sq, P[0], rstd, tmpB)
        residual_update(f_s3)
        S.dma("sp", hout_d[:, c0:c0 + T].rearrange("(k p) t -> p k t", p=128), hT.t[:], reads=[hT.r], writes=[K_out_res(K)])

        if do_win:
            emit_rstd(K, hT, sq, P[0], rstd, tmpB)
            emit_modnorm(K, hT, rstd, tmpA, x_s1, x_s2, uT)
            ngrp = (NIN + 3) // 4
            for g in range(ngrp):
                f0 = g * 4
                nf = min(4, NIN - f0)
                bg = wg[g % 2]
                S.dma("pool", bg.t[:, :, 0:nf * 128], win_d[:, f0 * 128:(f0 + nf) * 128].rearrange("(k p) n -> p k n", p=128),
                      writes=[bg.r])
                for ff in range(nf):
                    f = f0 + ff
                    pt = P[f % 4]
                    for h in range(2):
                        for k in range(8):
                            S.op("pe", lambda: nc.tensor.matmul(pt.t[:, h, 0:HALF], lhsT=bg.t[:, k, ff * 128:(ff + 1) * 128],
                                                                rhs=uT.t[:, k, h * HALF:(h + 1) * HALF], start=(k == 0), stop=(k == 7)),
                                 reads=[bg.r, uT.r], writes=[pt.r])
                    ob = yT
                    S.op("act" if f % 2 == 0 else "dve",
                         (lambda: nc.scalar.copy(out=ob.t[:, f % 8, :].rearrange("p (h t) -> p h t", h=2), in_=pt.t[:, :, 0:HALF]))
                         if f % 2 == 0 else
                         (lambda: nc.vector.tensor_copy(out=ob.t[:, f % 8, :].rearrange("p (h t) -> p h t", h=2), in_=pt.t[:, :, 0:HALF])),
                         reads=[pt.r], writes=[ob.r])
                    if f % 8 == 7 or f == NIN - 1:
                        fa = (f // 8) * 8
                        n = f - fa + 1
                        S.dma("sp", p_d[fa * 128:(fa + n) * 128, c0:c0 + T].rearrange("(k p) t -> p k t", p=128), ob.t[:, 0:n, :],
                              reads=[ob.r], writes=[K_out_res(K)])
    S.finish("sp", K.outs)
    return K


def K_out_res(K):
    if not hasattr(K, "outs"):
        K.outs = []
    r = Res()
    K.outs.append(r)
    return r


_CACHE = {}


def get_kernel(key, fn):
    if key not in _CACHE:
        _CACHE[key] = fn()
    return _CACHE[key]


TRACE = [False]


def run(K, in_maps):
    if TRACE[0]:
        res = run_bass_kernel_spmd(K.nc, in_maps, core_ids=list(range(NCORE)), trace=True)
        print("EXEC_NS", res.exec_time_ns)
        return res.results
    res = run_bass_kernel_spmd(K.nc, in_maps, core_ids=list(range(NCORE)))
    return res.results


def tok_layout(h_lat, h_ctx):
    outs = []
    for c in range(NCORE):
        b, q = c // 4, c % 4
        lat = h_lat[b, q * 2048:(q + 1) * 2048].reshape(NPASS, TL, -1)
        cx = h_ctx[b, q * 64:(q + 1) * 64].reshape(NPASS, TC, -1)
        a = np.concatenate([lat, cx], axis=1).reshape(TOK, -1)
        outs.append(np.ascontiguousarray(a.T))
    return outs


def tok_unlayout(per_core):
    F = per_core[0].shape[0]
    lat = np.zeros((2, SEQ, F), per_core[0].dtype)
    cx = np.zeros((2, CTX, F), per_core[0].dtype)
    for c in range(NCORE):
        b, q = c // 4, c % 4
        a = per_core[c].T.reshape(NPASS, T, F)
        lat[b, q * 2048:(q + 1) * 2048] = a[:, :TL].reshape(2048, F)
        cx[b, q * 64:(q + 1) * 64] = a[:, TL:].reshape(64, F)
    return lat, cx


def fm(v, n):
    return np.ascontiguousarray(v.reshape(n, 128).T)


def common_maps(l, m_all, norm_g, j0, nj):
    maps = []
    for core in range(NCORE):
        b = core // 4
        ml = m_all[l, b].reshape(9, 8, 128)[j0:j0 + nj]
        mc = m_all[l, 2].reshape(9, 8, 128)[j0:j0 + nj]
        m = np.stack([ml, mc], axis=-1).transpose(2, 0, 1, 3).reshape(128, nj * 8, 2)
        maps.append({
            "m": np.ascontiguousarray(m, dtype=np.float32),
            "ng": np.ascontiguousarray(norm_g[l].reshape(6, 8, 128).transpose(2, 0, 1)),
        })
    return maps


def build_mod_kernel():
    K = Ctx()
    nc, S = K.nc, K.S
    cc_d = K.din("cc", [128, 8, 3])
    w_d = K.din("w", [4, D, 1152])
    b_d = K.din("b", [128, 36])
    o_d = K.dout("m", [128, 36, 3])
    cc = K.sb([128, 8, 3]); S.dma("sp", cc.t[:], cc_d, writes=[cc.r])
    ccb = K.sb([128, 8, 3], BF16)
    S.op("act", lambda: nc.scalar.activation(out=ccb.t[:], in_=cc.t[:], func=AF.Silu), reads=[cc.r], writes=[ccb.r])
    mb = K.sb([128, 36]); S.dma("sp", mb.t[:], b_d, writes=[mb.r])
    ps = K.ps([128, 36, 3])
    wb = [K.sb([128, 8, 1152], BF16) for _ in range(2)]
    for l in range(4):
        w = wb[l % 2]
        S.dma("pool", w.t[:], w_d[l].rearrange("(k p) n -> p k n", p=128), writes=[w.r])
        for fc in range(9):
            for k in range(8):
                S.op("pe", lambda: nc.tensor.matmul(ps.t[:, l * 9 + fc, :], lhsT=w.t[:, k, fc * 128:(fc + 1) * 128], rhs=ccb.t[:, k, :],
                                                    start=(k == 0), stop=(k == 7)), reads=[w.r, ccb.r], writes=[ps.r])
    ms = K.sb([128, 36, 3])
    for s_ in range(3):
        S.op("dve", lambda: nc.vector.tensor_tensor(out=ms.t[:, :, s_], in0=ps.t[:, :, s_], in1=mb.t[:], op=ALU.add),
             reads=[ps.r, mb.r], writes=[ms.r])
    r_ = Res()
    S.dma("sp", o_d, ms.t[:], reads=[ms.r], writes=[r_])
    S.finish("sp", [r_])
    return K


def mod_maps(c, c_ctx, mod_w, mod_b):
    cc = np.stack([c[0].reshape(8, 128).T, c[1].reshape(8, 128).T, c_ctx.reshape(8, 128).T], axis=-1).astype(np.float32)
    maps = []
    for core in range(NCORE):
        cs = slice(core * 1152, (core + 1) * 1152)
        b = mod_b[:, cs].reshape(4, 9, 128).transpose(2, 0, 1).reshape(128, 36)
        maps.append({"cc": np.ascontiguousarray(cc), "w": np.ascontiguousarray(mod_w[:, :, cs]), "b": np.ascontiguousarray(b)})
    return maps


def mod_unpack(res):
    m_all = np.zeros((4, 3, 9 * D), np.float32)
    for core in range(NCORE):
        mm = res[core]["m"].reshape(128, 4, 9, 3)
        m_all[:, :, core * 1152:(core + 1) * 1152] = mm.transpose(1, 3, 2, 0).reshape(4, 3, 1152)
    return m_all


RW_EVERY = [6]
RN = CTX + SEQ
RC = 64
RBLK = 256
RNB = RN // RBLK
RROWS = 9 * 64 + 64 + 64 + 128


def build_rwkv_kernel(nblk=RNB):
    K = Ctx()
    nc, S = K.nc, K.S
    GOFF = [0, 128, 256, 384, 448, 512, 576, 640, 704]
    GROWS = [128, 128, 128, 64, 64, 64, 64, 64, 128]
    UNITS = [dict(P=128, heads=[(0, 0), (1, 64)], gr=(0, 1, 2)), dict(P=64, heads=[(2, 0)], gr=(3, 4, 5))]
    pin_d = K.din("pin", [RROWS, RN + 4])
    mu_d = K.din("mu", [128, 9, 2])
    hp_d = K.din("hp", [128, 2, 5])
    w2_d = K.din("w2", [64, 3, 64])
    a2_d = K.din("a2", [64, 3, 64])
    g2_d = K.din("g2", [128, 3, 64])
    cs_d = K.din("cs", [128, 2, RN])
    cm_d = K.din("cm", [128, 5, 256])
    bd_d = K.din("bd", [128, 2, 128])
    y_d = K.dout("y", [64, 3, RN // RC, 64])
    bon_d = K.dout("bon", [64, 3, RN])
    v_d = K.dout("vout", [64, 3, RN])
    g_d = K.dout("gout", [64, 3, RN])

    def A(tb, ap=None):
        return (tb, tb.t[:] if ap is None else ap)

    def TT(e, o, a, b, op):
        eng = nc.vector if e == "dve" else nc.gpsimd
        S.op(e, lambda: eng.tensor_tensor(out=o[1], in0=a[1], in1=b[1], op=op), reads=[a[0].r, b[0].r], writes=[o[0].r])

    def STT(o, a, sc, b, op0, op1, extra=()):
        S.op("dve", lambda: nc.vector.scalar_tensor_tensor(out=o[1], in0=a[1], scalar=sc, in1=b[1], op0=op0, op1=op1),
             reads=[a[0].r, b[0].r] + list(extra), writes=[o[0].r])

    def TS(o, a, s1, s2, op0, op1=None, extra=()):
        if op1 is None:
            S.op("dve", lambda: nc.vector.tensor_scalar(out=o[1], in0=a[1], scalar1=s1, scalar2=None, op0=op0),
                 reads=[a[0].r] + list(extra), writes=[o[0].r])
        else:
            S.op("dve", lambda: nc.vector.tensor_scalar(out=o[1], in0=a[1], scalar1=s1, scalar2=s2, op0=op0, op1=op1),
                 reads=[a[0].r] + list(extra), writes=[o[0].r])

    def ACT(o, a, func, scale=1.0, bias=None, extra=()):
        if bias is None:
            S.op("act", lambda: nc.scalar.activation(out=o[1], in_=a[1], func=func, scale=scale),
                 reads=[a[0].r] + list(extra), writes=[o[0].r])
        else:
            S.op("act", lambda: nc.scalar.activation(out=o[1], in_=a[1], func=func, scale=scale, bias=bias),
                 reads=[a[0].r] + list(extra), writes=[o[0].r])

    def MM(o, l, r, start=True, stop=True):
        S.op("pe", lambda: nc.tensor.matmul(o[1], lhsT=l[1], rhs=r[1], start=start, stop=stop),
             reads=[l[0].r, r[0].r], writes=[o[0].r])

    cm = K.sb([128, 5, 256]); S.dma("sp", cm.t[:], cm_d, writes=[cm.r])
    cmb = K.sb([128, 5, 256], BF16); S.dma("pool", cmb.t[:], cm_d, writes=[cmb.r])
    bd = K.sb([128, 2, 128]); S.dma("sp", bd.t[:], bd_d, writes=[bd.r])
    mu = K.sb([128, 9, 2]); S.dma("sp", mu.t[:], mu_d, writes=[mu.r])
    muc = K.sb([128, 9, 1])
    STT(A(muc), A(mu, mu.t[:, :, 0:1]), -1.0, A(mu, mu.t[:, :, 1:2]), ALU.mult, ALU.subtract)
    TS(A(muc), A(muc), 1.0, None, ALU.add)
    hp = K.sb([128, 2, 5]); S.dma("sp", hp.t[:], hp_d, writes=[hp.r])
    omka = K.sb([128, 2, 1])
    TS(A(omka), A(hp, hp.t[:, :, 1:2]), -1.0, 1.0, ALU.mult, ALU.add)
    w2 = K.sb([64, 3, 64], BF16); S.dma("pool", w2.t[:], w2_d, writes=[w2.r])
    a2 = K.sb([64, 3, 64], BF16); S.dma("pool", a2.t[:], a2_d, writes=[a2.r])
    g2 = K.sb([128, 3, 64], BF16); S.dma("pool", g2.t[:], g2_d, writes=[g2.r])
    rkb = K.sb([128, 2, 128])
    for u in range(2):
        TS(A(rkb, rkb.t[:, u, :]), A(bd, bd.t[:, 0, :]), hp.t[:, u, 2:3], None, ALU.mult, extra=[hp.r])

    def lw(w, u):
        return (w, w.t[:, 0:2, :].rearrange("p a b -> p (a b)")) if u == 0 else (w, w.t[:, 2, :])

    pm = [K.ps([128, 512]) for _ in range(3)]
    pc = [K.ps([128, 512]) for _ in range(3)]
    ptb = K.ps([128, 1024], BF16)
    pseq = K.ps([128, 512])
    cnt = {"pm": 0, "pc": 0}

    def PM(P):
        cnt["pm"] += 1
        tb = pm[cnt["pm"] % 3]
        return (tb, tb.t[0:P, 0:256])

    def PC(P):
        cnt["pc"] += 1
        tb = pc[cnt["pc"] % 3]
        return (tb, tb.t[0:P, 0:256].rearrange("p (c t) -> p c t", c=4))

    def c4(tb):
        return (tb, tb.t[:].rearrange("p (c t) -> p c t", c=4))

    NB3 = 3
    raw = [K.sb([128, 3, 256]) for _ in range(9)]
    SH = dict(twd=K.sb([64, 256], BF16), adb=K.sb([64, 256], BF16), sgd=K.sb([128, 256], BF16), tmp=K.sb([128, 256]),
              cs=K.sb([128, 2, 256]))
    keep_f = ["pt", "U0"]
    keep_b = ["Rt", "Bhtok", "Khtok", "Vtok", "NrbT", "NrkT", "WT"]
    eo_b = ["At", "Bt", "Kt", "Bh", "Kh", "vb"]
    scr_f = ["xr", "xk", "a", "kk", "kd", "bb", "t1", "t2", "t3", "ld", "cl"]
    scr_b = ["Atok", "Xb", "Mb", "TT", "MakT", "G"]
    PU = [128, 64]
    scratch = [dict([(n, K.sb([PU[u], 256])) for n in scr_f] + [(n, K.sb([PU[u], 256], BF16)) for n in scr_b]) for u in range(2)]
    eout = [[dict([(n, K.sb([PU[u], 256], BF16)) for n in eo_b]) for u in range(2)] for _ in range(2)]
    keep = [[dict([(n, K.sb([PU[u], 256])) for n in keep_f] + [(n, K.sb([PU[u], 256], BF16)) for n in keep_b]) for u in range(2)]
            for _ in range(NB3)]

    def HD(blk, u):
        d = dict(scratch[u])
        d.update(eout[blk % 2][u])
        d.update(keep[blk % NB3][u])
        for a_, b_ in (("rs", "xr"), ("kks", "kk"), ("kds", "kd"), ("bbs", "bb"), ("tmp", "t3")):
            d[a_] = d[b_]
        return d
    ost = [[dict(y=K.sb([PU[u], 4, 64]), bon=K.sb([PU[u], 256]), v=K.sb([PU[u], 256]), g=K.sb([PU[u], 256])) for u in range(2)]
           for _ in range(NB3)]
    H = [K.sb([PU[u], 64]) for u in range(2)]
    Hb = [[K.sb([PU[u], 64], BF16) for _ in range(2)] for u in range(2)]
    Ub = [[K.sb([PU[u], 64], BF16) for _ in range(2)] for u in range(2)]
    for u in range(2):
        S.op("pool", lambda: nc.gpsimd.memset(H[u].t[:], 0.0), writes=[H[u].r])
        S.op("pool", lambda: nc.gpsimd.memset(Hb[u][0].t[:], 0.0), writes=[Hb[u][0].r])
    outs = []
    x9 = K.sb([64, 256])
    x11 = K.sb([128, 256])

    def shift(gi, out, tmp_tb, rows):
        tmp = (tmp_tb, tmp_tb.t[0:rows, :])
        ACT(tmp, A(raw[gi], raw[gi].t[0:rows, 1, :]), AF.Identity, scale=muc.t[0:rows, gi, :], extra=[muc.r])
        STT(tmp, A(raw[gi], raw[gi].t[0:rows, 0, :]), mu.t[0:rows, gi, 0:1], tmp, ALU.mult, ALU.add, extra=[mu.r])
        STT(out, A(raw[gi], raw[gi].t[0:rows, 2, :]), mu.t[0:rows, gi, 1:2], tmp, ALU.mult, ALU.add, extra=[mu.r])

    def pre_shared(blk):
        t0 = blk * RBLK
        cb = t0 + 1 if t0 < CTX else t0 + 3
        for gi in range(9):
            rows, r0 = GROWS[gi], GOFF[gi]
            for s_ in range(3):
                S.dma("sp" if (gi + s_) % 2 == 0 else "act", raw[gi].t[0:rows, s_, :],
                      pin_d[r0:r0 + rows, cb - 1 + s_:cb - 1 + s_ + RBLK], writes=[raw[gi].r])
        S.dma("sp", SH["cs"].t[:], cs_d[:, :, t0:t0 + RBLK], writes=[SH["cs"].r])
        shift(6, A(x9), SH["tmp"], 64); ACT(A(SH["twd"]), A(x9), AF.Tanh)
        shift(7, A(x9), SH["tmp"], 64); ACT(A(SH["adb"]), A(x9), AF.Copy)
        shift(8, A(x11), SH["tmp"], 128); ACT(A(SH["sgd"]), A(x11), AF.Sigmoid)

    def pre_E(blk, u):
        U_ = UNITS[u]
        P = U_["P"]
        O = ost[blk % NB3][u]
        Dh = HD(blk, u)
        cosb = A(SH["cs"], SH["cs"].t[0:P, 0, :])
        sinb = A(SH["cs"], SH["cs"].t[0:P, 1, :])
        m01 = A(cm, cm.t[0:P, 0, :])
        shift(U_["gr"][0], A(Dh["xr"]), Dh["tmp"], P); yield
        shift(U_["gr"][1], A(Dh["xk"]), Dh["tmp"], P); yield
        shift(U_["gr"][2], A(O["v"]), Dh["tmp"], P); yield
        xv = A(O["v"])
        p1 = PM(P); MM(p1, lw(w2, u), A(SH["twd"]))
        ACT(A(Dh["ld"]), p1, AF.Sigmoid, bias=hp.t[0:P, u, 3:4], extra=[hp.r])
        TS(A(Dh["ld"]), A(Dh["ld"]), -0.6065306597126334, None, ALU.mult); yield
        p2 = PM(P); MM(p2, lw(a2, u), A(SH["adb"]))
        ACT(A(Dh["a"]), p2, AF.Sigmoid, bias=hp.t[0:P, u, 4:5], extra=[hp.r]); yield
        p3 = PM(P); MM(p3, lw(g2, u), A(SH["sgd"]))
        ACT(A(O["g"]), p3, AF.Copy); yield
        TS(A(Dh["t1"]), A(Dh["xk"]), hp.t[0:P, u, 0:1], None, ALU.mult, extra=[hp.r])
        TT("pool", A(Dh["t2"]), A(Dh["t1"]), A(Dh["t1"]), ALU.mult)
        p4 = PM(P); MM(p4, (bd, bd.t[0:P, 0, 0:P]), A(Dh["t2"])); yield
        ACT(A(Dh["t3"]), p4, AF.Sqrt)
        TS(A(Dh["t3"]), A(Dh["t3"]), 1e-12, None, ALU.max)
        S.op("dve", lambda: nc.vector.reciprocal(out=Dh["t3"].t[:], in_=Dh["t3"].t[:]), reads=[Dh["t3"].r], writes=[Dh["t3"].r])
        TT("pool", A(Dh["kk"]), A(Dh["t1"]), A(Dh["t3"]), ALU.mult); yield
        ACT(A(Dh["t1"]), A(Dh["a"]), AF.Identity, scale=hp.t[0:P, u, 1:2], bias=omka.t[0:P, u, :], extra=[hp.r, omka.r])
        TT("pool", A(Dh["kd"]), A(Dh["xk"]), A(Dh["t1"]), ALU.mult)
        TT("dve", A(Dh["bb"]), A(Dh["kk"]), A(Dh["a"]), ALU.mult); yield
        TT("pool", A(Dh["t2"]), A(Dh["xr"]), A(Dh["kd"]), ALU.mult)
        p5 = PM(P); MM(p5, (rkb, rkb.t[0:P, u, 0:P]), A(Dh["t2"]))
        ACT(A(O["bon"]), p5, AF.Copy); yield
        for i_, (src, dst) in enumerate((("xr", "rs"), ("kk", "kks"), ("kd", "kds"), ("bb", "bbs"))):
            pr = PM(P); MM(pr, (bd, bd.t[0:P, 1, 0:P]), A(Dh[src]))
            TT("pool", A(Dh["t1"]), A(Dh[src]), cosb, ALU.mult)
            TT("dve", A(Dh["t2"]), pr, sinb, ALU.mult)
            TT("pool" if i_ % 2 else "dve", A(Dh[dst]), A(Dh["t1"]), A(Dh["t2"]), ALU.add); yield
        S.op("dve", lambda: nc.vector.tensor_tensor_scan(out=Dh["cl"].t[:], data0=m01[1], data1=Dh["ld"].t[:], initial=0.0,
                                                         op0=ALU.mult, op1=ALU.add),
             reads=[cm.r, Dh["ld"].r], writes=[Dh["cl"].r])
        ACT(A(Dh["pt"]), A(Dh["cl"]), AF.Exp); yield
        ACT(A(Dh["t3"]), A(Dh["cl"]), AF.Exp, scale=-1.0)
        TT("pool", A(Dh["Bt"]), A(Dh["bbs"]), A(Dh["t3"]), ALU.mult)
        TT("dve", A(Dh["Kt"]), A(Dh["kds"]), A(Dh["t3"]), ALU.mult); yield
        TT("pool", A(Dh["t1"]), A(Dh["cl"]), A(Dh["ld"]), ALU.subtract)
        ACT(A(Dh["t1"]), A(Dh["t1"]), AF.Exp)
        STT(A(Dh["At"]), A(Dh["kks"]), -1.0, A(Dh["t1"]), ALU.mult, ALU.mult); yield
        cl3 = Dh["cl"].t[:].rearrange("p (c t) -> p c t", c=4)
        TT("dve", c4(Dh["t2"]), A(Dh["cl"], cl3[:, :, 63:64].to_broadcast([P, 4, 64])), A(Dh["cl"], cl3), ALU.subtract)
        ACT(A(Dh["t2"]), A(Dh["t2"]), AF.Exp)
        TT("pool", A(Dh["Rt"]), A(Dh["rs"]), A(Dh["pt"]), ALU.mult)
        TT("dve", A(Dh["Bh"]), A(Dh["bbs"]), A(Dh["t2"]), ALU.mult)
        TT("pool", A(Dh["Kh"]), A(Dh["kds"]), A(Dh["t2"]), ALU.mult)
        ACT(A(Dh["vb"]), xv, AF.Copy); yield

    def pre_I(blk, u):
        U_ = UNITS[u]
        P = U_["P"]
        Dh = HD(blk, u)
        m_su = A(cm, cm.t[0:P, 1, :].rearrange("p (c t) -> p c t", c=4))
        m_ui = A(cm, cm.t[0:P, 2, :].rearrange("p (c t) -> p c t", c=4))
        m_sl = A(cm, cm.t[0:P, 3, :].rearrange("p (c t) -> p c t", c=4))
        i4 = A(cm, cm.t[0:P, 4, :].rearrange("p (c t) -> p c t", c=4))
        for i_, (src, dst) in enumerate((("At", "Atok"), ("Bh", "Bhtok"), ("Kh", "Khtok"), ("vb", "Vtok"))):
            pt_ = (ptb, ptb.t[0:P, i_ * 256:(i_ + 1) * 256])
            for (h, po) in U_["heads"]:
                for c in range(4):
                    S.op("pe", lambda: nc.tensor.transpose(out=ptb.t[po:po + 64, i_ * 256 + c * 64:i_ * 256 + (c + 1) * 64],
                                                           in_=Dh[src].t[po:po + 64, c * 64:(c + 1) * 64],
                                                           identity=cmb.t[po:po + 64, 4, 0:64]),
                         reads=[Dh[src].r, cmb.r], writes=[ptb.r])
            ACT(A(Dh[dst]), pt_, AF.Copy); yield

        def chunk_mm(l, r):
            p_ = PC(P)
            for (h, po) in U_["heads"]:
                for c in range(4):
                    MM((p_[0], p_[0].t[po:po + 64, c * 64:(c + 1) * 64]), (l, l.t[po:po + 64, c * 64:(c + 1) * 64]),
                       (r, r.t[po:po + 64, c * 64:(c + 1) * 64]))
            return p_

        px = chunk_mm(Dh["Bt"], Dh["At"]); TT("dve", c4(Dh["Xb"]), px, m_su, ALU.mult); yield
        pmm = chunk_mm(Dh["At"], Dh["Bt"]); TT("dve", c4(Dh["Mb"]), pmm, m_sl, ALU.mult); yield
        pq = chunk_mm(Dh["Kt"], Dh["At"]); TT("dve", c4(Dh["MakT"]), pq, m_su, ALU.mult); yield
        pq = chunk_mm(Dh["Bt"], Dh["Rt"]); TT("dve", c4(Dh["NrbT"]), pq, m_ui, ALU.mult); yield
        pq = chunk_mm(Dh["Kt"], Dh["Rt"]); TT("dve", c4(Dh["NrkT"]), pq, m_ui, ALU.mult); yield
        TT("pool", c4(Dh["TT"]), c4(Dh["Xb"]), i4, ALU.add)
        for lev in range(5):
            pM2 = chunk_mm(Dh["Xb"], Dh["Mb"])
            if lev < 4:
                pX2 = chunk_mm(Dh["Mb"], Dh["Xb"])
                ACT(c4(Dh["Xb"]), pX2, AF.Copy)
            TT("dve", c4(Dh["G"]), pM2, i4, ALU.add)
            if lev < 4:
                S.op("dve", lambda: nc.vector.tensor_copy(out=Dh["Mb"].t[:].rearrange("p (c t) -> p c t", c=4), in_=pM2[1]),
                     reads=[pM2[0].r], writes=[Dh["Mb"].r])
            yield
            pT = chunk_mm(Dh["G"], Dh["TT"])
            ACT(c4(Dh["TT"]), pT, AF.Copy); yield
        pw = chunk_mm(Dh["Atok"], Dh["TT"]); ACT(c4(Dh["WT"]), pw, AF.Copy); yield
        pg_ = chunk_mm(Dh["MakT"], Dh["Vtok"]); ACT(c4(Dh["G"]), pg_, AF.Copy); yield
        pu0 = chunk_mm(Dh["TT"], Dh["G"])
        S.op("dve", lambda: nc.vector.tensor_copy(out=Dh["U0"].t[:].rearrange("p (c t) -> p c t", c=4), in_=pu0[1]),
             reads=[pu0[0].r], writes=[Dh["U0"].r])
        yield

    def seq_block(blk):
        for c in range(4):
            gch = blk * 4 + c
            cur, nxt = gch % 2, (gch + 1) % 2
            sl = slice(c * 64, (c + 1) * 64)
            for u in range(2):
                U_ = UNITS[u]
                P = U_["P"]
                Dh = HD(blk, u)
                pU = (pseq, pseq.t[0:P, u * 64:(u + 1) * 64])
                for (h, po) in U_["heads"]:
                    MM((pseq, pseq.t[po:po + 64, u * 64:(u + 1) * 64]), (Dh["WT"], Dh["WT"].t[po:po + 64, sl]),
                       (Hb[u][cur], Hb[u][cur].t[po:po + 64, :]))
                TT("dve", A(Ub[u][cur]), pU, (Dh["U0"], Dh["U0"].t[:, sl]), ALU.add)
                yield
            for u in range(2):
                U_ = UNITS[u]
                P = U_["P"]
                Dh = HD(blk, u)
                O = ost[blk % NB3][u]
                pH = (pseq, pseq.t[0:P, 128 + u * 64:128 + (u + 1) * 64])
                pY = (pm[2], pm[2].t[0:P, 256 + u * 64:256 + (u + 1) * 64])
                for (h, po) in U_["heads"]:
                    oh = (pseq, pseq.t[po:po + 64, 128 + u * 64:128 + (u + 1) * 64])
                    MM(oh, (Dh["Khtok"], Dh["Khtok"].t[po:po + 64, sl]), (Dh["Vtok"], Dh["Vtok"].t[po:po + 64, sl]), start=True, stop=False)
                    MM(oh, (Dh["Bhtok"], Dh["Bhtok"].t[po:po + 64, sl]), (Ub[u][cur], Ub[u][cur].t[po:po + 64, :]), start=False, stop=True)
                for (h, po) in U_["heads"]:
                    oy = (pm[2], pm[2].t[po:po + 64, 256 + u * 64:256 + (u + 1) * 64])
                    MM(oy, (Dh["Rt"], Dh["Rt"].t[po:po + 64, sl]), (Hb[u][cur], Hb[u][cur].t[po:po + 64, :]), start=True, stop=False)
                    MM(oy, (Dh["NrbT"], Dh["NrbT"].t[po:po + 64, sl]), (Ub[u][cur], Ub[u][cur].t[po:po + 64, :]), start=False, stop=False)
                    MM(oy, (Dh["NrkT"], Dh["NrkT"].t[po:po + 64, sl]), (Dh["Vtok"], Dh["Vtok"].t[po:po + 64, sl]), start=False, stop=True)
                STT(A(H[u]), A(H[u]), Dh["pt"].t[:, c * 64 + 63:c * 64 + 64], pH, ALU.mult, ALU.add, extra=[Dh["pt"].r])
                ACT(A(Hb[u][nxt]), A(H[u]), AF.Copy)
                ACT((O["y"], O["y"].t[:, c, :]), pY, AF.Copy)
                yield

    def drain(gens):
        gens = list(gens)
        while gens:
            for g_ in list(gens):
                try:
                    next(g_)
                except StopIteration:
                    gens.remove(g_)

    def block_out(blk):
        t0 = blk * RBLK
        ch0 = blk * 4
        for u in range(2):
            O = ost[blk % NB3][u]
            for (h, po) in UNITS[u]["heads"]:
                r_ = Res(); outs.append(r_)
                S.dma("sp", y_d[:, h, ch0:ch0 + 4, :], O["y"].t[po:po + 64], reads=[O["y"].r], writes=[r_])
                for (nm, dd) in (("bon", bon_d), ("v", v_d), ("g", g_d)):
                    r_ = Res(); outs.append(r_)
                    S.dma("act", dd[:, h, t0:t0 + RBLK], O[nm].t[po:po + 64, :], reads=[O[nm].r], writes=[r_])

    pre_shared(0)
    drain([pre_E(0, u) for u in range(2)])
    g0 = [pre_I(0, u) for u in range(2)]
    if nblk > 1:
        pre_shared(1)
        g0 += [pre_E(1, u) for u in range(2)]
    drain(g0)
    for blk in range(nblk):
        gens = [seq_block(blk)]
        if blk + 1 < nblk:
            gens += [pre_I(blk + 1, u) for u in range(2)]
        if blk + 2 < nblk:
            pre_shared(blk + 2)
            gens += [pre_E(blk + 2, u) for u in range(2)]
        drain(gens)
        block_out(blk)
    S.finish("sp", outs)
    return K


def K_tmp64(K, name, p=64):
    if not hasattr(K, "_tmps"):
        K._tmps = {}
    if name not in K._tmps:
        K._tmps[name] = K.sb([p, 256])
    return K._tmps[name]


def rope_tables():
    nf = 16
    t = np.arange(SEQ)
    row = (t // 64).astype(np.float32)
    col = (t % 64).astype(np.float32)
    inv = (np.float32(10000.0) ** (-np.arange(nf, dtype=np.float32) / np.float32(nf))).astype(np.float32)
    ang_r = row[:, None] * inv
    ang_c = col[:, None] * inv
    ang = np.concatenate([ang_r, ang_r, ang_c, ang_c], axis=-1).astype(np.float32)
    return np.cos(ang).astype(np.float32), np.sin(ang).astype(np.float32)


def rwkv_consts():
    cm = np.zeros((128, 5, 256), np.float32)
    tt = np.arange(256)
    cm[:, 0, :] = (tt % 64 != 0).astype(np.float32)[None, :]
    s = (np.arange(128) % 64)[:, None]
    t = np.arange(64)[None, :]
    for c in range(4):
        cm[:, 1, c * 64:(c + 1) * 64] = (t > s)
        cm[:, 2, c * 64:(c + 1) * 64] = (t >= s)
        cm[:, 3, c * 64:(c + 1) * 64] = (t < s)
        cm[:, 4, c * 64:(c + 1) * 64] = (t == s)
    Rm = np.zeros((64, 64), np.float32)
    for hf in range(2):
        for m in range(16):
            Rm[hf * 32 + m, hf * 32 + 16 + m] = -1.0
            Rm[hf * 32 + 16 + m, hf * 32 + m] = 1.0
    bd = np.zeros((128, 2, 128), np.float32)
    for k in range(2):
        bd[k * 64:(k + 1) * 64, 0, k * 64:(k + 1) * 64] = 1.0
        bd[k * 64:(k + 1) * 64, 1, k * 64:(k + 1) * 64] = Rm.T
    return cm, bd


def rwkv_maps(l, p_lat, p_ctx, rw_mu, rw_w0, rw_w2, rw_a0, rw_a2, rw_g2, rw_k_k, rw_k_a, rw_r_k):
    cos, sin = rope_tables()
    cm, bd = rwkv_consts()
    maps = []
    o = 768
    for core in range(NCORE):
        b, d, g = core // 4, (core // 2) % 2, core % 2
        lat = p_lat[b, :, o:o + 1536]
        cx = p_ctx[b, :, o:o + 1536]
        cs = np.zeros((128, 2, RN), np.float32)
        cs[:, 0, :CTX] = 1.0
        c_, s_ = (cos, sin) if d == 0 else (cos[::-1], sin[::-1])
        if d == 1:
            lat = lat[::-1]
            cx = cx[::-1]
        cs[:64, 0, CTX:] = c_.T
        cs[64:, 0, CTX:] = c_.T
        cs[:64, 1, CTX:] = s_.T
        cs[64:, 1, CTX:] = s_.T
        hs = [3 * g + hh for hh in range(3)]
        hr = lambda base, hd_: base + np.arange(hd_ * 64, hd_ * 64 + 64)
        feats = [np.concatenate([hr(0, hs[0]), hr(0, hs[1])]), np.concatenate([hr(384, hs[0]), hr(384, hs[1])]),
                 np.concatenate([hr(768, hs[0]), hr(768, hs[1])]), hr(0, hs[2]), hr(384, hs[2]), hr(768, hs[2]),
                 1152 + d * 64 + np.arange(64), 1280 + d * 64 + np.arange(64), 1408 + np.arange(128)]
        fidx = np.concatenate(feats)
        pin = np.zeros((RROWS, RN + 4), np.float32)
        pin[:, 1:1 + CTX] = cx[:, fidx].T
        pin[:, 3 + CTX:3 + CTX + SEQ] = lat[:, fidx].T
        mu = np.zeros((128, 9, 2), np.float32)
        for gi in range(9):
            f = feats[gi]
            mp, mn = rw_mu[l][0][f], rw_mu[l][1][f]
            if d == 1:
                mp, mn = mn, mp
            mu[:len(f), gi, 0] = mp
            mu[:len(f), gi, 1] = mn
        hp = np.zeros((128, 2, 5), np.float32)
        w2 = np.zeros((64, 3, 64), np.float32)
        a2 = np.zeros((64, 3, 64), np.float32)
        g2 = np.zeros((128, 3, 64), np.float32)
        for hh in range(3):
            hd_ = hs[hh]
            cs_ = slice(hd_ * 64, hd_ * 64 + 64)
            u, po = (0, hh * 64) if hh < 2 else (1, 0)
            hp[po:po + 64, u, 0] = rw_k_k[l][cs_]
            hp[po:po + 64, u, 1] = rw_k_a[l][cs_]
            hp[po:po + 64, u, 2] = rw_r_k[l][hd_]
            hp[po:po + 64, u, 3] = rw_w0[l][d][cs_]
            hp[po:po + 64, u, 4] = rw_a0[l][d][cs_]
            w2[:, hh, :] = rw_w2[l][d][:, cs_]
            a2[:, hh, :] = rw_a2[l][d][:, cs_]
            g2[:, hh, :] = rw_g2[l][:, cs_]
        maps.append({"pin": pin, "mu": mu, "hp": hp, "w2": w2, "a2": a2, "g2": g2, "cs": cs, "cm": cm, "bd": bd})
    return maps


def rwkv_unpack(res):
    y = np.zeros((2, 2, RN, 384), np.float32)
    bon = np.zeros((2, 2, RN, 384), np.float32)
    v = np.zeros((2, RN, 384), np.float32)
    g = np.zeros((2, RN, 384), np.float32)
    for core in range(NCORE):
        b, d, gg = core // 4, (core // 2) % 2, core % 2
        r = res[core]
        yy = r["y"].transpose(1, 2, 0, 3).reshape(3, RN, 64)
        bb = r["bon"].transpose(1, 2, 0)
        vv = r["vout"].transpose(1, 2, 0)
        gq = r["gout"].transpose(1, 2, 0)

        def unrev(a):
            if d == 0:
                return a
            return np.concatenate([a[:, :CTX][:, ::-1], a[:, CTX:][:, ::-1]], axis=1)
        yy, bb, vv, gq = unrev(yy), unrev(bb), unrev(vv), unrev(gq)
        for hh in range(3):
            hd_ = 3 * gg + hh
            y[d, b, :, hd_ * 64:(hd_ + 1) * 64] = yy[hh]
            bon[d, b, :, hd_ * 64:(hd_ + 1) * 64] = bb[hh]
            if d == 0:
                v[b, :, hd_ * 64:(hd_ + 1) * 64] = vv[hh]
                g[b, :, hd_ * 64:(hd_ + 1) * 64] = gq[hh]
    return y, bon, v, g


def build_natten_kernel(do_ctx=True):
    K = Ctx()
    nc, S = K.nc, K.S
    q_d = K.din("q", [64, 6, 2048])
    k_d = K.din("k", [64, 6, 2560])
    v_d = K.din("v", [64, 6, 40, 65])
    kc_d = K.din("kc", [64, 6, 256])
    vc_d = K.din("vc", [128, 6, 2, 65])
    qc_d = K.din("qc", [64, 6, 64])
    bt_d = K.din("bt", [64, 6, 15, 64])
    bsp_d = K.din("bsp", [64, 6, 8, 12, 64])
    ol_d = K.dout("ol", [64, 6, 32, 64])
    oc_d = K.dout("oc", [64, 6, 64])
    qT = K.sb([64, 6, 2048], BF16); S.dma("pool", qT.t[:], q_d, writes=[qT.r])
    kT = K.sb([64, 6, 2560], BF16); S.dma("pool", kT.t[:], k_d, writes=[kT.r])
    va = K.sb([64, 6, 40, 65], BF16); S.dma("pool", va.t[:], v_d, writes=[va.r])
    kcT = K.sb([64, 6, 256], BF16); S.dma("pool", kcT.t[:], kc_d, writes=[kcT.r])
    vca = K.sb([128, 6, 2, 65], BF16); S.dma("pool", vca.t[:], vc_d, writes=[vca.r])
    qcT = K.sb([64, 6, 64], BF16); S.dma("pool", qcT.t[:], qc_d, writes=[qcT.r])
    bt = K.sb([64, 6, 15, 64]); S.dma("sp", bt.t[:], bt_d, writes=[bt.r])
    ost = K.sb([64, 6, 32, 64])
    ocs = K.sb([64, 6, 64])
    pS = [K.ps([128, 512]) for _ in range(2)]
    pC = [K.ps([128, 512]) for _ in range(2)]
    pO = [K.ps([128, 512]) for _ in range(2)]
    sbt = [K.sb([64, 12, 64]) for _ in range(2)]
    Et = [K.sb([64, 12, 64], BF16) for _ in range(2)]
    pS2 = K.ps([128, 512])
    spb = [K.sb([64, 12, 64]) for _ in range(2)]
    Ect = [K.sb([128, 2, 64], BF16) for _ in range(2)]
    rd = [K.sb([64, 1]) for _ in range(2)]
    return K, dict(qT=qT, kT=kT, va=va, kcT=kcT, vca=vca, qcT=qcT, bt=bt, ost=ost, ocs=ocs, pS=pS, pC=pC, pO=pO, sbt=sbt, Et=Et,
                   Ect=Ect, rd=rd, ol_d=ol_d, oc_d=oc_d, pS2=pS2, spb=spb, bsp_d=bsp_d)


def emit_natten(K, T_, do_ctx):
    nc, S = K.nc, K.S
    qT, kT, va, kcT, vca, qcT, bt, ost, ocs = (T_[n] for n in ("qT", "kT", "va", "kcT", "vca", "qcT", "bt", "ost", "ocs"))
    it = 0
    for h in range(6):
        for il in range(32 + (1 if do_ctx else 0)):
            par = it % 2
            it += 1
            ps_, pc_, po_ = T_["pS"][par], T_["pC"][par], T_["pO"][par]
            sb_, E_, Ec_, rd_ = T_["sbt"][par], T_["Et"][par], T_["Ect"][par], T_["rd"][par]
            is_ctx = il == 32
            qv = (qcT, qcT.t[:, h, :]) if is_ctx else (qT, qT.t[:, h, il * 64:(il + 1) * 64])
            special = (not is_ctx) and (il < 4 or il >= 28)
            if not is_ctx:
                if il < 4:
                    lo, hi, sp = il, 12, il
                elif il >= 28:
                    lo, hi, sp = 28, il + 8, il - 24
                else:
                    lo, hi, sp = il, il + 8, None
                nr = hi - lo
                ps2 = T_["pS2"]
                for r in range(nr):
                    pt_ = ps_ if r < 8 else ps2
                    rr = r % 8
                    S.op("pe", lambda: nc.tensor.matmul(pt_.t[0:64, rr * 64:(rr + 1) * 64], lhsT=kT.t[:, h, (lo + r) * 64:(lo + r + 1) * 64],
                                                        rhs=qv[1], start=True, stop=True), reads=[kT.r, qv[0].r], writes=[pt_.r])
            for tci in range(2):
                S.op("pe", lambda: nc.tensor.matmul(pc_.t[:, tci * 64:(tci + 1) * 64], lhsT=kcT.t[:, h, tci * 128:(tci + 1) * 128],
                                                    rhs=qv[1], start=True, stop=True), reads=[kcT.r, qv[0].r], writes=[pc_.r])
            if not is_ctx:
                if special:
                    sb_b = T_["spb"][sp % 2]
                    S.dma("sp", sb_b.t[:, 0:nr, :], T_["bsp_d"][:, h, sp, 0:nr, :], writes=[sb_b.r])
                    bias_a = (sb_b, sb_b.t[:, 0:min(nr, 8), :])
                    bias_b = (sb_b, sb_b.t[:, 8:nr, :]) if nr > 8 else None
                else:
                    bias_a = (bt, bt.t[:, h, 3:11, :])
                    bias_b = None
                n1 = min(nr, 8)
                S.op("dve", lambda: nc.vector.scalar_tensor_tensor(out=sb_.t[:, 0:n1, :],
                                                                    in0=ps_.t[0:64, 0:n1 * 64].rearrange("p (r c) -> p r c", r=n1),
                                                                    scalar=0.125, in1=bias_a[1], op0=ALU.mult, op1=ALU.add),
                     reads=[ps_.r, bias_a[0].r], writes=[sb_.r])
                if bias_b is not None:
                    n2 = nr - 8
                    S.op("dve", lambda: nc.vector.scalar_tensor_tensor(out=sb_.t[:, 8:nr, :],
                                                                        in0=ps2.t[0:64, 0:n2 * 64].rearrange("p (r c) -> p r c", r=n2),
                                                                        scalar=0.125, in1=bias_b[1], op0=ALU.mult, op1=ALU.add),
                         reads=[ps2.r, bias_b[0].r], writes=[sb_.r])
                S.op("act", lambda: nc.scalar.activation(out=E_.t[:, 0:nr, :], in_=sb_.t[:, 0:nr, :], func=AF.Exp), reads=[sb_.r], writes=[E_.r])
            S.op("act", lambda: nc.scalar.activation(out=Ec_.t[:], in_=pc_.t[:, 0:128].rearrange("p (a c) -> p a c", a=2), func=AF.Exp,
                                                     scale=0.125), reads=[pc_.r], writes=[Ec_.r])
            first = True
            if not is_ctx:
                for r in range(nr):
                    S.op("pe", lambda: nc.tensor.matmul(po_.t[0:64, 0:65], lhsT=E_.t[:, r, :], rhs=va.t[:, h, lo + r, :],
                                                        start=first, stop=False), reads=[E_.r, va.r], writes=[po_.r])
                    first = False
            for tci in range(2):
                S.op("pe", lambda: nc.tensor.matmul(po_.t[0:64, 0:65], lhsT=Ec_.t[:, tci, :], rhs=vca.t[:, h, tci, :],
                                                    start=first, stop=(tci == 1)), reads=[Ec_.r, vca.r], writes=[po_.r])
                first = False
            S.op("dve", lambda: nc.vector.reciprocal(out=rd_.t[:], in_=po_.t[0:64, 64:65]), reads=[po_.r], writes=[rd_.r])
            dst = (ocs, ocs.t[:, h, :]) if is_ctx else (ost, ost.t[:, h, il, :])
            S.op("dve", lambda: nc.vector.tensor_scalar(out=dst[1], in0=po_.t[0:64, 0:64], scalar1=rd_.t[:, 0:1], scalar2=None, op0=ALU.mult),
                 reads=[po_.r, rd_.r], writes=[dst[0].r])
    r1, r2 = Res(), Res()
    S.dma("sp", T_["ol_d"], ost.t[:], reads=[ost.r], writes=[r1])
    if do_ctx:
        S.dma("sp", T_["oc_d"], ocs.t[:], reads=[ocs.r], writes=[r2])
    else:
        S.op("pool", lambda: nc.gpsimd.memset(ocs.t[:], 0.0), writes=[ocs.r])
        S.dma("sp", T_["oc_d"], ocs.t[:], reads=[ocs.r], writes=[r2])
    S.finish("sp", [r1, r2])


def natten_bias_table(rpb_l):
    c = np.arange(64)[None, :]
    ck = np.arange(64)[:, None]
    win0 = np.clip(c - 8, 0, 48)
    valid = (ck >= win0) & (ck < win0 + 16)
    off = np.clip(ck - c + 15, 0, 30)
    g = rpb_l[:, :, off]
    g = np.where(valid[None, None], g, np.float32(-30000.0)).astype(np.float32)
    return np.ascontiguousarray(g.transpose(2, 0, 1, 3))


def natten_maps(l, p_lat, p_ctx, na_rpb):
    o = 768 + 1536
    bt = natten_bias_table(na_rpb[l])
    maps = []
    for core in range(NCORE):
        b, qq = core // 4, core % 4
        na_l = p_lat[b, :, o:o + 1152].reshape(128, 64, 3, 6, 64)
        na_c = p_ctx[b, :, o:o + 1152].reshape(256, 3, 6, 64)
        r0 = 32 * qq
        q = na_l[r0:r0 + 32, :, 0].reshape(2048, 6, 64).transpose(2, 1, 0)
        kh = np.zeros((40, 64, 6, 64), np.float32)
        vh = np.zeros((40, 64, 6, 64), np.float32)
        lo, hi = max(r0 - 4, 0), min(r0 + 36, 128)
        kh[lo - (r0 - 4):hi - (r0 - 4)] = na_l[lo:hi, :, 1]
        vh[lo - (r0 - 4):hi - (r0 - 4)] = na_l[lo:hi, :, 2]
        k = kh.reshape(2560, 6, 64).transpose(2, 1, 0)
        v = np.ones((64, 6, 40, 65), np.float32)
        v[:, :, :, :64] = vh.transpose(1, 2, 0, 3)
        kc = na_c[:, 1].transpose(2, 1, 0)
        vc = np.ones((128, 6, 2, 65), np.float32)
        vc[:, :, :, :64] = na_c[:, 2].reshape(2, 128, 6, 64).transpose(1, 2, 0, 3)
        qc = na_c[qq * 64:(qq + 1) * 64, 0].transpose(2, 1, 0)
        bsp = np.full((64, 6, 8, 12, 64), -30000.0, np.float32)
        for sp in range(8):
            il = sp if sp < 4 else sp + 24
            lo = il if il < 4 else 28
            hi = 12 if il < 4 else il + 8
            i = r0 + il
            start = min(max(i - 4, 0), 120)
            for j in range(hi - lo):
                ar = r0 - 4 + lo + j
                if start <= ar < start + 8:
                    bsp[:, :, sp, j, :] = bt[:, :, ar - i + 7, :]
        maps.append({"q": np.ascontiguousarray(q), "k": np.ascontiguousarray(k), "v": v, "kc": np.ascontiguousarray(kc),
                     "vc": vc, "qc": np.ascontiguousarray(qc), "bt": bt, "bsp": bsp})
    return maps


def natten_unpack(res):
    o_lat = np.zeros((2, SEQ, 384), np.float32)
    o_ctx = np.zeros((2, CTX, 384), np.float32)
    for core in range(NCORE):
        b, qq = core // 4, core % 4
        ol = res[core]["ol"]
        o_lat[b, qq * 2048:(qq + 1) * 2048] = ol.transpose(2, 0, 1, 3).reshape(2048, 384)
        oc = res[core]["oc"]
        o_ctx[b, qq * 64:(qq + 1) * 64] = oc.reshape(64, 384)
    return o_lat, o_ctx


MAGIC = 12582912.0
TWO_PI = 6.283185307179586


HY_STAGE = [3]
HY_ONLY = [""]
HY_FLAGS = set()


def hyena_seq(K, tag, L, do_it, P_, params):
    nc, S = K.nc, K.S
    NJ = L // 128
    E2 = 2 * L
    ph_d = K.din("ph" + tag, [3, 128, 96, NJ, 2])
    zx_d = K.din("zx" + tag, [64, E2])
    wx_d = K.din("wx" + tag, [64, E2])
    out_d = K.dout("z" + tag, [128, 32, NJ, 2])
    kext = K.nc.dram_tensor("kext" + tag, [64, E2], BF16, kind="Internal")
    kext_ap = kext.ap()
    kres = Res()
    outs = []
    if not do_it:
        zt = K.sb([128, 32 * NJ * 2])
        S.op("pool", lambda: nc.gpsimd.memset(zt.t[:], 0.0), writes=[zt.r])
        r_ = Res()
        S.dma("sp", out_d.rearrange("p a b c -> p (a b c)"), zt.t[:], reads=[zt.r], writes=[r_])
        return [r_]
    w1, w2, w3, cw, fq, fb1, fb2, hb, ident = (params[n] for n in ("w1", "w2", "w3", "cw", "fq", "fb1", "fb2", "hb", "ident"))
    CH = 512 if E2 >= 512 else E2
    nchunk = E2 // CH
    zx = [K.sb([64, CH]) for _ in range(2)]
    wx = [K.sb([64, CH]) for _ in range(2)]
    ta = K.sb([64, CH]); tb = K.sb([64, CH]); h1 = K.sb([64, CH]); h2 = K.sb([64, CH])
    fk = [K.sb([64, CH], BF16) for _ in range(2)]
    ff = K.sb([64, CH])

    def sin_layer(ps, fbias, dst):
        S.op("dve", lambda: nc.vector.tensor_scalar(out=ta.t[:], in0=ps[1], scalar1=fq.t[:, 0:1], scalar2=fbias.t[:, 0:1],
                                                    op0=ALU.mult, op1=ALU.add), reads=[ps[0].r, fq.r, fbias.r], writes=[ta.r])
        S.op("dve", lambda: nc.vector.tensor_scalar(out=tb.t[:], in0=ta.t[:], scalar1=1.0 / TWO_PI, scalar2=MAGIC,
                                                    op0=ALU.mult, op1=ALU.add), reads=[ta.r], writes=[tb.r])
        S.op("dve", lambda: nc.vector.tensor_scalar(out=tb.t[:], in0=tb.t[:], scalar1=MAGIC, scalar2=-TWO_PI,
                                                    op0=ALU.subtract, op1=ALU.mult), reads=[tb.r], writes=[tb.r])
        S.op("dve", lambda: nc.vector.tensor_tensor(out=ta.t[:], in0=ta.t[:], in1=tb.t[:], op=ALU.add), reads=[ta.r, tb.r], writes=[ta.r])
        S.op("dve", lambda: nc.vector.tensor_scalar(out=ta.t[:], in0=ta.t[:], scalar1=3.141592, scalar2=-3.141592,
                                                    op0=ALU.min, op1=ALU.max), reads=[ta.r], writes=[ta.r])
        S.op("act", lambda: nc.scalar.activation(out=dst.t[:], in_=ta.t[:], func=AF.Sin), reads=[ta.r], writes=[dst.r])

    for ci in range(nchunk if "nofilt" not in HY_FLAGS else 0):
        e0 = ci * CH
        zt_, wt_ = zx[ci % 2], wx[ci % 2]
        S.dma("sp", zt_.t[:], zx_d[:, e0:e0 + CH], writes=[zt_.r])
        S.dma("act", wt_.t[:], wx_d[:, e0:e0 + CH], writes=[wt_.r])
        p1 = P_[ci % 2]
        S.op("pe", lambda: nc.tensor.matmul(p1.t[0:64, 0:CH], lhsT=w1.t[:], rhs=zt_.t[:], start=True, stop=True),
             reads=[w1.r, zt_.r], writes=[p1.r])
        sin_layer((p1, p1.t[0:64, 0:CH]), fb1, h1)
        p2 = P_[2 + ci % 2]
        S.op("pe", lambda: nc.tensor.matmul(p2.t[0:64, 0:CH], lhsT=w2.t[:], rhs=h1.t[:], start=True, stop=True),
             reads=[w2.r, h1.r], writes=[p2.r])
        sin_layer((p2, p2.t[0:64, 0:CH]), fb2, h2)
        segs = []
        if e0 < L:
            segs.append((0, min(CH, L - e0), 0))
        if e0 <= L < e0 + CH:
            segs.append((L - e0, L - e0 + 1, 2))
        if e0 + CH > L + 1:
            segs.append((max(0, L + 1 - e0), CH, 1))
        p3 = P_[4 + ci % 2]
        for (a, b_, wi) in segs:
            S.op("pe", lambda: nc.tensor.matmul(p3.t[0:64, a:b_], lhsT=w3.t[:, wi, :], rhs=h2.t[:, a:b_], start=True, stop=True),
                 reads=[w3.r, h2.r], writes=[p3.r])
        S.op("dve", lambda: nc.vector.tensor_tensor(out=ff.t[:], in0=p3.t[0:64, 0:CH], in1=wt_.t[:], op=ALU.mult),
             reads=[p3.r, wt_.r], writes=[ff.r])
        if e0 <= L < e0 + CH:
            S.op("dve", lambda: nc.vector.tensor_tensor(out=ff.t[:, L - e0:L - e0 + 1], in0=ff.t[:, L - e0:L - e0 + 1], in1=hb.t[:, 0:1],
                                                        op=ALU.add), reads=[ff.r, hb.r], writes=[ff.r])
        fkt = fk[ci % 2]
        S.op("act", lambda: nc.scalar.copy(out=fkt.t[:], in_=ff.t[:]), reads=[ff.r], writes=[fkt.r])
        S.dma("sp", kext_ap[:, e0:e0 + CH], fkt.t[:], reads=[fkt.r], writes=[kres])
    if "kdbg" in HY_FLAGS:
        kd_d = K.dout("kd" + tag, [64, E2], BF16)
        r_ = Res(); outs.append(r_)
        S.dma("sp", kd_d, kext_ap, reads=[kres], writes=[r_])
    if HY_STAGE[0] < 2:
        zt = K.sb([128, 32 * NJ * 2])
        S.op("pool", lambda: nc.gpsimd.memset(zt.t[:], 0.0), writes=[zt.r])
        r_ = Res()
        S.dma("sp", out_d.rearrange("p a b c -> p (a b c)"), zt.t[:], reads=[zt.r], writes=[r_])
        return [r_, kres]
    G = K.sb([128, 64, NJ, 2])
    Ub = K.sb([128, 32, NJ, 2], BF16)
    Z1b = K.sb([128, 32, NJ, 2], BF16)
    xs = [K.sb([128, 32, NJ * 2]) for _ in range(3)]
    Z2 = TB(xs[1].t[:].rearrange("p r (j b) -> p r j b", b=2))
    Z2.r = xs[1].r
    cwb = params["cwb"]
    for g in range(3):
        for s_ in range(3):
            S.dma("sp" if s_ != 1 else "act", xs[s_].t[:], ph_d[s_, :, g * 32:(g + 1) * 32].rearrange("p r j b -> p r (j b)"),
                  writes=[xs[s_].r])

        def wb(k_):
            return cwb.t[:, g * 32:(g + 1) * 32, k_:k_ + 1].to_broadcast([128, 32, NJ * 2])
        for s_ in range(3):
            S.op("dve", lambda: nc.vector.tensor_tensor(out=xs[s_].t[:], in0=xs[s_].t[:], in1=wb(s_), op=ALU.mult),
                 reads=[xs[s_].r, cwb.r], writes=[xs[s_].r])
        S.op("dve", lambda: nc.vector.tensor_tensor(out=xs[0].t[:], in0=xs[0].t[:], in1=xs[1].t[:], op=ALU.add),
             reads=[xs[0].r, xs[1].r], writes=[xs[0].r])
        S.op("dve", lambda: nc.vector.tensor_tensor(out=xs[0].t[:], in0=xs[0].t[:], in1=xs[2].t[:], op=ALU.add),
             reads=[xs[0].r, xs[2].r], writes=[xs[0].r])
        dst = (Ub, Ub.t[:].rearrange("p r j b -> p r (j b)")) if g == 0 else \
              (G, G.t[:, (g - 1) * 32:g * 32].rearrange("p r j b -> p r (j b)"))
        S.op("dve", lambda: nc.vector.tensor_tensor(out=dst[1], in0=xs[0].t[:], in1=wb(3), op=ALU.add),
             reads=[xs[0].r, cwb.r], writes=[dst[0].r])
    if HY_STAGE[0] < 3:
        r_ = Res()
        S.dma("sp", out_d, G.t[:, 0:32], reads=[G.r, Ub.r], writes=[r_])
        return [r_, kres]
    TW = E2 - 128
    Tz = [K.sb([128, TW], BF16) for _ in range(2)]
    it = 0
    for o in range(2):
        src_t = Ub if o == 0 else Z1b
        for c in range(32):
            tz = Tz[it % 2]
            py = P_[it % 4]
            it += 1
            row = o * 32 + c
            srcap = bass.AP(kext_ap.tensor, row * E2 + 1, [[1, 128], [1, TW]])
            S.dma("sp" if it % 2 == 0 else "act", tz.t[:], srcap, reads=[kres], writes=[tz.r])
            if "tzdbg" in HY_FLAGS and o == 0 and c in (0, 1):
                td_d = K.dout("tzd%d" % c + tag, [128, TW], BF16)
                r_ = Res(); outs.append(r_)
                S.dma("sp", td_d, tz.t[:], reads=[tz.r], writes=[r_])
            deltas = [0] + [d for k_ in range(1, NJ) for d in (k_, -k_)]
            for n_, dl in enumerate(deltas):
                Jlo, Jhi = max(0, -dl), min(NJ - 1, NJ - 1 - dl)
                nJ = Jhi - Jlo + 1
                m0 = L + 128 * (dl if o == 0 else -dl) - 128
                S.op("pe", lambda: nc.tensor.matmul(py.t[:, (Jlo + dl) * 2:(Jlo + dl + nJ) * 2], lhsT=tz.t[:, m0:m0 + 128],
                                                    rhs=src_t.t[:, c, Jlo:Jlo + nJ, :], start=(n_ == 0), stop=(n_ == len(deltas) - 1),
                                                    skip_group_check=True),
                     reads=[tz.r, src_t.r], writes=[py.r])
            if "tzdbg" in HY_FLAGS and o == 0 and c in (0, 1):
                pyd_d = K.dout("pyd%d" % c + tag, [128, NJ * 2])
                pys = K.sb([128, NJ * 2])
                S.op("act", lambda: nc.scalar.copy(out=pys.t[:], in_=py.t[:, 0:NJ * 2]), reads=[py.r], writes=[pys.r])
                r_ = Res(); outs.append(r_)
                S.dma("sp", pyd_d, pys.t[:], reads=[pys.r], writes=[r_])
            if o == 0:
                S.op("dve", lambda: nc.vector.tensor_tensor(out=Z1b.t[:, c, :, :], in0=py.t[:, 0:NJ * 2].rearrange("p (j b) -> p j b", b=2),
                                                            in1=G.t[:, c, :, :], op=ALU.mult), reads=[py.r, G.r], writes=[Z1b.r])
            else:
                S.op("dve", lambda: nc.vector.tensor_tensor(out=Z2.t[:, c, :, :], in0=py.t[:, 0:NJ * 2].rearrange("p (j b) -> p j b", b=2),
                                                            in1=G.t[:, 32 + c, :, :], op=ALU.mult), reads=[py.r, G.r], writes=[Z2.r])
    r_ = Res()
    S.dma("sp", out_d, Z2.t, reads=[Z2.r], writes=[r_])
    return [r_] + outs


def build_hyena_kernel(do_ctx=True):
    K = Ctx()
    nc, S = K.nc, K.S
    prm_d = K.din("prm", [128, 8])
    w1_d = K.din("w1", [64, 64])
    w2_d = K.din("w2", [64, 64])
    w3_d = K.din("w3", [64, 3, 64])
    id_d = K.din("ident", [128, 128])
    prm = K.sb([128, 8]); S.dma("sp", prm.t[:], prm_d, writes=[prm.r])
    w1 = K.sb([64, 64]); S.dma("sp", w1.t[:], w1_d, writes=[w1.r])
    w2 = K.sb([64, 64]); S.dma("sp", w2.t[:], w2_d, writes=[w2.r])
    w3 = K.sb([64, 3, 64]); S.dma("sp", w3.t[:], w3_d, writes=[w3.r])
    ident = K.sb([128, 128]); S.dma("sp", ident.t[:], id_d, writes=[ident.r])
    cwb_d = K.din("cwb", [128, 96, 4])
    cwb = K.sb([128, 96, 4]); S.dma("sp", cwb.t[:], cwb_d, writes=[cwb.r])
    fb = K.sb([64, 2])
    S.op("dve", lambda: nc.vector.tensor_scalar(out=fb.t[:], in0=prm.t[0:64, 5:7], scalar1=prm.t[0:64, 4:5], scalar2=None, op0=ALU.mult),
         reads=[prm.r], writes=[fb.r])

    class V:
        def __init__(s, tb, ap):
            s.t, s.r = ap, tb.r
    params = dict(w1=w1, w2=w2, w3=w3, cw=V(prm, prm.t[0:96, 0:4]), fq=V(prm, prm.t[0:64, 4:5]), fb1=V(fb, fb.t[:, 0:1]),
                  fb2=V(fb, fb.t[:, 1:2]), hb=V(prm, prm.t[0:64, 7:8]), ident=ident, cwb=cwb)
    P_ = [K.ps([128, 512]) for _ in range(8)]
    outs = hyena_seq(K, "l", SEQ, HY_ONLY[0] != "c", P_, params)
    outs += hyena_seq(K, "c", CTX, do_ctx and HY_ONLY[0] != "l", P_, params)
    S.finish("sp", outs)
    return K


def hyena_consts(L):
    f32 = np.float32
    t = np.linspace(0.0, 1.0, L, dtype=f32)
    ang = (f32(2.0 * np.pi) * np.arange(L, dtype=f32) / f32(L)).astype(f32)
    fr = np.linspace(1e-4, 15.0, 16, dtype=f32)
    z = np.concatenate([t[:, None], np.cos(fr[None, :] * ang[:, None]), -np.sin(fr[None, :] * ang[:, None])], axis=-1).astype(f32)
    deltas = np.abs(np.linspace(np.log(1e-2) / 1.5, np.log(1e-2) / 0.3, 256, dtype=f32))
    win = np.exp(-t[:, None] * deltas[None, :]).astype(f32)
    e = np.arange(2 * L)
    pos0 = np.abs(e - L)
    pos0[0] = 0
    zx = np.zeros((64, 2 * L), np.float32)
    zx[:33] = z[pos0].T
    w0_ = win[pos0]; w0_[0] = 0.0
    return zx, w0_


def hyena_maps(l, p_lat, p_ctx, hy_conv_w, hy_conv_b, hy_f_w1, hy_f_b1, hy_f_w2, hy_f_b2, hy_f_w3, hy_freq, hy_bias):
    zxl, wl = hyena_consts(SEQ)
    zxc, wc = hyena_consts(CTX)
    maps = []
    ident = np.eye(128, dtype=np.float32)
    for core in range(NCORE):
        ch = np.arange(32 * core, 32 * core + 32)
        rows = np.concatenate([ch, 256 + ch, 512 + ch])
        def blocked(p_, L):
            NJ = L // 128
            x = np.zeros((96, 2, L + 2), np.float32)
            x[:, :, 1:1 + L] = p_[:, :, rows].transpose(2, 0, 1)
            out = np.zeros((3, 128, 96, NJ, 2), np.float32)
            p = np.arange(128)[:, None]
            J = np.arange(NJ)[None, :]
            t_n = 128 * J + p
            t_r = 128 * J + 127 - p
            for s_ in range(3):
                out[s_, :, 32:64] = x[32:64][:, :, 1 + t_n + (s_ - 1)].transpose(2, 0, 3, 1)
                out[s_, :, 0:32] = x[0:32][:, :, 1 + t_r + (s_ - 1)].transpose(2, 0, 3, 1)
                out[s_, :, 64:96] = x[64:96][:, :, 1 + t_r + (s_ - 1)].transpose(2, 0, 3, 1)
            return out
        phl = blocked(p_lat, SEQ)
        phc = blocked(p_ctx, CTX)
        prm = np.zeros((128, 8), np.float32)
        prm[:96, 0:3] = hy_conv_w[l][:, rows].T
        prm[:96, 3] = hy_conv_b[l][rows]
        prm[:64, 4] = hy_freq[l]
        prm[:64, 5] = hy_f_b1[l]
        prm[:64, 6] = hy_f_b2[l]
        prm[:64, 7] = hy_bias[l][:, ch].reshape(64)
        cwb = np.ascontiguousarray(np.broadcast_to(prm[None, :96, 0:4], (128, 96, 4)))
        w3r = hy_f_w3[l].reshape(64, 2, 2, 256)[:, :, :, ch]
        WA = np.concatenate([w3r[:, 0, 1], w3r[:, 1, 0]], axis=-1)
        WB = np.concatenate([w3r[:, 0, 0], w3r[:, 1, 1]], axis=-1)
        WC = np.concatenate([w3r[:, 0, 0], w3r[:, 1, 0]], axis=-1)
        w3 = np.ascontiguousarray(np.stack([WA, WB, WC], axis=1))
        wxl = np.ascontiguousarray(np.tile(wl[:, ch].T, (2, 1)))
        wxc = np.ascontiguousarray(np.tile(wc[:, ch].T, (2, 1)))
        maps.append({"prm": prm, "w1": np.concatenate([hy_f_w1[l], np.zeros((31, 64), np.float32)], axis=0), "w2": hy_f_w2[l], "w3": w3, "ident": ident, "cwb": cwb,
                     "phl": phl, "zxl": zxl, "wxl": wxl, "phc": phc, "zxc": zxc, "wxc": wxc})
    return maps


def hyena_unpack(res):
    o_lat = np.zeros((2, SEQ, 256), np.float32)
    o_ctx = np.zeros((2, CTX, 256), np.float32)
    for core in range(NCORE):
        zl = res[core]["zl"][::-1]
        o_lat[:, :, 32 * core:32 * core + 32] = zl.transpose(3, 2, 0, 1).reshape(2, SEQ, 32)
        zc = res[core]["zc"][::-1]
        o_ctx[:, :, 32 * core:32 * core + 32] = zc.transpose(3, 2, 0, 1).reshape(2, CTX, 32)
    return o_lat, o_ctx


def gn_block():
    blk = np.zeros((128, 128), np.float32)
    blk[:64, :64] = 1.0 / 64
    blk[64:, 64:] = 1.0 / 64
    return blk


def run_front(l, h_lat, h_ctx, m_all, inp):
    K = get_kernel("A", lambda: build_token_kernel(do_wout=False, do_win=True))
    hs = tok_layout(h_lat, h_ctx)
    maps = common_maps(l, m_all, inp["norm_g"], 0, 5)
    for c in range(NCORE):
        maps[c].update({"h": hs[c], "wgu": inp["ffn1_wgu"][l], "wdn": inp["ffn1_wdn"][l], "win": inp["w_in"][l]})
    res = run(K, maps)
    hl, hc = tok_unlayout([r["hout"] for r in res])
    pl, pc = tok_unlayout([r["p"] for r in res])
    return hl, hc, pl, pc


def run_mixers(l, pl, pc, inp):
    Kh = get_kernel("H", lambda: build_hyena_kernel(True))
    names = ['hy_conv_w', 'hy_conv_b', 'hy_f_w1', 'hy_f_b1', 'hy_f_w2', 'hy_f_b2', 'hy_f_w3', 'hy_freq', 'hy_bias']
    hy_l, hy_c = hyena_unpack(run(Kh, hyena_maps(l, pl, pc, *[inp[n] for n in names])))
    Kr = get_kernel("R", lambda: build_rwkv_kernel())
    y, bon, v, g = rwkv_unpack(run(Kr, rwkv_maps(l, pl, pc, inp['rw_mu'], inp['rw_w0'], inp['rw_w2'], inp['rw_a0'], inp['rw_a2'],
                                                 inp['rw_g2'], inp['rw_k_k'], inp['rw_k_a'], inp['rw_r_k'])))

    def mkn():
        K, T_ = build_natten_kernel(True)
        emit_natten(K, T_, True)
        return K
    Kn = get_kernel("N", mkn)
    na_l, na_c = natten_unpack(run(Kn, natten_maps(l, pl, pc, inp['na_rpb'])))
    return dict(hy_l=hy_l, hy_c=hy_c, y=y, bon=bon, v=v, g=g, na_l=na_l, na_c=na_c)


def run_back(l, h_lat, h_ctx, mx, m_all, inp):
    K = get_kernel("C", lambda: build_token_kernel(do_wout=True, do_win=False))
    hs = tok_layout(h_lat, h_ctx)
    mixh = tok_layout(mx["hy_l"], mx["hy_c"])
    mixn = tok_layout(mx["na_l"], mx["na_c"])
    rws = []
    for arr in (mx["y"][0], mx["y"][1], mx["bon"][0], mx["bon"][1], mx["v"], mx["g"]):
        rws.append(tok_layout(arr[:, CTX:], arr[:, :CTX]))
    maps = common_maps(l, m_all, inp["norm_g"], 5, 4)
    lnp = np.stack([inp["rw_ln_w"][l].reshape(3, 128).T, inp["rw_ln_b"][l].reshape(3, 128).T], axis=-1).astype(np.float32)
    blk = gn_block()
    for c in range(NCORE):
        maps[c].update({"h": hs[c], "wgu": inp["ffn2_wgu"][l], "wdn": inp["ffn2_wdn"][l], "wout": inp["w_out"][l],
                        "mixh": mixh[c], "mixn": mixn[c], "rwin": np.stack([r[c] for r in rws]), "lnp": np.ascontiguousarray(lnp),
                        "blk": blk})
    res = run(K, maps)
    return tok_unlayout([r["hout"] for r in res])


def run_mod(inp):
    K = get_kernel("M", build_mod_kernel)
    return mod_unpack(run(K, mod_maps(inp["c"], inp["c_ctx"], inp["mod_w"], inp["mod_b"])))


def kernel(**inputs):
    inp = {k: np.asarray(v, dtype=np.float32) for k, v in inputs.items()}
    m_all = run_mod(inp)
    h_lat, h_ctx = inp["x"], inp["ctx"]
    for l in range(4):
        h_lat, h_ctx, pl, pc = run_front(l, h_lat, h_ctx, m_all, inp)
        mx = run_mixers(l, pl, pc, inp)
        h_lat, h_ctx = run_back(l, h_lat, h_ctx, mx, m_all, inp)
    return np.ascontiguousarray(h_lat, dtype=np.float32)
```
